# Optimizing a Trainium2 kernel written in Bass

```python
import math
import jax, jax.numpy as jnp
from jax import lax
import numpy as np

D_MODEL = 4096
BATCH = 2
SEQ = 4096
DEPTH = 1
DEC_BATCH = 8
DEC_SEQ = 32
PAST_LEN = 2048

CHUNK = 64
Q_BLOCK = 128
EPS = 1e-6
ROPE_THETA = 10000.0
MLA_HEADS = 16
MLA_NOPE_DIM = 128
MLA_ROPE_DIM = 64
MLA_V_DIM = 128
Q_LORA = 1024
KV_LORA = 512
MLA_WIDTH = MLA_HEADS * MLA_V_DIM
MLA_IN = Q_LORA + KV_LORA + MLA_ROPE_DIM
MLA_SCALE = (MLA_NOPE_DIM + MLA_ROPE_DIM) ** -0.5
SB_HEADS = 16
SB_HEAD_DIM = 128
SB_WIDTH = SB_HEADS * SB_HEAD_DIM
SB_SCALE = SB_HEAD_DIM ** -0.5
MIX_WIDTH = MLA_WIDTH + SB_WIDTH
IN_COLS = MLA_IN + 3 * SB_WIDTH
D_FF = -(-(8 * D_MODEL) // (3 * 256)) * 256
N_MOD = 6

kernel_name = "hybrid_mla_stickbreaking_streaming_step"


def rmsnorm(x, g):
    xf = x.astype(jnp.float32)
    y = xf * lax.rsqrt(jnp.mean(xf * xf, axis=-1, keepdims=True) + EPS)
    return (y * g.astype(jnp.float32)).astype(x.dtype)


def rope(x, pos):
    half = MLA_ROPE_DIM // 2
    inv = 1.0 / (ROPE_THETA ** (jnp.arange(half, dtype=jnp.float32) / half))
    ang = pos.astype(jnp.float32)[:, None] * inv[None, :]
    cos = jnp.cos(ang)[:, None, :].astype(x.dtype)
    sin = jnp.sin(ang)[:, None, :].astype(x.dtype)
    x1, x2 = x[..., :half], x[..., half:]
    return jnp.concatenate([x1 * cos - x2 * sin, x2 * cos + x1 * sin], axis=-1)


def adaln(c, w_ada, b_ada):
    mod = jax.nn.silu(c) @ w_ada + b_ada
    return jnp.split(mod[:, None, :], N_MOD, axis=-1)


def mixer_inputs(h, pos, w_in, g_q_lat, g_kv_lat, w_uq, w_uk):
    B, S, _ = h.shape
    proj = h @ w_in
    q_lat = proj[..., :Q_LORA]
    kv_lat = proj[..., Q_LORA:Q_LORA + KV_LORA]
    k_r = proj[..., Q_LORA + KV_LORA:MLA_IN]
    sb = proj[..., MLA_IN:].reshape(B, S, 3, SB_HEADS, SB_HEAD_DIM)
    q = jnp.einsum('bsr,rhe->bshe', rmsnorm(q_lat, g_q_lat), w_uq)
    q_abs = jnp.einsum('bshn,chn->bshc', q[..., :MLA_NOPE_DIM], w_uk)
    q_rope = rope(q[..., MLA_NOPE_DIM:], pos)
    latent = rmsnorm(kv_lat, g_kv_lat)
    k_rope = rope(k_r[:, :, None, :], pos)[:, :, 0, :]
    return q_abs, q_rope, latent, k_rope, sb[:, :, 0], sb[:, :, 1], sb[:, :, 2]


def mla_attend(q_abs, q_rope, q_pos, latent, k_rope, k_pos, w_uv):
    s = (jnp.einsum('bqhc,bkc->bhqk', q_abs, latent)
         + jnp.einsum('bqhr,bkr->bhqk', q_rope, k_rope)).astype(jnp.float32) * MLA_SCALE
    visible = (k_pos[None, :] // CHUNK) <= (q_pos[:, None] // CHUNK)
    p = jax.nn.softmax(jnp.where(visible, s, -jnp.inf), axis=-1).astype(latent.dtype)
    o_lat = jnp.einsum('bhqk,bkc->bqhc', p, latent)
    return jnp.einsum('bqhc,chv->bqhv', o_lat, w_uv)


def sb_attend(q, q_pos, k, v, k_pos):
    z = jnp.einsum('bqhd,bkhd->bhqk', q, k).astype(jnp.float32) * SB_SCALE
    before = k_pos[None, :] < q_pos[:, None]
    log_keep = jnp.where(before, jax.nn.log_sigmoid(-z), 0.0)
    tail = lax.cumsum(log_keep, axis=3, reverse=True) - log_keep
    log_a = jnp.where(before, jax.nn.log_sigmoid(z) + tail, -jnp.inf)
    a = jnp.exp(log_a).astype(v.dtype)
    return jnp.einsum('bhqk,bkhd->bqhd', a, v)


def to_blocks(x):
    B, S = x.shape[0], x.shape[1]
    return jnp.moveaxis(x.reshape((B, S // Q_BLOCK, Q_BLOCK) + x.shape[2:]), 1, 0)


def from_blocks(x):
    nb, B = x.shape[0], x.shape[1]
    x = jnp.moveaxis(x, 0, 1)
    return x.reshape((B, nb * Q_BLOCK) + x.shape[3:])


def trunk_layer(x, c, pos, past, w_ada, b_ada, g_mix, g_ffn, w_in, g_q_lat, g_kv_lat,
                w_uq, w_uk, w_uv, g_out_mla, g_out_sb, w_out, w_gate, w_up, w_down):
    B, S, _ = x.shape
    sh_m, sc_m, gt_m, sh_f, sc_f, gt_f = adaln(c, w_ada, b_ada)
    h = rmsnorm(x, g_mix) * (1.0 + sc_m) + sh_m
    q_abs, q_rope, lat, k_rope, sb_q, sb_k, sb_v = mixer_inputs(
        h, pos, w_in, g_q_lat, g_kv_lat, w_uq, w_uk)
    if past is None:
        def one_block(blk):
            qa, qr, sq, qp = blk
            return (mla_attend(qa, qr, qp, lat, k_rope, pos, w_uv),
                    sb_attend(sq, qp, sb_k, sb_v, pos))
        blocks = (to_blocks(q_abs), to_blocks(q_rope), to_blocks(sb_q),
                  pos.reshape(S // Q_BLOCK, Q_BLOCK))
        o_mla, o_sb = lax.map(one_block, blocks)
        o_mla, o_sb = from_blocks(o_mla), from_blocks(o_sb)
    else:
        c_lat, c_kr, c_k, c_v = past
        k_pos = jnp.arange(c_lat.shape[1] + S)
        o_mla = mla_attend(q_abs, q_rope, pos, jnp.concatenate([c_lat, lat], axis=1),
                           jnp.concatenate([c_kr, k_rope], axis=1), k_pos, w_uv)
        o_sb = sb_attend(sb_q, pos, jnp.concatenate([c_k, sb_k], axis=1),
                         jnp.concatenate([c_v, sb_v], axis=1), k_pos)
    merged = jnp.concatenate([rmsnorm(o_mla.reshape(B, S, MLA_WIDTH), g_out_mla),
                              rmsnorm(o_sb.reshape(B, S, SB_WIDTH), g_out_sb)], axis=-1)
    x = x + gt_m * (merged @ w_out)
    h = rmsnorm(x, g_ffn) * (1.0 + sc_f) + sh_f
    x = x + gt_f * ((jax.nn.silu(h @ w_gate) * (h @ w_up)) @ w_down)
    return x, (lat, k_rope, sb_k, sb_v)


def setup_inputs(seed: int = 0) -> dict:
    key = jax.random.key(seed)
    ks = jax.random.split(key, 32)
    f32 = jnp.float32
    nrm = lambda k, shape, scale: jax.random.normal(k, shape, f32) * scale
    gain = lambda k, shape: 1.0 + 0.02 * jax.random.normal(k, shape, f32)
    L = DEPTH
    return {
        "x_prompt": nrm(ks[0], (BATCH, SEQ, D_MODEL), 1.0),
        "x_sample": nrm(ks[1], (DEC_BATCH, DEC_SEQ, D_MODEL), 1.0),
        "cache_mla_latent": nrm(ks[2], (L, DEC_BATCH, PAST_LEN, KV_LORA), 1.0),
        "cache_mla_krope": nrm(ks[3], (L, DEC_BATCH, PAST_LEN, MLA_ROPE_DIM), 1.0),
        "cache_sb_k": nrm(ks[4], (L, DEC_BATCH, PAST_LEN, SB_HEADS, SB_HEAD_DIM), 1.0),
        "cache_sb_v": nrm(ks[5], (L, DEC_BATCH, PAST_LEN, SB_HEADS, SB_HEAD_DIM), 1.0),
        "c_prompt": nrm(ks[6], (BATCH, D_MODEL), 1.0),
        "c_sample": nrm(ks[7], (DEC_BATCH, D_MODEL), 1.0),
        "w_ada": nrm(ks[8], (L, D_MODEL, N_MOD * D_MODEL), 0.5 * D_MODEL ** -0.5),
        "b_ada": nrm(ks[9], (L, N_MOD * D_MODEL), 0.02),
        "g_mix": gain(ks[10], (L, D_MODEL)),
        "g_ffn": gain(ks[11], (L, D_MODEL)),
        "w_in": nrm(ks[12], (L, D_MODEL, IN_COLS), D_MODEL ** -0.5),
        "g_q_lat": gain(ks[13], (L, Q_LORA)),
        "g_kv_lat": gain(ks[14], (L, KV_LORA)),
        "w_uq": nrm(ks[15], (L, Q_LORA, MLA_HEADS, MLA_NOPE_DIM + MLA_ROPE_DIM), Q_LORA ** -0.5),
        "w_uk": nrm(ks[16], (L, KV_LORA, MLA_HEADS, MLA_NOPE_DIM), KV_LORA ** -0.5),
        "w_uv": nrm(ks[17], (L, KV_LORA, MLA_HEADS, MLA_V_DIM), KV_LORA ** -0.5),
        "g_out_mla": gain(ks[18], (L, MLA_WIDTH)),
        "g_out_sb": gain(ks[19], (L, SB_WIDTH)),
        "w_out": nrm(ks[20], (L, MIX_WIDTH, D_MODEL), MIX_WIDTH ** -0.5),
        "w_gate": nrm(ks[21], (L, D_MODEL, D_FF), D_MODEL ** -0.5),
        "w_up": nrm(ks[22], (L, D_MODEL, D_FF), D_MODEL ** -0.5),
        "w_down": nrm(ks[23], (L, D_FF, D_MODEL), D_FF ** -0.5),
        "g_final": gain(ks[24], (D_MODEL,)),
    }


def reference(x_prompt, x_sample, cache_mla_latent, cache_mla_krope, cache_sb_k, cache_sb_v,
              c_prompt, c_sample, w_ada, b_ada, g_mix, g_ffn, w_in, g_q_lat, g_kv_lat,
              w_uq, w_uk, w_uv, g_out_mla, g_out_sb, w_out, w_gate, w_up, w_down, g_final):
    pos_p = jnp.arange(x_prompt.shape[1])
    pos_s = cache_mla_latent.shape[2] + jnp.arange(x_sample.shape[1])
    xp, xs = x_prompt, x_sample
    new_p, new_s = [], []
    for l in range(DEPTH):
        w = (w_ada[l], b_ada[l], g_mix[l], g_ffn[l], w_in[l], g_q_lat[l], g_kv_lat[l],
             w_uq[l], w_uk[l], w_uv[l], g_out_mla[l], g_out_sb[l], w_out[l],
             w_gate[l], w_up[l], w_down[l])
        xp, st_p = trunk_layer(xp, c_prompt, pos_p, None, *w)
        past = (cache_mla_latent[l], cache_mla_krope[l], cache_sb_k[l], cache_sb_v[l])
        xs, st_s = trunk_layer(xs, c_sample, pos_s, past, *w)
        new_p.append(st_p)
        new_s.append(st_s)
    y_prompt = rmsnorm(xp, g_final)
    y_sample = rmsnorm(xs, g_final)
    p_lat = jnp.stack([s[0] for s in new_p], axis=0)
    p_krope = jnp.stack([s[1] for s in new_p], axis=0)
    p_sbk = jnp.stack([s[2] for s in new_p], axis=0)
    p_sbv = jnp.stack([s[3] for s in new_p], axis=0)
    s_lat = jnp.stack([s[0] for s in new_s], axis=0)
    s_krope = jnp.stack([s[1] for s in new_s], axis=0)
    s_sbk = jnp.stack([s[2] for s in new_s], axis=0)
    s_sbv = jnp.stack([s[3] for s in new_s], axis=0)
    return (y_prompt, y_sample, p_lat, p_krope, p_sbk, p_sbv, s_lat, s_krope, s_sbk, s_sbv)
```

```python
import contextlib
import math
import types
import os
DBG = set(os.environ.get('KDBG', '').split(','))
import numpy as np
import concourse.bass as bass
import concourse.mybir as mybir
from concourse.bass_utils import run_bass_kernel_spmd

F32 = mybir.dt.float32
BF16 = mybir.dt.bfloat16
AF = mybir.ActivationFunctionType
ALU = mybir.AluOpType
AX = mybir.AxisListType
EPS = 1e-6
NEG = -30000.0


class Cfg:
    def __init__(s, D=4096, QL=1024, DFF=11008, SEQ=4096, PAST=2048, DS=32, NB=2, G=4, NCORES=8, TB=1024, GMAX=5):
        s.D, s.QL, s.DFF, s.SEQ, s.PAST, s.DS, s.NB, s.G, s.NCORES = D, QL, DFF, SEQ, PAST, DS, NB, G, NCORES
        s.H = D // 256
        s.HW = s.H * 128
        s.KC = D // 128
        s.MLA_IN = QL + 512 + 64
        s.INC = s.MLA_IN + 3 * s.HW
        s.NS = SEQ // (128 * G)
        s.NTQ = s.NS * 128
        s.NTA = s.NTQ + DS
        s.TB = min(TB, SEQ)
        s.GMAX = GMAX
        s.FC = DFF // 128
        s.KS = PAST + DS
        s.KMAX = max(SEQ, s.KS)
        s.MLA_SCALE = (128 + 64) ** -0.5
        s.SB_SCALE = 128 ** -0.5


PSUM_IDS = set()


def _freeze(fn):
    if fn.__closure__ is None:
        return fn
    cells = []
    for cl in fn.__closure__:
        try:
            cells.append(types.CellType(cl.cell_contents))
        except ValueError:
            cells.append(cl)
    return types.FunctionType(fn.__code__, fn.__globals__, fn.__name__, fn.__defaults__, tuple(cells))


class Res:
    __slots__ = ("name", "w", "rs", "excl")

    def __init__(self, name="", excl=False):
        self.name = name
        self.w = None
        self.rs = []
        self.excl = excl


class Op:
    __slots__ = ("eng", "fn", "reads", "writes", "dma", "deps", "marked", "sem", "cnt", "waits", "ph")

    def __init__(self, eng, fn, reads, writes, dma):
        self.eng = eng; self.fn = fn; self.reads = reads; self.writes = writes; self.dma = dma
        self.deps = []; self.marked = False; self.sem = None; self.cnt = 0; self.waits = []; self.ph = 0


class Prog:
    ENGS = ("pe", "act", "dve", "pool", "sp")

    def __init__(self, nc, stack, n_dma_sems=8, sem_limit=4000):
        self.nc = nc
        self.stack = stack
        self.n_dma_sems = n_dma_sems
        self.sem_limit = sem_limit
        self.eng_sem = {}
        self.eng_cnt = {}
        self.dma_sems = {}
        self.dma_rr = {}
        self.dma_last = {}
        self.nsem = 0
        self.prev_done = None
        self.ops = []
        self.nops_total = 0
        self.phase_idx = 0

    def newsem(self, tag):
        self.nsem += 1
        return self.stack.enter_context(self.nc.semaphore("s_%s_%d" % (tag, self.nsem)))

    def begin(self):
        self.ops = []
        self.phase_idx += 1

    def add(self, eng, fn, reads=(), writes=(), dma=False):
        op = Op(eng, _freeze(fn), list(reads), list(writes), dma)
        op.ph = self.phase_idx
        self.ops.append(op)
        return op

    def mark(self, name):
        if "marks" in DBG:
            print("MARK phase %d %s ops=%d" % (self.phase_idx, name, len(self.ops)))

    def end(self):
        nc = self.nc
        mo = os.environ.get("KMAXOPS")
        if mo:
            ph, n = mo.split(":")
            if int(ph) == self.phase_idx:
                self.ops = self.ops[:int(n)]
        ops = self.ops
        self.nops_total += len(ops)
        for op in ops:
            deps = []
            reads = [r for r in op.reads if not r.excl]
            writes = op.writes + [r for r in op.reads if r.excl]
            for r in reads:
                if r.w is not None:
                    deps.append(r.w)
            for r in writes:
                if r.w is not None:
                    deps.append(r.w)
                deps.extend(r.rs)
            for r in reads:
                r.rs.append(op)
            for r in writes:
                r.w = op
                r.rs = []
            seen = set()
            for d in deps:
                if d is op or id(d) in seen or d.ph != op.ph:
                    continue
                seen.add(id(d))
                if (not d.dma) and (not op.dma) and d.eng == "pe" and op.eng == "pe":
                    continue
                op.deps.append(d)
                d.marked = True
        last_comp = {}
        for op in ops:
            if not op.dma:
                last_comp[op.eng] = op
        for op in last_comp.values():
            op.marked = True
        waited = {e: {} for e in self.ENGS}
        for op in ops:
            e = op.eng
            w = waited[e]
            waits = []
            for d in op.deps:
                key = id(d.sem)
                if w.get(key, 0) >= d.cnt:
                    continue
                w[key] = d.cnt
                waits.append((d.sem, d.cnt))
            if op.dma:
                if e not in self.dma_sems:
                    self.dma_sems[e] = [self.newsem("dma" + e) for _ in range(self.n_dma_sems)]
                    self.dma_rr[e] = 0
                    self.dma_last[e] = [0] * self.n_dma_sems
                i = self.dma_rr[e]
                self.dma_rr[e] = (i + 1) % self.n_dma_sems
                s = self.dma_sems[e][i]
                prev = self.dma_last[e][i]
                if prev > 0 and w.get(id(s), 0) < prev:
                    w[id(s)] = prev
                    waits.append((s, prev))
                op.sem = s
                op.cnt = prev + 16
                self.dma_last[e][i] = op.cnt
                op.marked = True
            elif op.marked:
                if e not in self.eng_sem or self.eng_cnt[e] >= self.sem_limit:
                    self.eng_sem[e] = self.newsem(e)
                    self.eng_cnt[e] = 0
                self.eng_cnt[e] += 1
                op.sem = self.eng_sem[e]
                op.cnt = self.eng_cnt[e]
            m = {}
            for s, c in waits:
                k = id(s)
                if k not in m or m[k][1] < c:
                    m[k] = (s, c)
            op.waits = list(m.values())
        by_eng = {e: [op for op in ops if op.eng == e] for e in self.ENGS}
        done = self.newsem("done")
        prev_done = self.prev_done
        prog = self

        with nc.Block() as block:
            def run(eng_obj, ename):
                if prev_done is not None:
                    eng_obj.wait_ge(prev_done, len(prog.ENGS))
                for op in by_eng[ename]:
                    for s, c in op.waits:
                        eng_obj.wait_ge(s, c)
                    ins = op.fn(eng_obj)
                    if op.marked:
                        ins.then_inc(op.sem, 16 if op.dma else 1)
                if ename in prog.dma_sems:
                    for s, c in zip(prog.dma_sems[ename], prog.dma_last[ename]):
                        if c > 0:
                            eng_obj.wait_ge(s, c)
                lc = last_comp.get(ename)
                if lc is not None:
                    eng_obj.wait_ge(lc.sem, lc.cnt)
                eng_obj.sem_inc(done, 1)

            @block.tensor
            def _(eng):
                run(eng, "pe")

            @block.scalar
            def _(eng):
                run(eng, "act")

            @block.vector
            def _(eng):
                run(eng, "dve")

            @block.gpsimd
            def _(eng):
                run(eng, "pool")

            @block.sync
            def _(eng):
                run(eng, "sp")

        self.prev_done = done
        self.ops = []


class Rot:
    def __init__(self, bufs):
        self.bufs = bufs
        self.res = [Res(excl=(id(b) in PSUM_IDS)) for b in bufs]
        self.i = 0

    def next(self):
        k = self.i % len(self.bufs)
        self.i += 1
        return self.bufs[k], self.res[k]


class Builder:
    def __init__(self, cfg):
        self.c = cfg
        self.nc = bass.Bass("TRN2", target_bir_lowering=False)
        self.din = {}
        self.dout = {}

    def declare(self):
        c, nc = self.c, self.nc

        def I(name, shape):
            self.din[name] = nc.dram_tensor(name, list(shape), F32, kind="ExternalInput").ap()

        def O(name, shape):
            self.dout[name] = nc.dram_tensor(name, list(shape), F32, kind="ExternalOutput").ap()

        def S(name, shape, dt):
            return nc.dram_tensor(name, list(shape), dt).ap()

        D, H, HW, KC = c.D, c.H, c.HW, c.KC
        I("xp", [c.SEQ, D]); I("xq", [c.NTQ, D]); I("xs", [c.DS, D])
        I("c_lat", [c.PAST, 512]); I("c_kr", [c.PAST, 64]); I("c_k", [c.PAST, HW]); I("c_v", [c.PAST, HW])
        I("cT", [128, KC * 2])
        I("w_ada", [D, 6 * D]); I("b_ada", [6 * D])
        I("g_mixT", [128, KC]); I("g_ffnT", [128, KC]); I("g_outT", [128, KC])
        I("w_in", [D, c.INC]); I("g_q", [c.QL]); I("g_kv", [512])
        I("w_uq", [c.QL, H * 192]); I("w_uqs", [c.QL, H * 64]); I("w_ukT", [H, 128, 512]); I("w_uv", [512, HW])
        I("w_out", [D, D]); I("w_gate", [D, c.DFF]); I("w_up", [D, c.DFF]); I("w_down", [c.DFF, D]); I("g_fin", [D])
        I("ropeKp", [c.SEQ, 128]); I("ropeKs", [c.DS, 128]); I("ropeQ", [64, 2, c.NTA])
        I("m_mla", [128, 512]); I("m_sb", [128, 512]); I("m_nsb", [128, 512])
        I("ms_sb", [c.DS, c.DS]); I("ms_nsb", [c.DS, c.DS])
        O("y_p", [c.NTQ, D]); O("y_s", [c.DS, D])
        O("p_lat", [c.SEQ, 512]); O("p_kr", [c.SEQ, 64]); O("p_sbk", [c.SEQ, HW]); O("p_sbv", [c.SEQ, HW])
        O("s_lat", [c.DS, 512]); O("s_kr", [c.DS, 64]); O("s_sbk", [c.DS, HW]); O("s_sbv", [c.DS, HW])
        self.modD = S("modD", [2, 6 * D], F32)
        self.scr = {}
        for st, nk in (("P", c.SEQ), ("S", c.KS)):
            self.scr[st] = dict(
                lat=S("lat" + st, [nk, 512], BF16), latT=S("latT" + st, [128, 4, nk], BF16),
                krT=S("krT" + st, [64, nk], BF16), kT=S("kT" + st, [128, H, nk], BF16), v=S("v" + st, [nk, HW], BF16), nk=nk)
        self.qabs = S("qabs", [c.NS + 1, 128, H, 4, 128], BF16)
        self.qrope = S("qrope", [c.NS + 1, 64, H, 128], BF16)
        self.sbq = S("sbq", [c.NS + 1, 128, H, 128], BF16)
        self.mT = S("mTs", [128, KC, c.NTA], BF16)
        self.x2 = S("x2s", [c.NTA, D], F32)
        self.x3 = S("x3s", [c.NTA, D], F32)

    def dma(self, q, out, in_, reads=(), writes=()):
        return self.P.add(q, lambda e: e.dma_start(out=out, in_=in_), reads, writes, dma=True)

    def rsqrt_chain(self, ss, np_, inv_n, tmp, out, r_ss, r_tmp, r_out):
        P = self.P
        P.add("dve", lambda e: e.tensor_scalar(out=tmp[:np_], in0=ss[:np_], scalar1=inv_n, scalar2=EPS, op0=ALU.mult, op1=ALU.add),
              reads=[r_ss], writes=[r_tmp])
        nh = self.nhalf
        P.add("pool", lambda e: e.tensor_tensor(out=out[:np_], in0=tmp[:np_], in1=nh[:np_], op=ALU.pow),
              reads=[r_tmp, self.r_const], writes=[r_out])

    def transposes_evac(self, src, np_, nchunk, ptr_rot, evac, r_src, dtype_is_bf16=True, grp=8):
        P = self.P
        ident = self.identb if dtype_is_bf16 else self.identf
        k0 = 0
        while k0 < nchunk:
            n = min(grp, nchunk - k0)
            pt, r_pt = ptr_rot.next()
            for j in range(n):
                k = k0 + j
                P.add("pe", lambda e, k=k, j=j, pt=pt: e.transpose(out=pt[:, j, :np_], in_=src[:np_, k * 128:(k + 1) * 128], identity=ident[:np_, :np_]),
                      reads=[r_src, self.r_const], writes=[r_pt])
            evac(k0, n, pt, r_pt)
            k0 += n

    def prep_hT(self, x_ap, np_, GT, SHT, r_mod, hT, col0, r_hT, W):
        c, P = self.c, self.P
        xt, r_xt = W["xt"].next()
        xn, r_xn = W["xn"].next()
        ss, r_ss = W["ss"].next()
        sv, r_sv = W["sv"].next()
        rs, r_rs = W["rs"].next()
        self.dma("sp", xt[:np_], x_ap, writes=[r_xt])
        P.add("act", lambda e: e.activation(out=xn[:np_], in_=xt[:np_], func=AF.Square, accum_out=ss[:np_]), reads=[r_xt], writes=[r_xn, r_ss])
        self.rsqrt_chain(ss, np_, 1.0 / c.D, sv, rs, r_ss, r_sv, r_rs)
        P.add("dve", lambda e: e.tensor_scalar(out=xn[:np_], in0=xt[:np_], scalar1=rs[:np_], scalar2=None, op0=ALU.mult), reads=[r_xt, r_rs], writes=[r_xn])
        cnt = [0]

        def evac(k0, n, pt, r_pt):
            for j in range(n):
                kc = k0 + j
                if cnt[0] % 2 == 0:
                    P.add("act", lambda e, kc=kc, j=j, pt=pt: e.activation(out=hT[:, kc, col0:col0 + np_], in_=pt[:, j, :np_], func=AF.Identity,
                                                                        scale=GT[:, kc:kc + 1], bias=SHT[:, kc:kc + 1]),
                          reads=[r_pt, r_mod], writes=[r_hT])
                else:
                    P.add("dve", lambda e, kc=kc, j=j, pt=pt: e.tensor_scalar(out=hT[:, kc, col0:col0 + np_], in0=pt[:, j, :np_], scalar1=GT[:, kc:kc + 1],
                                                                           scalar2=SHT[:, kc:kc + 1], op0=ALU.mult, op1=ALU.add),
                          reads=[r_pt, r_mod], writes=[r_hT])
                cnt[0] += 1

        self.transposes_evac(xn, np_, c.KC, W["ptr"], evac, r_xn)

    def load_modvecs(self, st, chunks, psum_rot):
        c, P = self.c, self.P
        r_mod = Res("mod")
        out = {}
        for r in range(2):
            for ch in chunks:
                t = st.enter_context(self.nc.sbuf_tensor("mv_%d_%d_%d" % (self.uid(), r, ch), [128, c.KC], F32))
                tmp = st.enter_context(self.nc.sbuf_tensor("mvt_%d_%d_%d" % (self.uid(), r, ch), [c.KC, 128], F32))
                r_tmp = Res()
                self.dma("sp", tmp[:], self.modD[r, ch * c.D:(ch + 1) * c.D].rearrange("(k p) -> k p", p=128), writes=[r_tmp])
                pt, r_pt = psum_rot.next()
                P.add("pe", lambda e, pt=pt, tmp=tmp: e.transpose(out=pt[:, :c.KC], in_=tmp[:, :], identity=self.identf[:c.KC, :c.KC]),
                      reads=[r_tmp, self.r_const], writes=[r_pt])
                P.add("dve", lambda e, pt=pt, t=t: e.tensor_copy(out=t[:], in_=pt[:, :c.KC]), reads=[r_pt], writes=[r_mod])
                out[(r, ch)] = t
        return out, r_mod

    _uid = 0

    def uid(self):
        Builder._uid += 1
        return Builder._uid

    def sb(self, st, name, shape, dt):
        return st.enter_context(self.nc.sbuf_tensor("%s_%d" % (name, self.uid()), list(shape), dt))

    def ps(self, st, name, shape, dt):
        t = st.enter_context(self.nc.psum_tensor("%s_%d" % (name, self.uid()), list(shape), dt))
        PSUM_IDS.add(id(t))
        self._keep = getattr(self, "_keep", [])
        self._keep.append(t)
        return t

    def make_GS(self, st, mv, r_mod, gT_name, ch_scale, ch_shift):
        c, P = self.c, self.P
        g = self.sb(st, "gT", [128, c.KC], F32)
        r_g = Res()
        self.dma("sp", g[:], self.din[gT_name][:, :], writes=[r_g])
        GT = []
        SHT = []
        for r in range(2):
            G = self.sb(st, "G", [128, c.KC], F32)
            sc = mv[(r, ch_scale)]
            P.add("dve", lambda e, G=G, sc=sc: e.scalar_tensor_tensor(out=G[:], in0=sc[:], scalar=1.0, in1=g[:], op0=ALU.add, op1=ALU.mult),
                  reads=[r_mod, r_g], writes=[r_mod])
            GT.append(G)
            SHT.append(mv[(r, ch_shift)])
        return GT, SHT

    def ph_mod(self):
        c, P, nc = self.c, self.P, self.nc
        P.begin()
        with contextlib.ExitStack() as st:
            cTf = self.sb(st, "cTf", [128, c.KC * 2], F32)
            scT = self.sb(st, "scT", [128, c.KC * 2], BF16)
            wb = Rot([self.sb(st, "wb", [128, c.KC, 512], BF16) for _ in range(2)])
            bt = Rot([self.sb(st, "bt", [2, 512], F32) for _ in range(2)])
            mo = Rot([self.sb(st, "mo", [2, 512], F32) for _ in range(2)])
            pm = Rot([self.ps(st, "pm", [128, 512], F32) for _ in range(2)])
            W = {}
            W["ptf"] = Rot([self.ps(st, "ptf", [128, 4, 128], F32) for _ in range(2)])
            W["of32"] = Rot([self.sb(st, "of32", [128, 512], F32) for _ in range(4)])
            W["latst"] = Rot([self.sb(st, "latst", [128, 4, 128], BF16) for _ in range(2)])
            W["krst"] = Rot([self.sb(st, "krst", [64, 128], BF16) for _ in range(2)])
            W["kst"] = Rot([self.sb(st, "kst", [128, 4, 128], BF16) for _ in range(3)])
            W["kr1"] = Rot([self.sb(st, "kr1", [128, 64], F32) for _ in range(2)])
            sS = self.scr["S"]
            self.kv_evacs(st, W, None, sS, None, None)
            self.dma("pool", sS["lat"][0:c.PAST, :], self.din["c_lat"][:, :])
            self.dma("pool", sS["v"][0:c.PAST, :], self.din["c_v"][:, :])

            def cache_tile(ct):
                k0 = ct * 128
                o32, r_o32 = W["of32"].next()
                self.dma("sp", o32[:], self.din["c_lat"][k0:k0 + 128, :], writes=[r_o32])
                self.lat_T_store(o32, r_o32, 128, k0)
                t1, r_t1 = W["kr1"].next()
                self.dma("sp", t1[:], self.din["c_kr"][k0:k0 + 128, :], writes=[r_t1])
                self.kr_T_store(t1, r_t1, 128, k0)
                for u in range(c.H // 4):
                    ck, r_ck = W["of32"].next()
                    self.dma("sp", ck[:], self.din["c_k"][k0:k0 + 128, u * 512:(u + 1) * 512], writes=[r_ck])
                    self.k_T_group(ck, r_ck, 128, 4 * u, k0)
            ncache = c.PAST // 128
            r_c, r_sc = Res(), Res()
            self.dma("sp", cTf[:], self.din["cT"][:, :], writes=[r_c])
            P.add("act", lambda e: e.activation(out=scT[:], in_=cTf[:], func=AF.Silu), reads=[r_c], writes=[r_sc])
            wa, ba = self.din["w_ada"], self.din["b_ada"]
            for cb in range(6 * c.D // 512):
                w, r_w = wb.next()
                b, r_b = bt.next()
                m, r_m = mo.next()
                p, r_p = pm.next()
                self.dma("pool", w[:], wa[:, cb * 512:(cb + 1) * 512].rearrange("(k p) n -> p k n", p=128), writes=[r_w])
                self.dma("sp", b[:], ba[cb * 512:(cb + 1) * 512].partition_broadcast(2), writes=[r_b])
                for kc in range(c.KC):
                    P.add("pe", lambda e, kc=kc, w=w, p=p: e.matmul(p[0:2, :], lhsT=scT[:, kc * 2:kc * 2 + 2], rhs=w[:, kc, :], start=(kc == 0), stop=(kc == c.KC - 1)),
                          reads=[r_sc, r_w], writes=[r_p])
                P.add("dve", lambda e, m=m, p=p, b=b: e.tensor_tensor(out=m[:], in0=p[0:2, :], in1=b[:], op=ALU.add), reads=[r_p, r_b], writes=[r_m])
                self.dma("sp", self.modD[:, cb * 512:(cb + 1) * 512], m[:], reads=[r_m])
                nblk = 6 * c.D // 512
                per = (ncache + nblk - 1) // nblk
                for ct in range(cb * per, min(ncache, (cb + 1) * per)):
                    cache_tile(ct)
            P.end()

    def proj_pass(self, st, tiles, units, w_ap, hT, r_hT, wrot, prot, after_tile=None):
        c, P = self.c, self.P
        for u in units:
            wbs = []
            off = 0
            for (c0, n) in u["blocks"]:
                w, r_w = wrot.next()
                self.dma("pool", w[:, :, :n], w_ap[:, c0:c0 + n].rearrange("(k p) n -> p k n", p=128), writes=[r_w])
                wbs.append((w, r_w, off, n))
                off += n
            for ti, t in enumerate(tiles):
                np_, col0 = t["np"], t["col0"]
                pp, r_pp = prot.next()
                for (w, r_w, o, n) in wbs:
                    for kc in range(c.KC):
                        P.add("pe", lambda e, kc=kc, w=w, pp=pp, o=o, n=n, np_=np_, col0=col0: e.matmul(
                            pp[:np_, o:o + n], lhsT=hT[:, kc, col0:col0 + np_], rhs=w[:, kc, :n], start=(kc == 0), stop=(kc == c.KC - 1)),
                            reads=[r_hT[ti], r_w], writes=[r_pp])
                u["evac"](ti, t, pp, r_pp)
                if after_tile is not None and u is units[-1]:
                    after_tile(ti)

    def kv_evacs(self, st, W, outs, scr, gkv, r_gkv):
        c, P = self.c, self.P
        H = c.H
        units = []

        def lat_T_store(src32, r_src, np_, k0):
            lst, r_lst = W["latst"].next()

            def ev(kk, n, pt, r_pt):
                P.add("act", lambda e: e.activation(out=lst[:, 0:4, :np_], in_=pt[:, 0:4, :np_], func=AF.Copy), reads=[r_pt], writes=[r_lst])
            self.transposes_evac(src32, np_, 4, W["ptf"], ev, r_src, dtype_is_bf16=False, grp=4)
            self.dma("sp", scr["latT"][:, :, k0:k0 + np_], lst[:, :, :np_], reads=[r_lst])

        def kr_T_store(src32, r_src, np_, k0):
            kst, r_kst = W["krst"].next()
            pt, r_pt = W["ptf"].next()
            P.add("pe", lambda e: e.transpose(out=pt[0:64, 0, :np_], in_=src32[:np_, 0:64], identity=self.identf[:np_, :np_]),
                  reads=[r_src, self.r_const], writes=[r_pt])
            P.add("act", lambda e: e.activation(out=kst[0:64, :np_], in_=pt[0:64, 0, :np_], func=AF.Copy), reads=[r_pt], writes=[r_kst])
            self.dma("sp", scr["krT"][:, k0:k0 + np_], kst[0:64, :np_], reads=[r_kst])

        def k_T_group(src32, r_src, np_, h0, k0):
            kst, r_kst = W["kst"].next()

            def ev(kk, n, pt, r_pt):
                P.add("dve", lambda e: e.tensor_copy(out=kst[:, 0:4, :np_], in_=pt[:, 0:4, :np_]), reads=[r_pt], writes=[r_kst])
            self.transposes_evac(src32, np_, 4, W["ptf"], ev, r_src, dtype_is_bf16=False, grp=4)
            self.dma("sp", scr["kT"][:, h0:h0 + 4, k0:k0 + np_], kst[:, :, :np_], reads=[r_kst])

        self.lat_T_store, self.kr_T_store, self.k_T_group = lat_T_store, kr_T_store, k_T_group

        def ev_lat(ti, t, pp, r_pp):
            np_, k0 = t["np"], t["tok0"]
            ss, r_ss = W["ss"].next(); sv, r_sv = W["sv"].next(); rl, r_rl = W["rs"].next()
            j16, r_j16 = W["ob16"].next()
            o32, r_o32 = W["of32"].next()
            P.add("act", lambda e: e.activation(out=j16[:np_], in_=pp[:np_, :], func=AF.Square, accum_out=ss[:np_]), reads=[r_pp], writes=[r_j16, r_ss])
            self.rsqrt_chain(ss, np_, 1.0 / 512, sv, rl, r_ss, r_sv, r_rl)
            P.add("dve", lambda e: e.scalar_tensor_tensor(out=o32[:np_], in0=pp[:np_, :], scalar=rl[:np_], in1=gkv[:np_], op0=ALU.mult, op1=ALU.mult),
                  reads=[r_pp, r_rl, r_gkv], writes=[r_o32])
            self.dma("sp", outs["lat"][k0:k0 + np_, :], o32[:np_], reads=[r_o32])
            P.add("act", lambda e: e.activation(out=j16[:np_], in_=o32[:np_], func=AF.Copy), reads=[r_o32], writes=[r_j16])
            self.dma("sp", scr["lat"][k0:k0 + np_, :], j16[:np_], reads=[r_j16])
            lat_T_store(o32, r_o32, np_, k0)

        units.append(dict(blocks=[(c.QL, 512)], evac=ev_lat))

        def ev_kr(ti, t, pp, r_pp):
            np_, k0 = t["np"], t["tok0"]
            rt, r_rt = W["rt"].next()
            t1, r_t1 = W["kr1"].next(); t2, r_t2 = W["kr2"].next()
            self.dma("sp", rt[:np_], t["rope"], writes=[r_rt])
            P.add("dve", lambda e: e.tensor_tensor(out=t1[:np_], in0=pp[:np_, 0:64], in1=rt[:np_, 0:64], op=ALU.mult), reads=[r_pp, r_rt], writes=[r_t1])
            P.add("dve", lambda e: e.tensor_tensor(out=t2[:np_, 0:32], in0=pp[:np_, 32:64], in1=rt[:np_, 64:96], op=ALU.mult), reads=[r_pp, r_rt], writes=[r_t2])
            P.add("dve", lambda e: e.tensor_tensor(out=t2[:np_, 32:64], in0=pp[:np_, 0:32], in1=rt[:np_, 96:128], op=ALU.mult), reads=[r_pp, r_rt], writes=[r_t2])
            P.add("dve", lambda e: e.tensor_tensor(out=t1[:np_], in0=t1[:np_], in1=t2[:np_], op=ALU.add), reads=[r_t1, r_t2], writes=[r_t1])
            self.dma("sp", outs["kr"][k0:k0 + np_, :], t1[:np_], reads=[r_t1])
            kr_T_store(t1, r_t1, np_, k0)

        units.append(dict(blocks=[(c.QL + 512, 64)], evac=ev_kr))

        kstate = {}

        def mk_sbk(u):
            def ev(ti, t, pp, r_pp):
                np_, k0 = t["np"], t["tok0"]
                o32, r_o32 = W["of32"].next()
                P.add("act", lambda e: e.activation(out=o32[:np_], in_=pp[:np_, :], func=AF.Copy), reads=[r_pp], writes=[r_o32])
                self.dma("sp", outs["sbk"][k0:k0 + np_, u * 512:(u + 1) * 512], o32[:np_], reads=[r_o32])
                k_T_group(o32, r_o32, np_, 4 * u, k0)
            return ev

        def mk_sbv(u):
            def ev(ti, t, pp, r_pp):
                np_, k0 = t["np"], t["tok0"]
                o32, r_o32 = W["of32"].next()
                b16, r_b16 = W["ob16"].next()
                P.add("act", lambda e: e.activation(out=o32[:np_], in_=pp[:np_, :], func=AF.Copy), reads=[r_pp], writes=[r_o32])
                self.dma("sp", outs["sbv"][k0:k0 + np_, u * 512:(u + 1) * 512], o32[:np_], reads=[r_o32])
                P.add("dve", lambda e: e.tensor_copy(out=b16[:np_], in_=pp[:np_, :]), reads=[r_pp], writes=[r_b16])
                self.dma("sp", scr["v"][k0:k0 + np_, u * 512:(u + 1) * 512], b16[:np_], reads=[r_b16])
            return ev

        for u in range(H // 4):
            base = c.MLA_IN + c.HW + u * 512
            units.append(dict(blocks=[(base, 512)], evac=mk_sbk(u)))
        for u in range(H // 4):
            base = c.MLA_IN + 2 * c.HW + u * 512
            units.append(dict(blocks=[(base, 512)], evac=mk_sbv(u)))
        return units

    def ph_a1(self):
        c, P, nc = self.c, self.P, self.nc
        P.begin()
        with contextlib.ExitStack() as st:
            ntb = c.TB // 128
            W = {}
            W["xt"] = Rot([self.sb(st, "xt", [128, c.D], F32) for _ in range(2)])
            W["xn"] = Rot([self.sb(st, "xn", [128, c.D], BF16)])
            for k in ("ss", "sv", "rs"):
                W[k] = Rot([self.sb(st, k, [128, 1], F32) for _ in range(4)])
            W["ptr"] = Rot([self.ps(st, "ptr", [128, 8, 128], BF16) for _ in range(2)])
            W["ptf"] = Rot([self.ps(st, "ptf", [128, 4, 128], F32) for _ in range(2)])
            W["of32"] = Rot([self.sb(st, "of32", [128, 512], F32) for _ in range(3)])
            W["ob16"] = Rot([self.sb(st, "ob16", [128, 512], BF16) for _ in range(3)])
            W["latst"] = Rot([self.sb(st, "latst", [128, 4, 128], BF16) for _ in range(2)])
            W["krst"] = Rot([self.sb(st, "krst", [64, 128], BF16) for _ in range(2)])
            W["kst"] = Rot([self.sb(st, "kst", [128, 4, 128], BF16) for _ in range(3)])
            W["rt"] = Rot([self.sb(st, "rt", [128, 128], F32) for _ in range(2)])
            W["kr1"] = Rot([self.sb(st, "kr1", [128, 64], F32) for _ in range(2)])
            W["kr2"] = Rot([self.sb(st, "kr2", [128, 64], F32) for _ in range(2)])
            hT = self.sb(st, "hT", [128, c.KC, c.TB], BF16)
            wrot = Rot([self.sb(st, "wA", [128, c.KC, 512], BF16) for _ in range(2)])
            prot = Rot([self.ps(st, "pA", [128, 512], F32) for _ in range(3)])
            gkv = self.sb(st, "gkv", [128, 512], F32)
            r_gkv = Res()
            self.dma("sp", gkv[:], self.din["g_kv"].partition_broadcast(128), writes=[r_gkv])
            mv, r_mod = self.load_modvecs(st, [0, 1], Rot([self.ps(st, "pmv", [128, 128], F32)]))
            GT, SHT = self.make_GS(st, mv, r_mod, "g_mixT", 1, 0)
            P.mark("after_GS")
            streams = []
            for tb in range(c.SEQ // c.TB):
                tiles = []
                for k in range(ntb):
                    tok0 = tb * c.TB + k * 128
                    tiles.append(dict(np=128, col0=k * 128, tok0=tok0, x=self.din["xp"][tok0:tok0 + 128, :], rope=self.din["ropeKp"][tok0:tok0 + 128, :]))
                streams.append((0, "P", tiles, dict(lat=self.dout["p_lat"], kr=self.dout["p_kr"], sbk=self.dout["p_sbk"], sbv=self.dout["p_sbv"])))
            streams.append((1, "S", [dict(np=c.DS, col0=0, tok0=c.PAST, x=self.din["xs"][:, :], rope=self.din["ropeKs"][:, :], otok0=0)],
                            dict(lat=self.dout["s_lat"], kr=self.dout["s_kr"], sbk=self.dout["s_sbk"], sbv=self.dout["s_sbv"])))
            RH = [Res() for _ in range(ntb)]
            r0, _, tiles0, _ = streams[0]
            for ti, t in enumerate(tiles0):
                self.prep_hT(t["x"], t["np"], GT[r0], SHT[r0], r_mod, hT, t["col0"], RH[ti], W)
            for si, (r, sname, tiles, outs) in enumerate(streams):
                nxt = streams[si + 1] if si + 1 < len(streams) else None

                def after_tile(ti, nxt=nxt):
                    if nxt is None:
                        return
                    r2, _, tiles2, _ = nxt
                    if ti < len(tiles2):
                        t2 = tiles2[ti]
                        self.prep_hT(t2["x"], t2["np"], GT[r2], SHT[r2], r_mod, hT, t2["col0"], RH[ti], W)
                if sname == "S":
                    outs = {k: _Shift(v, -c.PAST) for k, v in outs.items()}
                units = self.kv_evacs(st, W, outs, self.scr[sname], gkv, r_gkv)
                self.proj_pass(st, tiles, units, self.din["w_in"], hT, RH[:len(tiles)], wrot, prot, after_tile=after_tile)
            P.end()

    def ph_a2(self):
        c, P, nc = self.c, self.P, self.nc
        H, QL = c.H, c.QL
        RC = QL // 128
        with contextlib.ExitStack() as st0:
            qlatT = self.sb(st0, "qlatT", [128, RC, c.NTA], BF16)
            r_qlatT = Res()
            P.begin()
            with contextlib.ExitStack() as st:
                GMAX = c.GMAX
                W = {}
                W["xt"] = Rot([self.sb(st, "xt", [128, c.D], F32) for _ in range(2)])
                W["xn"] = Rot([self.sb(st, "xn", [128, c.D], BF16)])
                for k in ("ss", "sv", "rs"):
                    W[k] = Rot([self.sb(st, k, [128, 1], F32) for _ in range(4)])
                W["ptr"] = Rot([self.ps(st, "ptr", [128, 8, 128], BF16) for _ in range(2)])
                hT = self.sb(st, "hT2", [128, c.KC, GMAX * 128], BF16)
                wrot = Rot([self.sb(st, "wA", [128, c.KC, 512], BF16) for _ in range(2)])
                prot = Rot([self.ps(st, "pA", [128, 512], F32) for _ in range(2)])
                mv, r_mod = self.load_modvecs(st, [0, 1], Rot([self.ps(st, "pmv", [128, 128], F32)]))
                GT, SHT = self.make_GS(st, mv, r_mod, "g_mixT", 1, 0)
                gq = self.sb(st, "gq", [128, QL], F32)
                r_gq = Res()
                self.dma("sp", gq[:], self.din["g_q"].partition_broadcast(128), writes=[r_gq])
                alltiles = []
                for s in range(c.NS):
                    alltiles.append(dict(np=128, gcol=s * 128, slot=s, r=0, x=self.din["xq"][s * 128:(s + 1) * 128, :]))
                alltiles.append(dict(np=c.DS, gcol=c.NTQ, slot=c.NS, r=1, x=self.din["xs"][:, :]))
                groups = []
                k = 0
                while k < len(alltiles):
                    n = (len(alltiles) - k) if len(alltiles) - k <= GMAX else GMAX - 1
                    groups.append(alltiles[k:k + n])
                    k += n
                qlf = [self.sb(st, "qlf", [128, QL], F32) for _ in range(GMAX)]
                ssq = [self.sb(st, "ssq", [128, 4], F32) for _ in range(GMAX)]
                junk = Rot([self.sb(st, "junk", [128, 512], BF16) for _ in range(2)])
                qn = Rot([self.sb(st, "qn", [128, QL], BF16) for _ in range(2)])
                b16 = Rot([self.sb(st, "b16", [128, 512], BF16) for _ in range(2)])
                sbqst = Rot([self.sb(st, "sbqst", [128, 4, 128], BF16) for _ in range(3)])
                nqu = QL // 512
                RH = [Res() for _ in range(GMAX)]
                r_qlf = [Res() for _ in range(GMAX)]
                r_ssq = [Res() for _ in range(GMAX)]
                gtiles = [[dict(t, col0=i * 128) for i, t in enumerate(g)] for g in groups]
                for ti, t in enumerate(gtiles[0]):
                    self.prep_hT(t["x"], t["np"], GT[t["r"]], SHT[t["r"]], r_mod, hT, t["col0"], RH[ti], W)
                for gi, tiles in enumerate(gtiles):
                    nxt = gtiles[gi + 1] if gi + 1 < len(gtiles) else None
                    r_hT = RH[:len(tiles)]

                    def after_tile(ti, nxt=nxt):
                        if nxt is not None and ti < len(nxt):
                            t2 = nxt[ti]
                            self.prep_hT(t2["x"], t2["np"], GT[t2["r"]], SHT[t2["r"]], r_mod, hT, t2["col0"], RH[ti], W)
                        if nxt is not None and ti == len(tiles) - 1:
                            for tj in range(len(tiles), len(nxt)):
                                t2 = nxt[tj]
                                self.prep_hT(t2["x"], t2["np"], GT[t2["r"]], SHT[t2["r"]], r_mod, hT, t2["col0"], RH[tj], W)
                    units = []

                    def mk_q(u, r_qlf=r_qlf, r_ssq=r_ssq):
                        def ev(ti, t, pp, r_pp):
                            np_ = t["np"]
                            j, r_j = junk.next()
                            P.add("act", lambda e: e.activation(out=qlf[ti][:np_, u * 512:(u + 1) * 512], in_=pp[:np_, :], func=AF.Copy), reads=[r_pp], writes=[r_qlf[ti]])
                            P.add("act", lambda e: e.activation(out=j[:np_], in_=pp[:np_, :], func=AF.Square, accum_out=ssq[ti][:np_, u:u + 1]), reads=[r_pp], writes=[r_j, r_ssq[ti]])
                            if u == nqu - 1:
                                ss, r_ss = W["ss"].next(); sv, r_sv = W["sv"].next(); rq, r_rq = W["rs"].next()
                                P.add("dve", lambda e: e.tensor_reduce(out=ss[:np_], in_=ssq[ti][:np_, 0:nqu], axis=AX.X, op=ALU.add), reads=[r_ssq[ti]], writes=[r_ss])
                                self.rsqrt_chain(ss, np_, 1.0 / QL, sv, rq, r_ss, r_sv, r_rq)
                                q, r_q = qn.next()
                                P.add("dve", lambda e: e.scalar_tensor_tensor(out=q[:np_], in0=qlf[ti][:np_], scalar=rq[:np_], in1=gq[:np_], op0=ALU.mult, op1=ALU.mult),
                                      reads=[r_qlf[ti], r_rq, r_gq], writes=[r_q])
                                gcol = t["gcol"]

                                def evq(k0, n, pt, r_pt):
                                    P.add("act", lambda e: e.activation(out=qlatT[:, k0:k0 + n, gcol:gcol + np_], in_=pt[:, 0:n, :np_], func=AF.Copy), reads=[r_pt], writes=[r_qlatT])
                                self.transposes_evac(q, np_, RC, W["ptr"], evq, r_q)
                        return ev

                    def mk_sq(u):
                        def ev(ti, t, pp, r_pp):
                            np_ = t["np"]
                            b, r_b = b16.next()
                            P.add("dve", lambda e: e.tensor_copy(out=b[:np_], in_=pp[:np_, :]), reads=[r_pp], writes=[r_b])
                            sq, r_sq = sbqst.next()

                            def evs(k0, n, pt, r_pt):
                                P.add("act", lambda e: e.activation(out=sq[:, 0:4, :np_], in_=pt[:, 0:4, :np_], func=AF.Copy), reads=[r_pt], writes=[r_sq])
                            self.transposes_evac(b, np_, 4, W["ptr"], evs, r_b, grp=4)
                            self.dma("sp", self.sbq[t["slot"], :, 4 * u:4 * u + 4, 0:np_], sq[:, :, :np_], reads=[r_sq])
                        return ev

                    for u in range(nqu):
                        units.append(dict(blocks=[(u * 512, 512)], evac=mk_q(u)))
                    for u in range(H // 4):
                        base = c.MLA_IN + u * 512
                        units.append(dict(blocks=[(base, 512)], evac=mk_sq(u)))
                    self.proj_pass(st, tiles, units, self.din["w_in"], hT, r_hT, wrot, prot, after_tile=after_tile)
                P.end()
            P.begin()
            r_qlatT = Res()
            with contextlib.ExitStack() as st:
                wuq = self.sb(st, "wuq", [128, RC, H * 192], BF16)
                wuqs = self.sb(st, "wuqs", [128, RC, H * 64], BF16)
                wuk = self.sb(st, "wuk", [128, H, 512], BF16)
                cs = self.sb(st, "cs", [64, 2, c.NTA], F32)
                r_w = Res()
                self.dma("pool", wuq[:], self.din["w_uq"].rearrange("(k p) n -> p k n", p=128), writes=[r_w])
                self.dma("pool", wuqs[:], self.din["w_uqs"].rearrange("(k p) n -> p k n", p=128), writes=[r_w])
                self.dma("pool", wuk[:], self.din["w_ukT"].rearrange("h n c -> n h c"), writes=[r_w])
                self.dma("sp", cs[:], self.din["ropeQ"][:, :, :], writes=[r_w])
                tgs = []
                s0 = 0
                while s0 < c.NS:
                    ns = min(4, c.NS - s0)
                    tgs.append((s0 * 128, ns * 128, s0, ns))
                    s0 += ns
                tgs.append((c.NTQ, c.DS, c.NS, 0))
                pq = Rot([self.ps(st, "pq", [128, 512], F32) for _ in range(6)])
                qnope = Rot([self.sb(st, "qnope", [128, 512], BF16) for _ in range(2)])
                qt1 = Rot([self.sb(st, "qt1", [64, 512], F32) for _ in range(2)])
                qt2 = Rot([self.sb(st, "qt2", [64, 512], F32) for _ in range(2)])
                qrst = Rot([self.sb(st, "qrst", [64, 512], BF16) for _ in range(2)])
                qast = Rot([self.sb(st, "qast", [128, 4, 512], BF16) for _ in range(2)])
                for h in range(H):
                    for (col0, n, s0, ns) in tgs:
                        pn, r_pn = pq.next()
                        for rc in range(RC):
                            P.add("pe", lambda e, rc=rc, pn=pn, h=h, col0=col0, n=n: e.matmul(pn[:, :n], lhsT=wuq[:, rc, h * 192:h * 192 + 128], rhs=qlatT[:, rc, col0:col0 + n], start=(rc == 0), stop=(rc == RC - 1)),
                                  reads=[r_w, r_qlatT], writes=[r_pn])
                        qp, r_qp = qnope.next()
                        P.add("act", lambda e, qp=qp, pn=pn, n=n: e.activation(out=qp[:, :n], in_=pn[:, :n], func=AF.Copy), reads=[r_pn], writes=[r_qp])
                        pr, r_pr = pq.next()
                        for rc in range(RC):
                            P.add("pe", lambda e, rc=rc, pr=pr, h=h, col0=col0, n=n: e.matmul(pr[0:64, :n], lhsT=wuq[:, rc, h * 192 + 128:h * 192 + 192], rhs=qlatT[:, rc, col0:col0 + n], start=(rc == 0), stop=(rc == RC - 1)),
                                  reads=[r_w, r_qlatT], writes=[r_pr])
                        pz, r_pz = pq.next()
                        for rc in range(RC):
                            P.add("pe", lambda e, rc=rc, pz=pz, h=h, col0=col0, n=n: e.matmul(pz[0:64, :n], lhsT=wuqs[:, rc, h * 64:(h + 1) * 64], rhs=qlatT[:, rc, col0:col0 + n], start=(rc == 0), stop=(rc == RC - 1)),
                                  reads=[r_w, r_qlatT], writes=[r_pz])
                        a1, r_a1 = qt1.next(); a2, r_a2 = qt2.next(); qr, r_qr = qrst.next()
                        P.add("dve", lambda e, a1=a1, pr=pr, col0=col0, n=n: e.tensor_tensor(out=a1[:, :n], in0=pr[0:64, :n], in1=cs[:, 0, col0:col0 + n], op=ALU.mult), reads=[r_pr, r_w], writes=[r_a1])
                        P.add("dve", lambda e, a2=a2, pz=pz, col0=col0, n=n: e.tensor_tensor(out=a2[:, :n], in0=pz[0:64, :n], in1=cs[:, 1, col0:col0 + n], op=ALU.mult), reads=[r_pz, r_w], writes=[r_a2])
                        P.add("dve", lambda e, a1=a1, a2=a2, qr=qr, n=n: e.tensor_tensor(out=qr[:, :n], in0=a1[:, :n], in1=a2[:, :n], op=ALU.add), reads=[r_a1, r_a2], writes=[r_qr])
                        qa, r_qa = qast.next()
                        for cc in range(4):
                            pa, r_pa = pq.next()
                            P.add("pe", lambda e, cc=cc, pa=pa, qp=qp, h=h, n=n: e.matmul(pa[:, :n], lhsT=wuk[:, h, cc * 128:(cc + 1) * 128], rhs=qp[:, :n], start=True, stop=True),
                                  reads=[r_w, r_qp], writes=[r_pa])
                            if cc % 2 == 0:
                                P.add("act", lambda e, cc=cc, pa=pa, qa=qa, n=n: e.activation(out=qa[:, cc, :n], in_=pa[:, :n], func=AF.Copy), reads=[r_pa], writes=[r_qa])
                            else:
                                P.add("dve", lambda e, cc=cc, pa=pa, qa=qa, n=n: e.tensor_copy(out=qa[:, cc, :n], in_=pa[:, :n]), reads=[r_pa], writes=[r_qa])
                        if ns > 0:
                            self.dma("sp", self.qrope[s0:s0 + ns, :, h, :].rearrange("s p q -> p s q"), qr[:, :n].rearrange("p (s q) -> p s q", q=128), reads=[r_qr])
                            for cc in range(4):
                                self.dma("sp", self.qabs[s0:s0 + ns, :, h, cc, :].rearrange("s p q -> p s q"), qa[:, cc, :n].rearrange("p (s q) -> p s q", q=128), reads=[r_qa])
                        else:
                            self.dma("sp", self.qrope[c.NS, :, h, 0:n], qr[:, :n], reads=[r_qr])
                            self.dma("sp", self.qabs[c.NS, :, h, :, 0:n], qa[:, :, :n], reads=[r_qa])
                P.end()

    def slots(self):
        c = self.c
        out = []
        for i in range(c.NS):
            blocks = [dict(k0=kb * 512, nk=512, mask=(kb == i)) for kb in range(i + 1)]
            out.append(dict(np=128, slot=i, stream="P", nkeys=(i + 1) * 512, blocks=blocks, col0=i * 128))
        blocks = [dict(k0=kb * 512, nk=min(512, c.PAST - kb * 512), mask=False) for kb in range((c.PAST + 511) // 512)]
        blocks.append(dict(k0=c.PAST, nk=c.DS, mask=True))
        out.append(dict(np=c.DS, slot=c.NS, stream="S", nkeys=c.KS, blocks=blocks, col0=c.NTQ))
        return out

    def merged_store(self, st, W, o32, r_o32, np_, kc0, col0, gT, r_g):
        c, P = self.c, self.P
        nch = c.HW // 128
        ss, r_ss = W["ss"].next(); sv, r_sv = W["sv"].next(); rm, r_rm = W["rs"].next()
        on, r_on = W["on16"].next()
        P.add("act", lambda e: e.activation(out=on[:np_], in_=o32[:np_], func=AF.Square, accum_out=ss[:np_]), reads=[r_o32], writes=[r_on, r_ss])
        self.rsqrt_chain(ss, np_, 1.0 / c.HW, sv, rm, r_ss, r_sv, r_rm)
        P.add("dve", lambda e: e.tensor_scalar(out=on[:np_], in0=o32[:np_], scalar1=rm[:np_], scalar2=None, op0=ALU.mult), reads=[r_o32, r_rm], writes=[r_on])
        mst, r_mst = W["mst"].next()

        def ev(k0, n, pt, r_pt):
            for j in range(n):
                k = k0 + j
                P.add("act", lambda e, k=k, j=j: e.activation(out=mst[:, k, :np_], in_=pt[:, j, :np_], func=AF.Identity, scale=gT[:, kc0 + k:kc0 + k + 1]),
                      reads=[r_pt, r_g], writes=[r_mst])
        self.transposes_evac(on, np_, nch, W["ptr"], ev, r_on)
        self.dma("sp", self.mT[:, kc0:kc0 + nch, col0:col0 + np_], mst[:, :, :np_], reads=[r_mst])

    @staticmethod
    def pipeline(units, stages):
        n = len(units)
        maxd = max(d for d, _ in stages)
        for step in range(n + maxd):
            for d, fn in stages:
                u = step - d
                if 0 <= u < n:
                    fn(units[u])

    def ph_mla(self):
        c, P, nc = self.c, self.P, self.nc
        H = c.H
        P.begin()
        with contextlib.ExitStack() as st:
            KT = (c.KMAX + 127) // 128
            lat = self.sb(st, "lat", [128, KT, 512], BF16)
            latT = self.sb(st, "latT", [128, 4, c.KMAX], BF16)
            krT = self.sb(st, "krT", [64, c.KMAX], BF16)
            r_K = Res()
            wuv = self.sb(st, "wuv", [128, 4, c.HW], BF16)
            r_wuv = Res()
            self.dma("pool", wuv[:], self.din["w_uv"].rearrange("(k p) n -> p k n", p=128), writes=[r_wuv])
            gT = self.sb(st, "goT", [128, c.KC], F32)
            r_g = Res()
            self.dma("sp", gT[:], self.din["g_outT"][:, :], writes=[r_g])
            mb = self.sb(st, "mb", [128, 512], F32)
            r_mb = Res()
            self.dma("sp", mb[:], self.din["m_mla"][:, :], writes=[r_mb])
            qa = Rot([self.sb(st, "qa", [128, H, 4, 128], BF16) for _ in range(2)])
            qr = Rot([self.sb(st, "qr", [64, H, 128], BF16) for _ in range(2)])
            W = {}
            for k in ("ss", "sv", "rs"):
                W[k] = Rot([self.sb(st, k, [128, 1], F32) for _ in range(4)])
            W["ptr"] = Rot([self.ps(st, "ptr", [128, 8, 128], BF16) for _ in range(2)])
            W["on16"] = Rot([self.sb(st, "on16", [128, c.HW], BF16)])
            W["mst"] = Rot([self.sb(st, "mst", [128, c.HW // 128, 128], BF16) for _ in range(2)])
            pS = Rot([self.ps(st, "pS", [128, 512], F32) for _ in range(3)])
            pO = Rot([self.ps(st, "pO", [128, 512], F32) for _ in range(2)])
            pV = Rot([self.ps(st, "pV", [128, 512], F32)])
            sm = Rot([self.sb(st, "sm", [128, 512], F32) for _ in range(2)])
            pexp = Rot([self.sb(st, "pexp", [128, 512], BF16) for _ in range(4)])
            pT = Rot([self.sb(st, "pT", [128, 4, 128], BF16) for _ in range(3)])
            rsum = Rot([self.sb(st, "rsum", [128, 16], F32) for _ in range(7)])
            for bb_, rr_ in zip(rsum.bufs, rsum.res):
                P.add("dve", lambda e, bb_=bb_: e.memset(bb_[:], 0.0), writes=[rr_])
            den = Rot([self.sb(st, "den", [128, 1], F32) for _ in range(2)])
            rden = Rot([self.sb(st, "rden", [128, 1], F32) for _ in range(2)])
            oln = Rot([self.sb(st, "oln", [128, 512], BF16) for _ in range(2)])
            olT = Rot([self.sb(st, "olT", [128, 4, 128], BF16) for _ in range(2)])
            o32 = Rot([self.sb(st, "o32", [128, c.HW], F32) for _ in range(2)])
            cur_stream = None
            for sl in self.slots():
                np_ = sl["np"]
                if sl["stream"] != cur_stream:
                    cur_stream = sl["stream"]
                    sc = self.scr[cur_stream]
                    nk = sc["nk"]
                    nfull = nk // 128
                    self.dma("sp", lat[:, 0:nfull, :], sc["lat"][0:nfull * 128, :].rearrange("(t p) c -> p t c", p=128), writes=[r_K])
                    if nk % 128:
                        self.dma("sp", lat[0:nk % 128, nfull, :], sc["lat"][nfull * 128:nk, :], writes=[r_K])
                    self.dma("sp", latT[:, :, 0:nk], sc["latT"][:, :, :], writes=[r_K])
                    self.dma("sp", krT[:, 0:nk], sc["krT"][:, :], writes=[r_K])
                qa_t, r_qa = qa.next()
                qr_t, r_qr = qr.next()
                self.dma("sp", qa_t[:, :, :, 0:np_], self.qabs[sl["slot"], :, :, :, 0:np_], writes=[r_qa])
                self.dma("sp", qr_t[:, :, 0:np_], self.qrope[sl["slot"], :, :, 0:np_], writes=[r_qr])
                o_t, r_o = o32.next()
                nb = len(sl["blocks"])
                is_p = (cur_stream == "P")
                units = []
                for h in range(H):
                    hd = dict(h=h)
                    for bi, blk in enumerate(sl["blocks"]):
                        units.append(dict(h=h, hd=hd, bi=bi, k0=blk["k0"], nk=blk["nk"], mask=(blk["mask"] and is_p), first=(bi == 0), last=(bi == nb - 1)))

                def stA(u):
                    h, k0, nk, bi = u["h"], u["k0"], u["nk"], u["bi"]
                    if u["first"]:
                        u["hd"]["O"] = pO.next()
                        u["hd"]["rs"] = rsum.next()
                    rs_t, r_rs = u["hd"]["rs"]
                    S, r_S = pS.next()
                    for cc in range(4):
                        P.add("pe", lambda e, cc=cc: e.matmul(S[:np_, :nk], lhsT=qa_t[:, h, cc, :np_], rhs=latT[:, cc, k0:k0 + nk], start=(cc == 0), stop=False),
                              reads=[r_qa, r_K], writes=[r_S])
                    P.add("pe", lambda e: e.matmul(S[:np_, :nk], lhsT=qr_t[0:64, h, :np_], rhs=krT[0:64, k0:k0 + nk], start=False, stop=True),
                          reads=[r_qr, r_K], writes=[r_S])
                    pe_t, r_pe = pexp.next()
                    u["pe"] = (pe_t, r_pe)
                    if u["mask"]:
                        sm_t, r_sm = sm.next()
                        P.add("dve", lambda e: e.scalar_tensor_tensor(out=sm_t[:np_, :nk], in0=S[:np_, :nk], scalar=c.MLA_SCALE, in1=mb[:np_, :nk], op0=ALU.mult, op1=ALU.add),
                              reads=[r_S, r_mb], writes=[r_sm])
                        P.add("act", lambda e: e.activation(out=pe_t[:np_, :nk], in_=sm_t[:np_, :nk], func=AF.Exp, accum_out=rs_t[:np_, bi:bi + 1]),
                              reads=[r_sm], writes=[r_pe, r_rs])
                    else:
                        P.add("act", lambda e: e.activation(out=pe_t[:np_, :nk], in_=S[:np_, :nk], func=AF.Exp, scale=c.MLA_SCALE, accum_out=rs_t[:np_, bi:bi + 1]),
                              reads=[r_S], writes=[r_pe, r_rs])

                def stB(u):
                    nk = u["nk"]
                    pe_t, r_pe = u["pe"]
                    nsub = (nk + 127) // 128
                    pt, r_pt = W["ptr"].next()
                    for j in range(nsub):
                        nkt = min(128, nk - j * 128)
                        P.add("pe", lambda e, j=j, nkt=nkt: e.transpose(out=pt[:nkt, j, :np_], in_=pe_t[:np_, j * 128:j * 128 + nkt], identity=self.identb[:np_, :np_]),
                              reads=[r_pe, self.r_const], writes=[r_pt])
                    pT_t, r_pT = pT.next()
                    u["pT"] = (pT_t, r_pT)
                    if nk % 128 == 0:
                        P.add("dve", lambda e: e.tensor_copy(out=pT_t[:, 0:nsub, :np_], in_=pt[:, 0:nsub, :np_]), reads=[r_pt], writes=[r_pT])
                    else:
                        for j in range(nsub):
                            nkt = min(128, nk - j * 128)
                            P.add("dve", lambda e, j=j, nkt=nkt: e.tensor_copy(out=pT_t[:nkt, j, :np_], in_=pt[:nkt, j, :np_]), reads=[r_pt], writes=[r_pT])

                def stC(u):
                    nk, k0 = u["nk"], u["k0"]
                    pT_t, r_pT = u["pT"]
                    O, r_O = u["hd"]["O"]
                    nsub = (nk + 127) // 128
                    for j in range(nsub):
                        nkt = min(128, nk - j * 128)
                        P.add("pe", lambda e, j=j, nkt=nkt, st_=(u["first"] and j == 0), sp_=(u["last"] and j == nsub - 1): e.matmul(
                            O[:np_, :], lhsT=pT_t[:nkt, j, :np_], rhs=lat[:nkt, k0 // 128 + j, :], start=st_, stop=sp_),
                            reads=[r_pT, r_K], writes=[r_O])

                def stD(u):
                    if not u["last"]:
                        return
                    O, r_O = u["hd"]["O"]
                    rs_t, r_rs = u["hd"]["rs"]
                    dn, r_dn = den.next()
                    rd, r_rd = rden.next()
                    P.add("dve", lambda e: e.tensor_reduce(out=dn[:np_], in_=rs_t[:np_, 0:nb], axis=AX.X, op=ALU.add), reads=[r_rs], writes=[r_dn])
                    P.add("dve", lambda e: e.reciprocal(out=rd[:np_], in_=dn[:np_]), reads=[r_dn], writes=[r_rd])
                    ol, r_ol = oln.next()
                    P.add("act", lambda e: e.activation(out=ol[:np_], in_=O[:np_, :], func=AF.Identity, scale=rd[:np_]), reads=[r_O, r_rd], writes=[r_ol])
                    u["hd"]["ol"] = (ol, r_ol)

                def stE(u):
                    if not u["last"]:
                        return
                    ol, r_ol = u["hd"]["ol"]
                    oT, r_oT = olT.next()
                    u["hd"]["oT"] = (oT, r_oT)

                    def ev(k0_, n, pt, r_pt):
                        P.add("dve", lambda e: e.tensor_copy(out=oT[:, 0:4, :np_], in_=pt[:, 0:4, :np_]), reads=[r_pt], writes=[r_oT])
                    self.transposes_evac(ol, np_, 4, W["ptr"], ev, r_ol, grp=4)

                def stF(u):
                    if not u["last"]:
                        return
                    h = u["h"]
                    oT, r_oT = u["hd"]["oT"]
                    V, r_V = pV.next()
                    for cc in range(4):
                        P.add("pe", lambda e, cc=cc: e.matmul(V[:np_, 0:128], lhsT=oT[:, cc, :np_], rhs=wuv[:, cc, h * 128:(h + 1) * 128], start=(cc == 0), stop=(cc == 3)),
                              reads=[r_oT, r_wuv], writes=[r_V])
                    P.add("act", lambda e: e.activation(out=o_t[:np_, h * 128:(h + 1) * 128], in_=V[:np_, 0:128], func=AF.Copy), reads=[r_V], writes=[r_o])

                self.pipeline(units, [(0, stA), (1, stB), (2, stC), (3, stD), (4, stE), (5, stF)])
                self.merged_store(st, W, o_t, r_o, np_, 0, sl["col0"], gT, r_g)
            P.end()

    def ph_sb(self):
        c, P, nc = self.c, self.P, self.nc
        H = c.H
        P.begin()
        with contextlib.ExitStack() as st:
            KT = (c.KMAX + 127) // 128
            gT = self.sb(st, "goT", [128, c.KC], F32)
            r_g = Res()
            self.dma("sp", gT[:], self.din["g_outT"][:, :], writes=[r_g])
            mk = self.sb(st, "mk", [128, 512], F32); nmk = self.sb(st, "nmk", [128, 512], F32)
            mks = self.sb(st, "mks", [c.DS, c.DS], F32); nmks = self.sb(st, "nmks", [c.DS, c.DS], F32)
            zeros = self.sb(st, "zeros", [128, 512], F32)
            r_mk = Res()
            self.dma("sp", mk[:], self.din["m_sb"][:, :], writes=[r_mk])
            self.dma("sp", nmk[:], self.din["m_nsb"][:, :], writes=[r_mk])
            self.dma("sp", mks[:], self.din["ms_sb"][:, :], writes=[r_mk])
            self.dma("sp", nmks[:], self.din["ms_nsb"][:, :], writes=[r_mk])
            P.add("dve", lambda e: e.memset(zeros[:], 0.0), writes=[r_mk])
            qs = Rot([self.sb(st, "qs", [128, H, 128], BF16) for _ in range(2)])
            kTh = Rot([self.sb(st, "kTh", [128, c.KMAX], BF16) for _ in range(5)])
            vh = Rot([self.sb(st, "vh", [128, KT, 128], BF16) for _ in range(5)])
            W = {}
            for k in ("ss", "sv", "rs"):
                W[k] = Rot([self.sb(st, k, [128, 1], F32) for _ in range(4)])
            W["ptr"] = Rot([self.ps(st, "ptr", [128, 8, 128], BF16) for _ in range(2)])
            W["on16"] = Rot([self.sb(st, "on16", [128, c.HW], BF16)])
            W["mst"] = Rot([self.sb(st, "mst", [128, c.HW // 128, 128], BF16) for _ in range(2)])
            ND = 6
            pZ = Rot([self.ps(st, "pZ", [128, 512], F32) for _ in range(3)])
            pOS = Rot([self.ps(st, "pOS", [128, 512], F32) for _ in range(2)])
            beta = Rot([self.sb(st, "beta", [128, 512], F32) for _ in range(ND)])
            nu = Rot([self.sb(st, "nu", [128, 512], F32) for _ in range(ND)])
            Pb = Rot([self.sb(st, "Pb", [128, 514], F32) for _ in range(ND)])
            a16 = Rot([self.sb(st, "a16", [128, 512], BF16) for _ in range(ND)])
            aT = Rot([self.sb(st, "aT", [128, 4, 128], BF16) for _ in range(3)])
            o32 = Rot([self.sb(st, "o32", [128, c.HW], F32) for _ in range(2)])
            for sl in self.slots():
                np_ = sl["np"]
                sc = self.scr[sl["stream"]]
                nkeys = sl["nkeys"]
                qs_t, r_qs = qs.next()
                self.dma("sp", qs_t[:, :, 0:np_], self.sbq[sl["slot"], :, :, 0:np_], writes=[r_qs])
                o_t, r_o = o32.next()
                blocks = list(reversed(sl["blocks"]))
                nb = len(blocks)
                m_ap, nm_ap = (mk, nmk) if sl["stream"] == "P" else (mks, nmks)
                units = []
                for h in range(H):
                    hd = dict(h=h, prevPb=None)
                    for bi, blk in enumerate(blocks):
                        units.append(dict(h=h, hd=hd, bi=bi, k0=blk["k0"], nk=blk["nk"], mask=blk["mask"], first=(bi == 0), last=(bi == nb - 1)))

                def stA(u):
                    h, k0, nk = u["h"], u["k0"], u["nk"]
                    hd = u["hd"]
                    if u["first"]:
                        kT_t, r_kT = kTh.next()
                        v_t, r_v = vh.next()
                        hd["kT"] = (kT_t, r_kT); hd["v"] = (v_t, r_v)
                        self.dma("sp", kT_t[:, 0:nkeys], sc["kT"][:, h, 0:nkeys], writes=[r_kT])
                        nfull = nkeys // 128
                        self.dma("sp", v_t[:, 0:nfull, :], sc["v"][0:nfull * 128, h * 128:(h + 1) * 128].rearrange("(t p) d -> p t d", p=128), writes=[r_v])
                        if nkeys % 128:
                            self.dma("sp", v_t[0:nkeys % 128, nfull, :], sc["v"][nfull * 128:nkeys, h * 128:(h + 1) * 128], writes=[r_v])
                        hd["OS"] = pOS.next()
                    kT_t, r_kT = hd["kT"]
                    Z, r_Z = pZ.next()
                    P.add("pe", lambda e: e.matmul(Z[:np_, :nk], lhsT=qs_t[:, h, :np_], rhs=kT_t[:, k0:k0 + nk], start=True, stop=True),
                          reads=[r_qs, r_kT], writes=[r_Z])
                    b_t, r_b = beta.next()
                    n_t, r_n = nu.next()
                    P.add("act", lambda e: e.activation(out=b_t[:np_, :nk], in_=Z[:np_, :nk], func=AF.Sigmoid, scale=c.SB_SCALE), reads=[r_Z], writes=[r_b])
                    P.add("act", lambda e: e.activation(out=n_t[:np_, :nk], in_=Z[:np_, :nk], func=AF.Sigmoid, scale=-c.SB_SCALE), reads=[r_Z], writes=[r_n])
                    if u["mask"]:
                        P.add("dve", lambda e: e.tensor_tensor(out=n_t[:np_, :nk], in0=n_t[:np_, :nk], in1=nm_ap[:np_, :nk], op=ALU.max),
                              reads=[r_n, r_mk], writes=[r_n])
                    pb_t, r_pb = Pb.next()
                    if hd["prevPb"] is None:
                        P.add("dve", lambda e: e.memset(pb_t[:np_, nk:nk + 1], 1.0), writes=[r_pb])
                    else:
                        ppb, r_ppb = hd["prevPb"]
                        P.add("dve", lambda e: e.tensor_copy(out=pb_t[:np_, nk:nk + 1], in_=ppb[:np_, 0:1]), reads=[r_ppb], writes=[r_pb])
                    P.add("dve", lambda e: e.tensor_tensor_scan(
                        out=pb_t[:np_, 0:nk][:, ::-1], data0=n_t[:np_, 0:nk][:, ::-1], data1=zeros[:np_, 0:nk], initial=pb_t[:np_, nk:nk + 1], op0=ALU.mult, op1=ALU.add),
                        reads=[r_n, r_mk, r_pb], writes=[r_pb])
                    hd["prevPb"] = (pb_t, r_pb)
                    a_t, r_a = a16.next()
                    u["a"] = (a_t, r_a)
                    P.add("pool", lambda e: e.tensor_tensor(out=a_t[:np_, :nk], in0=b_t[:np_, :nk], in1=pb_t[:np_, 1:nk + 1], op=ALU.mult),
                          reads=[r_b, r_pb], writes=[r_a])
                    if u["mask"]:
                        P.add("pool", lambda e: e.tensor_tensor(out=a_t[:np_, :nk], in0=a_t[:np_, :nk], in1=m_ap[:np_, :nk], op=ALU.mult),
                              reads=[r_a, r_mk], writes=[r_a])

                def stB(u):
                    nk = u["nk"]
                    a_t, r_a = u["a"]
                    nsub = (nk + 127) // 128
                    pt, r_pt = W["ptr"].next()
                    for j in range(nsub):
                        nkt = min(128, nk - j * 128)
                        P.add("pe", lambda e, j=j, nkt=nkt: e.transpose(out=pt[:nkt, j, :np_], in_=a_t[:np_, j * 128:j * 128 + nkt], identity=self.identb[:np_, :np_]),
                              reads=[r_a, self.r_const], writes=[r_pt])
                    aT_t, r_aT = aT.next()
                    u["aT"] = (aT_t, r_aT)
                    if nk % 128 == 0:
                        P.add("act", lambda e: e.activation(out=aT_t[:, 0:nsub, :np_], in_=pt[:, 0:nsub, :np_], func=AF.Copy), reads=[r_pt], writes=[r_aT])
                    else:
                        for j in range(nsub):
                            nkt = min(128, nk - j * 128)
                            P.add("act", lambda e, j=j, nkt=nkt: e.activation(out=aT_t[:nkt, j, :np_], in_=pt[:nkt, j, :np_], func=AF.Copy), reads=[r_pt], writes=[r_aT])

                def stC(u):
                    nk, k0, h = u["nk"], u["k0"], u["h"]
                    aT_t, r_aT = u["aT"]
                    OS, r_OS = u["hd"]["OS"]
                    v_t, r_v = u["hd"]["v"]
                    nsub = (nk + 127) // 128
                    for j in range(nsub):
                        nkt = min(128, nk - j * 128)
                        P.add("pe", lambda e, j=j, nkt=nkt, st_=(u["first"] and j == 0), sp_=(u["last"] and j == nsub - 1): e.matmul(
                            OS[:np_, 0:128], lhsT=aT_t[:nkt, j, :np_], rhs=v_t[:nkt, k0 // 128 + j, :], start=st_, stop=sp_),
                            reads=[r_aT, r_v], writes=[r_OS])
                    if u["last"]:
                        P.add("act", lambda e: e.activation(out=o_t[:np_, h * 128:(h + 1) * 128], in_=OS[:np_, 0:128], func=AF.Copy), reads=[r_OS], writes=[r_o])

                self.pipeline(units, [(0, stA), (3, stB), (4, stC)])
                self.merged_store(st, W, o_t, r_o, np_, c.HW // 128, sl["col0"], gT, r_g)
            P.end()

    def own_tiles(self):
        c = self.c
        tiles = []
        for s in range(c.NS):
            tiles.append(dict(np=128, col0=s * 128, r=0, x=self.din["xq"][s * 128:(s + 1) * 128, :], idx=s))
        tiles.append(dict(np=c.DS, col0=c.NTQ, r=1, x=self.din["xs"][:, :], idx=c.NS))
        return tiles

    def ph_wout(self):
        c, P, nc = self.c, self.P, self.nc
        P.begin()
        with contextlib.ExitStack() as st:
            mT = self.sb(st, "mT", [128, c.KC, c.NTA], BF16)
            r_mT = Res()
            self.dma("sp", mT[:], self.mT[:, :, :], writes=[r_mT])
            tiles = self.own_tiles()
            r_hT = [r_mT for _ in tiles]
            wrot = Rot([self.sb(st, "wA", [128, c.KC, 512], BF16) for _ in range(2)])
            prot = Rot([self.ps(st, "pA", [128, 512], F32) for _ in range(3)])
            gbc = [Rot([self.sb(st, "gbc", [128, 512], F32) for _ in range(2)]) for _ in range(2)]
            xb = Rot([self.sb(st, "xb", [128, 512], F32) for _ in range(3)])
            tb = Rot([self.sb(st, "tb", [128, 512], F32) for _ in range(3)])
            junk = Rot([self.sb(st, "junk", [128, 512], BF16) for _ in range(2)])
            units = []
            ss2, r_ss2 = self.ss2, self.r_ss2
            nU = c.D // 512

            def mk(u):
                cur = {}

                def ev(ti, t, pp, r_pp):
                    np_, r = t["np"], t["r"]
                    if r not in cur:
                        g, r_g = gbc[r].next()
                        self.dma("sp", g[:], self.modD[r, 2 * c.D + u * 512:2 * c.D + (u + 1) * 512].partition_broadcast(128), writes=[r_g])
                        cur[r] = (g, r_g)
                    g, r_g = cur[r]
                    x_t, r_x = xb.next()
                    t_t, r_t = tb.next()
                    j, r_j = junk.next()
                    self.dma("sp", x_t[:np_], t["x"][:, u * 512:(u + 1) * 512], writes=[r_x])
                    P.add("dve", lambda e: e.tensor_tensor(out=t_t[:np_], in0=pp[:np_, :], in1=g[:np_], op=ALU.mult), reads=[r_pp, r_g], writes=[r_t])
                    P.add("dve", lambda e: e.tensor_tensor(out=t_t[:np_], in0=t_t[:np_], in1=x_t[:np_], op=ALU.add), reads=[r_t, r_x], writes=[r_t])
                    P.add("act", lambda e: e.activation(out=j[:np_], in_=t_t[:np_], func=AF.Square, accum_out=ss2[:np_, t["idx"] * nU + u:t["idx"] * nU + u + 1]),
                          reads=[r_t], writes=[r_j, r_ss2])
                    self.dma("sp", self.x2[t["col0"]:t["col0"] + np_, u * 512:(u + 1) * 512], t_t[:np_], reads=[r_t])
                return ev

            for u in range(nU):
                units.append(dict(blocks=[(u * 512, 512)], evac=mk(u)))
            self.proj_pass(st, tiles, units, self.din["w_out"], mT, r_hT, wrot, prot)
            P.end()

    def ph_ffn(self):
        c, P, nc = self.c, self.P, self.nc
        FC = c.FC
        nU = c.D // 512
        P.begin()
        with contextlib.ExitStack() as st:
            W = {}
            DH = c.D // 2
            W["xt"] = Rot([self.sb(st, "xt", [128, DH], F32)])
            W["xn"] = Rot([self.sb(st, "xn", [128, DH], BF16)])
            for k in ("ss", "sv", "rs"):
                W[k] = Rot([self.sb(st, k, [128, 1], F32) for _ in range(4)])
            W["ptr"] = Rot([self.ps(st, "ptr", [128, 8, 128], BF16) for _ in range(1)])
            mv, r_mod = self.load_modvecs(st, [3, 4], Rot([self.ps(st, "pmv", [128, 128], F32)]))
            GT, SHT = self.make_GS(st, mv, r_mod, "g_ffnT", 4, 3)
            h2T = self.sb(st, "h2T", [128, c.KC, 512 + c.DS], BF16)
            actT = self.sb(st, "actT", [128, FC, 512 + c.DS], BF16)
            wg = Rot([self.sb(st, "wg", [128, c.KC, 128], BF16) for _ in range(2)])
            wu = Rot([self.sb(st, "wu", [128, c.KC, 128], BF16) for _ in range(2)])
            wd = Rot([self.sb(st, "wd", [128, 4, 512], BF16) for _ in range(2)])
            pbank = [self.ps(st, "pb", [128, 512], F32) for _ in range(5)]
            r_pbank = [Res(excl=True) for _ in range(5)]
            sil = Rot([self.sb(st, "sil", [128, 512 + c.DS], F32) for _ in range(2)])
            gbc = [Rot([self.sb(st, "gbc", [128, 512], F32) for _ in range(1)]) for _ in range(2)]
            xb = Rot([self.sb(st, "xb", [128, 512], F32) for _ in range(2)])
            tb = Rot([self.sb(st, "tb", [128, 512], F32) for _ in range(2)])
            junk = Rot([self.sb(st, "junk", [128, 512], BF16) for _ in range(1)])
            ss2, r_ss2, ss3, r_ss3 = self.ss2, self.r_ss2, self.ss3, self.r_ss3
            all_tiles = self.own_tiles()
            groups = []
            s0 = 0
            while s0 < c.NS:
                ns = min(4, c.NS - s0)
                g = [dict(t, lcol=k * 128) for k, t in enumerate(all_tiles[s0:s0 + ns])]
                groups.append(g)
                s0 += ns
            groups[0].append(dict(all_tiles[-1], lcol=512))
            for g in groups:
                r_h2 = Res()
                r_act = Res()
                npr = sum(t["np"] for t in g if t["r"] == 0)
                has_s = any(t["r"] == 1 for t in g)
                for t in g:
                    np_ = t["np"]
                    ss, r_ss = W["ss"].next(); sv, r_sv = W["sv"].next(); rs, r_rs = W["rs"].next()
                    P.add("dve", lambda e, ss=ss, t=t, np_=np_: e.tensor_reduce(out=ss[:np_], in_=ss2[:np_, t["idx"] * nU:(t["idx"] + 1) * nU], axis=AX.X, op=ALU.add), reads=[r_ss2], writes=[r_ss])
                    self.rsqrt_chain(ss, np_, 1.0 / c.D, sv, rs, r_ss, r_sv, r_rs)
                    GTr, SHr = GT[t["r"]], SHT[t["r"]]
                    lcol = t["lcol"]
                    for hf in range(2):
                        xt, r_xt = W["xt"].next(); xn, r_xn = W["xn"].next()
                        self.dma("sp", xt[:np_], self.x2[t["col0"]:t["col0"] + np_, hf * DH:(hf + 1) * DH], writes=[r_xt])
                        P.add("dve", lambda e, xn=xn, xt=xt, rs=rs, np_=np_: e.tensor_scalar(out=xn[:np_], in0=xt[:np_], scalar1=rs[:np_], scalar2=None, op0=ALU.mult), reads=[r_xt, r_rs], writes=[r_xn])
                        kb = hf * (c.KC // 2)

                        def evac(k0, n, pt, r_pt, np_=np_, lcol=lcol, GTr=GTr, SHr=SHr, kb=kb):
                            for j in range(n):
                                kc = kb + k0 + j
                                if j % 2 == 0:
                                    P.add("act", lambda e, kc=kc, j=j: e.activation(out=h2T[:, kc, lcol:lcol + np_], in_=pt[:, j, :np_], func=AF.Identity, scale=GTr[:, kc:kc + 1], bias=SHr[:, kc:kc + 1]),
                                          reads=[r_pt, r_mod], writes=[r_h2])
                                else:
                                    P.add("dve", lambda e, kc=kc, j=j: e.tensor_scalar(out=h2T[:, kc, lcol:lcol + np_], in0=pt[:, j, :np_], scalar1=GTr[:, kc:kc + 1], scalar2=SHr[:, kc:kc + 1], op0=ALU.mult, op1=ALU.add),
                                          reads=[r_pt, r_mod], writes=[r_h2])
                        self.transposes_evac(xn, np_, c.KC // 2, W["ptr"], evac, r_xn)
                for fc in range(FC):
                    wg_t, r_wg = wg.next()
                    wu_t, r_wu = wu.next()
                    self.dma("pool", wg_t[:], self.din["w_gate"][:, fc * 128:(fc + 1) * 128].rearrange("(k p) n -> p k n", p=128), writes=[r_wg])
                    self.dma("pool", wu_t[:], self.din["w_up"][:, fc * 128:(fc + 1) * 128].rearrange("(k p) n -> p k n", p=128), writes=[r_wu])
                    bg, bu = (fc % 2) * 2, (fc % 2) * 2 + 1
                    for (w_t, r_w, bk) in ((wg_t, r_wg, bg), (wu_t, r_wu, bu)):
                        for kc in range(c.KC):
                            P.add("pe", lambda e, kc=kc, w_t=w_t, bk=bk: e.matmul(pbank[bk][:, 0:npr], lhsT=w_t[:, kc, :], rhs=h2T[:, kc, 0:npr], start=(kc == 0), stop=(kc == c.KC - 1)),
                                  reads=[r_w, r_h2], writes=[r_pbank[bk]])
                    if has_s:
                        for (w_t, r_w, o) in ((wg_t, r_wg, 0), (wu_t, r_wu, 64)):
                            for kc in range(c.KC):
                                P.add("pe", lambda e, kc=kc, w_t=w_t, o=o: e.matmul(pbank[4][:, o:o + c.DS], lhsT=w_t[:, kc, :], rhs=h2T[:, kc, 512:512 + c.DS], start=(kc == 0), stop=(kc == c.KC - 1)),
                                      reads=[r_w, r_h2], writes=[r_pbank[4]])
                    s_t, r_s = sil.next()
                    P.add("act", lambda e, s_t=s_t, bg=bg: e.activation(out=s_t[:, 0:npr], in_=pbank[bg][:, 0:npr], func=AF.Silu), reads=[r_pbank[bg]], writes=[r_s])
                    P.add("dve", lambda e, s_t=s_t, bu=bu, fc=fc: e.tensor_tensor(out=actT[:, fc, 0:npr], in0=pbank[bu][:, 0:npr], in1=s_t[:, 0:npr], op=ALU.mult), reads=[r_pbank[bu], r_s], writes=[r_act])
                    if has_s:
                        P.add("act", lambda e, s_t=s_t: e.activation(out=s_t[:, 512:512 + c.DS], in_=pbank[4][:, 0:c.DS], func=AF.Silu), reads=[r_pbank[4]], writes=[r_s])
                        P.add("dve", lambda e, s_t=s_t, fc=fc: e.tensor_tensor(out=actT[:, fc, 512:512 + c.DS], in0=pbank[4][:, 64:64 + c.DS], in1=s_t[:, 512:512 + c.DS], op=ALU.mult),
                              reads=[r_pbank[4], r_s], writes=[r_act])
                for u in range(nU):
                    cur = {}
                    nfg = (FC + 3) // 4
                    for fg in range(nfg):
                        ng = min(4, FC - fg * 4)
                        wd_t, r_wd = wd.next()
                        self.dma("pool", wd_t[:, 0:ng, :], self.din["w_down"][fg * 512:fg * 512 + ng * 128, u * 512:(u + 1) * 512].rearrange("(g p) n -> p g n", p=128), writes=[r_wd])
                        for k, t in enumerate(g):
                            np_, lcol = t["np"], t["lcol"]
                            for gi in range(ng):
                                fc = fg * 4 + gi
                                P.add("pe", lambda e, k=k, gi=gi, fc=fc, np_=np_, lcol=lcol, wd_t=wd_t: e.matmul(
                                    pbank[k][:np_, :], lhsT=actT[:, fc, lcol:lcol + np_], rhs=wd_t[:, gi, :], start=(fc == 0), stop=(fc == FC - 1)),
                                    reads=[r_act, r_wd], writes=[r_pbank[k]])
                    for k, t in enumerate(g):
                        np_, r = t["np"], t["r"]
                        if r not in cur:
                            gb, r_gb = gbc[r].next()
                            self.dma("sp", gb[:], self.modD[r, 5 * c.D + u * 512:5 * c.D + (u + 1) * 512].partition_broadcast(128), writes=[r_gb])
                            cur[r] = (gb, r_gb)
                        gb, r_gb = cur[r]
                        x_t, r_x = xb.next()
                        t_t, r_t = tb.next()
                        j, r_j = junk.next()
                        self.dma("sp", x_t[:np_], self.x2[t["col0"]:t["col0"] + np_, u * 512:(u + 1) * 512], writes=[r_x])
                        P.add("dve", lambda e, k=k, t_t=t_t, gb=gb, np_=np_: e.tensor_tensor(out=t_t[:np_], in0=pbank[k][:np_, :], in1=gb[:np_], op=ALU.mult), reads=[r_pbank[k], r_gb], writes=[r_t])
                        P.add("dve", lambda e, t_t=t_t, x_t=x_t, np_=np_: e.tensor_tensor(out=t_t[:np_], in0=t_t[:np_], in1=x_t[:np_], op=ALU.add), reads=[r_t, r_x], writes=[r_t])
                        P.add("act", lambda e, j=j, t_t=t_t, np_=np_, t=t, u=u: e.activation(out=j[:np_], in_=t_t[:np_], func=AF.Square, accum_out=ss3[:np_, t["idx"] * nU + u:t["idx"] * nU + u + 1]),
                              reads=[r_t], writes=[r_j, r_ss3])
                        self.dma("sp", self.x3[t["col0"]:t["col0"] + np_, u * 512:(u + 1) * 512], t_t[:np_], reads=[r_t])
            P.end()

    def ph_final(self):
        c, P, nc = self.c, self.P, self.nc
        nU = c.D // 512
        P.begin()
        with contextlib.ExitStack() as st:
            gf = self.sb(st, "gf", [128, c.D], F32)
            r_gf = Res()
            self.dma("sp", gf[:], self.din["g_fin"].partition_broadcast(128), writes=[r_gf])
            xt = Rot([self.sb(st, "xt", [128, c.D], F32) for _ in range(2)])
            yt = Rot([self.sb(st, "yt", [128, c.D], F32) for _ in range(2)])
            W = {}
            for k in ("ss", "sv", "rs"):
                W[k] = Rot([self.sb(st, k, [128, 1], F32) for _ in range(4)])
            for t in self.own_tiles():
                np_ = t["np"]
                x_t, r_x = xt.next()
                y_t, r_y = yt.next()
                ss, r_ss = W["ss"].next(); sv, r_sv = W["sv"].next(); rs, r_rs = W["rs"].next()
                self.dma("sp", x_t[:np_], self.x3[t["col0"]:t["col0"] + np_, :], writes=[r_x])
                P.add("dve", lambda e, ss=ss, t=t, np_=np_: e.tensor_reduce(out=ss[:np_], in_=self.ss3[:np_, t["idx"] * nU:(t["idx"] + 1) * nU], axis=AX.X, op=ALU.add), reads=[self.r_ss3], writes=[r_ss])
                self.rsqrt_chain(ss, np_, 1.0 / c.D, sv, rs, r_ss, r_sv, r_rs)
                P.add("dve", lambda e, y_t=y_t, x_t=x_t, rs=rs, np_=np_: e.scalar_tensor_tensor(out=y_t[:np_], in0=x_t[:np_], scalar=rs[:np_], in1=gf[:np_], op0=ALU.mult, op1=ALU.mult),
                      reads=[r_x, r_rs, r_gf], writes=[r_y])
                if t["r"] == 0:
                    self.dma("sp", self.dout["y_p"][t["col0"]:t["col0"] + np_, :], y_t[:np_], reads=[r_y])
                else:
                    self.dma("sp", self.dout["y_s"][:, :], y_t[:np_], reads=[r_y])
            P.end()

    def build(self, phases=None):
        c, nc = self.c, self.nc
        self.declare()
        with contextlib.ExitStack() as st:
            self.P = Prog(nc, st)
            P = self.P
            self.identf = st.enter_context(nc.sbuf_tensor("identf", [128, 128], F32))
            self.identb = st.enter_context(nc.sbuf_tensor("identb", [128, 128], BF16))
            self.nhalf = st.enter_context(nc.sbuf_tensor("nhalf", [128, 1], F32))
            nU = c.D // 512
            self.ss2 = st.enter_context(nc.sbuf_tensor("ss2", [128, (c.NS + 1) * nU], F32))
            self.ss3 = st.enter_context(nc.sbuf_tensor("ss3", [128, (c.NS + 1) * nU], F32))
            self.r_ss2, self.r_ss3 = Res(), Res()
            self.r_const = Res()
            P.begin()
            idf, idb, nh = self.identf, self.identb, self.nhalf
            P.add("pool", lambda e: e.memset(idf[:], 1.0), writes=[self.r_const])
            P.add("pool", lambda e: e.affine_select(out=idf[:], in_=idf[:], pattern=[[-1, 128]], compare_op=ALU.is_equal, fill=0.0, base=0, channel_multiplier=1),
                  reads=[self.r_const], writes=[self.r_const])
            P.add("pool", lambda e: e.tensor_copy(out=idb[:], in_=idf[:]), reads=[self.r_const], writes=[self.r_const])
            P.add("pool", lambda e: e.memset(nh[:], -0.5), writes=[self.r_const])
            P.end()
            allph = [("mod", self.ph_mod), ("a1", self.ph_a1), ("a2", self.ph_a2), ("mla", self.ph_mla), ("sb", self.ph_sb),
                     ("wout", self.ph_wout), ("ffn", self.ph_ffn), ("final", self.ph_final)]
            for name, fn in allph:
                if phases is None or name in phases:
                    fn()
        return nc


class _Shift:
    def __init__(self, ap, shift):
        self.ap, self.shift = ap, shift

    def __getitem__(self, key):
        rows, cols = key
        rows = slice(rows.start + self.shift, rows.stop + self.shift)
        return self.ap[rows, cols]


def rope_tables(pos, dtype=np.float32):
    half = 32
    inv = (1.0 / (np.float32(10000.0) ** (np.arange(half, dtype=np.float32) / np.float32(half)))).astype(np.float32)
    ang = pos.astype(np.float32)[:, None] * inv[None, :]
    return np.cos(ang).astype(dtype), np.sin(ang).astype(dtype)


def prep_inputs(cfg, inp):
    c = cfg
    D, H, KC = c.D, c.H, c.KC
    f = lambda a: np.ascontiguousarray(np.asarray(a, dtype=np.float32))
    shared = {}
    shared["w_ada"] = f(inp["w_ada"][0]); shared["b_ada"] = f(inp["b_ada"][0])
    featT = lambda v: f(np.asarray(v).reshape(KC, 128).T)
    shared["g_mixT"] = featT(inp["g_mix"][0]); shared["g_ffnT"] = featT(inp["g_ffn"][0])
    shared["g_outT"] = featT(np.concatenate([np.asarray(inp["g_out_mla"][0]), np.asarray(inp["g_out_sb"][0])]))
    shared["w_in"] = f(inp["w_in"][0]); shared["g_q"] = f(inp["g_q_lat"][0]); shared["g_kv"] = f(inp["g_kv_lat"][0])
    wuq = np.asarray(inp["w_uq"][0])
    shared["w_uq"] = f(wuq.reshape(c.QL, H * 192))
    sw = np.concatenate([wuq[:, :, 160:192], wuq[:, :, 128:160]], axis=2)
    shared["w_uqs"] = f(sw.reshape(c.QL, H * 64))
    shared["w_ukT"] = f(np.transpose(np.asarray(inp["w_uk"][0]), (1, 2, 0)))
    shared["w_uv"] = f(np.asarray(inp["w_uv"][0]).reshape(512, H * 128))
    shared["w_out"] = f(inp["w_out"][0]); shared["w_gate"] = f(inp["w_gate"][0]); shared["w_up"] = f(inp["w_up"][0])
    shared["w_down"] = f(inp["w_down"][0]); shared["g_fin"] = f(inp["g_final"])
    cosp, sinp = rope_tables(np.arange(c.SEQ))
    shared["ropeKp"] = f(np.concatenate([cosp, cosp, -sinp, sinp], axis=1))
    coss, sins = rope_tables(c.PAST + np.arange(c.DS))
    shared["ropeKs"] = f(np.concatenate([coss, coss, -sins, sins], axis=1))
    qi = np.arange(c.DS)[:, None]; ki = np.arange(c.DS)[None, :]
    ms = (ki < qi).astype(np.float32)
    shared["ms_sb"] = f(ms); shared["ms_nsb"] = f(1.0 - ms)
    xpr = np.asarray(inp["x_prompt"]); xsm = np.asarray(inp["x_sample"])
    cp = np.asarray(inp["c_prompt"]); csm = np.asarray(inp["c_sample"])
    in_maps = []
    for core in range(c.NCORES):
        b, j = core // c.G, core % c.G
        m = dict(shared)
        m["xp"] = f(xpr[b])
        own_tiles = [c.G * i + j for i in range(c.NS)]
        pos_own = np.concatenate([np.arange(t * 128, (t + 1) * 128) for t in own_tiles])
        m["xq"] = f(xpr[b][pos_own])
        m["xs"] = f(xsm[core])
        m["c_lat"] = f(inp["cache_mla_latent"][0][core]); m["c_kr"] = f(inp["cache_mla_krope"][0][core])
        m["c_k"] = f(np.asarray(inp["cache_sb_k"][0][core]).reshape(c.PAST, c.HW))
        m["c_v"] = f(np.asarray(inp["cache_sb_v"][0][core]).reshape(c.PAST, c.HW))
        cc = np.stack([cp[b], csm[core]], axis=0)
        m["cT"] = f(cc.reshape(2, KC, 128).transpose(2, 1, 0).reshape(128, KC * 2))
        pos_all = np.concatenate([pos_own, c.PAST + np.arange(c.DS)])
        cq, sq = rope_tables(pos_all)
        cs1 = np.concatenate([cq, cq], axis=1).T
        cs2 = np.concatenate([-sq, sq], axis=1).T
        m["ropeQ"] = f(np.stack([cs1, cs2], axis=1))
        qpos = 128 * j + np.arange(128)[:, None]
        kk = np.arange(512)[None, :]
        vis_mla = (kk // 64) <= (qpos // 64)
        m["m_mla"] = f(np.where(vis_mla, 0.0, NEG))
        vis_sb = kk < qpos
        m["m_sb"] = f(vis_sb.astype(np.float32)); m["m_nsb"] = f(1.0 - vis_sb.astype(np.float32))
        in_maps.append(m)
    return in_maps


def assemble(cfg, results):
    c = cfg
    D, H = c.D, c.H
    y_p = np.zeros((c.NB, c.SEQ, D), np.float32)
    y_s = np.zeros((c.NCORES, c.DS, D), np.float32)
    p_lat = np.zeros((1, c.NB, c.SEQ, 512), np.float32); p_kr = np.zeros((1, c.NB, c.SEQ, 64), np.float32)
    p_k = np.zeros((1, c.NB, c.SEQ, H, 128), np.float32); p_v = np.zeros((1, c.NB, c.SEQ, H, 128), np.float32)
    s_lat = np.zeros((1, c.NCORES, c.DS, 512), np.float32); s_kr = np.zeros((1, c.NCORES, c.DS, 64), np.float32)
    s_k = np.zeros((1, c.NCORES, c.DS, H, 128), np.float32); s_v = np.zeros((1, c.NCORES, c.DS, H, 128), np.float32)
    for core in range(c.NCORES):
        r = results[core]
        b, j = core // c.G, core % c.G
        for i in range(c.NS):
            t = c.G * i + j
            y_p[b, t * 128:(t + 1) * 128] = r["y_p"][i * 128:(i + 1) * 128]
        y_s[core] = r["y_s"]
        if j == 0:
            p_lat[0, b] = r["p_lat"]; p_kr[0, b] = r["p_kr"]
            p_k[0, b] = r["p_sbk"].reshape(c.SEQ, H, 128); p_v[0, b] = r["p_sbv"].reshape(c.SEQ, H, 128)
        s_lat[0, core] = r["s_lat"]; s_kr[0, core] = r["s_kr"]
        s_k[0, core] = r["s_sbk"].reshape(c.DS, H, 128); s_v[0, core] = r["s_sbv"].reshape(c.DS, H, 128)
    return (y_p, y_s, p_lat, p_kr, p_k, p_v, s_lat, s_kr, s_k, s_v)


def run_cfg(cfg, inputs, phases=None, trace=False):
    b = Builder(cfg)
    nc = b.build(phases)
    in_maps = prep_inputs(cfg, inputs)
    res = run_bass_kernel_spmd(nc, in_maps, core_ids=list(range(cfg.NCORES)), **({"trace": True} if trace else {}))
    return res


def kernel(**inputs):
    cfg = Cfg()
    res = run_cfg(cfg, inputs)
    return assemble(cfg, res.results)
```

```python
import contextlib
import math
import types
import os
DBG = set(os.environ.get('KDBG', '').split(','))
import numpy as np
import concourse.bass as bass
import concourse.mybir as mybir
from concourse.bass_utils import run_bass_kernel_spmd

F32 = mybir.dt.float32
BF16 = mybir.dt.bfloat16
AF = mybir.ActivationFunctionType
ALU = mybir.AluOpType
AX = mybir.AxisListType
EPS = 1e-6
NEG = -30000.0


class Cfg:
    def __init__(s, D=4096, QL=1024, DFF=11008, SEQ=4096, PAST=2048, DS=32, NB=2, G=4, NCORES=8, TB=1024, GMAX=5):
        s.D, s.QL, s.DFF, s.SEQ, s.PAST, s.DS, s.NB, s.G, s.NCORES = D, QL, DFF, SEQ, PAST, DS, NB, G, NCORES
        s.H = D // 256
        s.HW = s.H * 128
        s.KC = D // 128
        s.MLA_IN = QL + 512 + 64
        s.INC = s.MLA_IN + 3 * s.HW
        s.NS = SEQ // (128 * G)
        s.NTQ = s.NS * 128
        s.NTA = s.NTQ + DS
        s.TB = min(TB, SEQ)
        s.GMAX = GMAX
        s.FC = DFF // 128
        s.KS = PAST + DS
        s.KMAX = max(SEQ, s.KS)
        s.MLA_SCALE = (128 + 64) ** -0.5
        s.SB_SCALE = 128 ** -0.5


PSUM_IDS = set()


def _freeze(fn):
    if fn.__closure__ is None:
        return fn
    cells = []
    for cl in fn.__closure__:
        try:
            cells.append(types.CellType(cl.cell_contents))
        except ValueError:
            cells.append(cl)
    return types.FunctionType(fn.__code__, fn.__globals__, fn.__name__, fn.__defaults__, tuple(cells))


class Res:
    __slots__ = ("name", "w", "rs", "excl")

    def __init__(self, name="", excl=False):
        self.name = name
        self.w = None
        self.rs = []
        self.excl = excl


class Op:
    __slots__ = ("eng", "fn", "reads", "writes", "dma", "deps", "marked", "sem", "cnt", "waits", "ph")

    def __init__(self, eng, fn, reads, writes, dma):
        self.eng = eng; self.fn = fn; self.reads = reads; self.writes = writes; self.dma = dma
        self.deps = []; self.marked = False; self.sem = None; self.cnt = 0; self.waits = []; self.ph = 0


class Prog:
    ENGS = ("pe", "act", "dve", "pool", "sp")

    def __init__(self, nc, stack, n_dma_sems=8, sem_limit=4000):
        self.nc = nc
        self.stack = stack
        self.n_dma_sems = n_dma_sems
        self.sem_limit = sem_limit
        self.eng_sem = {}
        self.eng_cnt = {}
        self.dma_sems = {}
        self.dma_rr = {}
        self.dma_last = {}
        self.nsem = 0
        self.prev_done = None
        self.ops = []
        self.nops_total = 0
        self.phase_idx = 0

    def newsem(self, tag):
        self.nsem += 1
        return self.stack.enter_context(self.nc.semaphore("s_%s_%d" % (tag, self.nsem)))

    def begin(self):
        self.ops = []
        self.phase_idx += 1

    def add(self, eng, fn, reads=(), writes=(), dma=False):
        op = Op(eng, _freeze(fn), list(reads), list(writes), dma)
        op.ph = self.phase_idx
        self.ops.append(op)
        return op

    def mark(self, name):
        if "marks" in DBG:
            print("MARK phase %d %s ops=%d" % (self.phase_idx, name, len(self.ops)))

    def end(self):
        nc = self.nc
        mo = os.environ.get("KMAXOPS")
        if mo:
            ph, n = mo.split(":")
            if int(ph) == self.phase_idx:
                self.ops = self.ops[:int(n)]
        ops = self.ops
        self.nops_total += len(ops)
        for op in ops:
            deps = []
            reads = [r for r in op.reads if not r.excl]
            writes = op.writes + [r for r in op.reads if r.excl]
            for r in reads:
                if r.w is not None:
                    deps.append(r.w)
            for r in writes:
                if r.w is not None:
                    deps.append(r.w)
                deps.extend(r.rs)
            for r in reads:
                r.rs.append(op)
            for r in writes:
                r.w = op
                r.rs = []
            seen = set()
            for d in deps:
                if d is op or id(d) in seen or d.ph != op.ph:
                    continue
                seen.add(id(d))
                if (not d.dma) and (not op.dma) and d.eng == "pe" and op.eng == "pe":
                    continue
                op.deps.append(d)
                d.marked = True
        last_comp = {}
        for op in ops:
            if not op.dma:
                last_comp[op.eng] = op
        for op in last_comp.values():
            op.marked = True
        waited = {e: {} for e in self.ENGS}
        for op in ops:
            e = op.eng
            w = waited[e]
            waits = []
            for d in op.deps:
                key = id(d.sem)
                if w.get(key, 0) >= d.cnt:
                    continue
                w[key] = d.cnt
                waits.append((d.sem, d.cnt))
            if op.dma:
                if e not in self.dma_sems:
                    self.dma_sems[e] = [self.newsem("dma" + e) for _ in range(self.n_dma_sems)]
                    self.dma_rr[e] = 0
                    self.dma_last[e] = [0] * self.n_dma_sems
                i = self.dma_rr[e]
                self.dma_rr[e] = (i + 1) % self.n_dma_sems
                s = self.dma_sems[e][i]
                prev = self.dma_last[e][i]
                if prev > 0 and w.get(id(s), 0) < prev:
                    w[id(s)] = prev
                    waits.append((s, prev))
                op.sem = s
                op.cnt = prev + 16
                self.dma_last[e][i] = op.cnt
                op.marked = True
            elif op.marked:
                if e not in self.eng_sem or self.eng_cnt[e] >= self.sem_limit:
                    self.eng_sem[e] = self.newsem(e)
                    self.eng_cnt[e] = 0
                self.eng_cnt[e] += 1
                op.sem = self.eng_sem[e]
                op.cnt = self.eng_cnt[e]
            m = {}
            for s, c in waits:
                k = id(s)
                if k not in m or m[k][1] < c:
                    m[k] = (s, c)
            op.waits = list(m.values())
        by_eng = {e: [op for op in ops if op.eng == e] for e in self.ENGS}
        done = self.newsem("done")
        prev_done = self.prev_done
        prog = self

        with nc.Block() as block:
            def run(eng_obj, ename):
                if prev_done is not None:
                    eng_obj.wait_ge(prev_done, len(prog.ENGS))
                for op in by_eng[ename]:
                    for s, c in op.waits:
                        eng_obj.wait_ge(s, c)
                    ins = op.fn(eng_obj)
                    if op.marked:
                        ins.then_inc(op.sem, 16 if op.dma else 1)
                if ename in prog.dma_sems:
                    for s, c in zip(prog.dma_sems[ename], prog.dma_last[ename]):
                        if c > 0:
                            eng_obj.wait_ge(s, c)
                lc = last_comp.get(ename)
                if lc is not None:
                    eng_obj.wait_ge(lc.sem, lc.cnt)
                eng_obj.sem_inc(done, 1)

            @block.tensor
            def _(eng):
                run(eng, "pe")

            @block.scalar
            def _(eng):
                run(eng, "act")

            @block.vector
            def _(eng):
                run(eng, "dve")

            @block.gpsimd
            def _(eng):
                run(eng, "pool")

            @block.sync
            def _(eng):
                run(eng, "sp")

        self.prev_done = done
        self.ops = []


class Rot:
    def __init__(self, bufs):
        self.bufs = bufs
        self.res = [Res(excl=(id(b) in PSUM_IDS)) for b in bufs]
        self.i = 0

    def next(self):
        k = self.i % len(self.bufs)
        self.i += 1
        return self.bufs[k], self.res[k]


class Builder:
    def __init__(self, cfg):
        self.c = cfg
        self.nc = bass.Bass("TRN2", target_bir_lowering=False)
        self.din = {}
        self.dout = {}

    def declare(self):
        c, nc = self.c, self.nc

        def I(name, shape):
            self.din[name] = nc.dram_tensor(name, list(shape), F32, kind="ExternalInput").ap()

        def O(name, shape):
            self.dout[name] = nc.dram_tensor(name, list(shape), F32, kind="ExternalOutput").ap()

        def S(name, shape, dt):
            return nc.dram_tensor(name, list(shape), dt).ap()

        D, H, HW, KC = c.D, c.H, c.HW, c.KC
        I("xp", [c.SEQ, D]); I("xq", [c.NTQ, D]); I("xs", [c.DS, D])
        I("c_lat", [c.PAST, 512]); I("c_kr", [c.PAST, 64]); I("c_k", [c.PAST, HW]); I("c_v", [c.PAST, HW])
        I("cT", [128, KC * 2])
        I("w_ada", [D, 6 * D]); I("b_ada", [6 * D])
        I("g_mixT", [128, KC]); I("g_ffnT", [128, KC]); I("g_outT", [128, KC])
        I("w_in", [D, c.INC]); I("g_q", [c.QL]); I("g_kv", [512])
        I("w_uq", [c.QL, H * 192]); I("w_uqs", [c.QL, H * 64]); I("w_ukT", [H, 128, 512]); I("w_uv", [512, HW])
        I("w_out", [D, D]); I("w_gate", [D, c.DFF]); I("w_up", [D, c.DFF]); I("w_down", [c.DFF, D]); I("g_fin", [D])
        I("ropeKp", [c.SEQ, 128]); I("ropeKs", [c.DS, 128]); I("ropeQ", [64, 2, c.NTA])
        I("m_mla", [128, 512]); I("m_sb", [128, 512]); I("m_nsb", [128, 512])
        I("ms_sb", [c.DS, c.DS]); I("ms_nsb", [c.DS, c.DS])
        O("y_p", [c.NTQ, D]); O("y_s", [c.DS, D])
        O("p_lat", [c.SEQ, 512]); O("p_kr", [c.SEQ, 64]); O("p_sbk", [c.SEQ, HW]); O("p_sbv", [c.SEQ, HW])
        O("s_lat", [c.DS, 512]); O("s_kr", [c.DS, 64]); O("s_sbk", [c.DS, HW]); O("s_sbv", [c.DS, HW])
        self.modD = S("modD", [2, 6 * D], F32)
        self.scr = {}
        for st, nk in (("P", c.SEQ), ("S", c.KS)):
            self.scr[st] = dict(
                lat=S("lat" + st, [nk, 512], BF16), latT=S("latT" + st, [128, 4, nk], BF16),
                krT=S("krT" + st, [64, nk], BF16), kT=S("kT" + st, [128, H, nk], BF16), v=S("v" + st, [nk, HW], BF16), nk=nk)
        self.qabs = S("qabs", [c.NS + 1, 128, H, 4, 128], BF16)
        self.qrope = S("qrope", [c.NS + 1, 64, H, 128], BF16)
        self.sbq = S("sbq", [c.NS + 1, 128, H, 128], BF16)
        self.mT = S("mTs", [128, KC, c.NTA], BF16)
        self.x2 = S("x2s", [c.NTA, D], F32)
        self.x3 = S("x3s", [c.NTA, D], F32)

    def dma(self, q, out, in_, reads=(), writes=()):
        return self.P.add(q, lambda e: e.dma_start(out=out, in_=in_), reads, writes, dma=True)

    def rsqrt_chain(self, ss, np_, inv_n, tmp, out, r_ss, r_tmp, r_out):
        P = self.P
        P.add("dve", lambda e: e.tensor_scalar(out=tmp[:np_], in0=ss[:np_], scalar1=inv_n, scalar2=EPS, op0=ALU.mult, op1=ALU.add),
              reads=[r_ss], writes=[r_tmp])
        nh = self.nhalf
        P.add("pool", lambda e: e.tensor_tensor(out=out[:np_], in0=tmp[:np_], in1=nh[:np_], op=ALU.pow),
              reads=[r_tmp, self.r_const], writes=[r_out])

    def transposes_evac(self, src, np_, nchunk, ptr_rot, evac, r_src, dtype_is_bf16=True, grp=8):
        P = self.P
        ident = self.identb if dtype_is_bf16 else self.identf
        k0 = 0
        while k0 < nchunk:
            n = min(grp, nchunk - k0)
            pt, r_pt = ptr_rot.next()
            for j in range(n):
                k = k0 + j
                P.add("pe", lambda e, k=k, j=j, pt=pt: e.transpose(out=pt[:, j, :np_], in_=src[:np_, k * 128:(k + 1) * 128], identity=ident[:np_, :np_]),
                      reads=[r_src, self.r_const], writes=[r_pt])
            evac(k0, n, pt, r_pt)
            k0 += n

    def prep_s1(self, x_ap, np_, W):
        c, P = self.c, self.P
        xt, r_xt = W["xt"].next()
        xn, r_xn = W["xn"].next()
        ss, r_ss = W["ss"].next()
        sv, r_sv = W["sv"].next()
        rs, r_rs = W["rs"].next()
        self.dma("sp", xt[:np_], x_ap, writes=[r_xt])
        P.add("act", lambda e: e.activation(out=xn[:np_], in_=xt[:np_], func=AF.Square, accum_out=ss[:np_]), reads=[r_xt], writes=[r_xn, r_ss])
        self.rsqrt_chain(ss, np_, 1.0 / c.D, sv, rs, r_ss, r_sv, r_rs)
        P.add("dve", lambda e: e.tensor_scalar(out=xn[:np_], in0=xt[:np_], scalar1=rs[:np_], scalar2=None, op0=ALU.mult), reads=[r_xt, r_rs], writes=[r_xn])
        return (xn, r_xn, np_)

    def prep_s2(self, h, GT, SHT, r_mod, hT, col0, r_hT, W):
        c, P = self.c, self.P
        xn, r_xn, np_ = h
        cnt = [0]

        def evac(k0, n, pt, r_pt):
            for j in range(n):
                kc = k0 + j
                if cnt[0] % 2 == 0:
                    P.add("act", lambda e, kc=kc, j=j, pt=pt: e.activation(out=hT[:, kc, col0:col0 + np_], in_=pt[:, j, :np_], func=AF.Identity,
                                                                        scale=GT[:, kc:kc + 1], bias=SHT[:, kc:kc + 1]),
                          reads=[r_pt, r_mod], writes=[r_hT])
                else:
                    P.add("dve", lambda e, kc=kc, j=j, pt=pt: e.tensor_scalar(out=hT[:, kc, col0:col0 + np_], in0=pt[:, j, :np_], scalar1=GT[:, kc:kc + 1],
                                                                           scalar2=SHT[:, kc:kc + 1], op0=ALU.mult, op1=ALU.add),
                          reads=[r_pt, r_mod], writes=[r_hT])
                cnt[0] += 1

        self.transposes_evac(xn, np_, c.KC, W["ptr"], evac, r_xn)

    def prep_hT(self, x_ap, np_, GT, SHT, r_mod, hT, col0, r_hT, W):
        h = self.prep_s1(x_ap, np_, W)
        self.prep_s2(h, GT, SHT, r_mod, hT, col0, r_hT, W)

    def load_modvecs(self, st, chunks, psum_rot):
        c, P = self.c, self.P
        r_mod = Res("mod")
        out = {}
        for r in range(2):
            for ch in chunks:
                t = st.enter_context(self.nc.sbuf_tensor("mv_%d_%d_%d" % (self.uid(), r, ch), [128, c.KC], F32))
                tmp = st.enter_context(self.nc.sbuf_tensor("mvt_%d_%d_%d" % (self.uid(), r, ch), [c.KC, 128], F32))
                r_tmp = Res()
                self.dma("sp", tmp[:], self.modD[r, ch * c.D:(ch + 1) * c.D].rearrange("(k p) -> k p", p=128), writes=[r_tmp])
                pt, r_pt = psum_rot.next()
                P.add("pe", lambda e, pt=pt, tmp=tmp: e.transpose(out=pt[:, :c.KC], in_=tmp[:, :], identity=self.identf[:c.KC, :c.KC]),
                      reads=[r_tmp, self.r_const], writes=[r_pt])
                P.add("dve", lambda e, pt=pt, t=t: e.tensor_copy(out=t[:], in_=pt[:, :c.KC]), reads=[r_pt], writes=[r_mod])
                out[(r, ch)] = t
        return out, r_mod

    _uid = 0

    def uid(self):
        Builder._uid += 1
        return Builder._uid

    def sb(self, st, name, shape, dt):
        return st.enter_context(self.nc.sbuf_tensor("%s_%d" % (name, self.uid()), list(shape), dt))

    def ps(self, st, name, shape, dt):
        t = st.enter_context(self.nc.psum_tensor("%s_%d" % (name, self.uid()), list(shape), dt))
        PSUM_IDS.add(id(t))
        self._keep = getattr(self, "_keep", [])
        self._keep.append(t)
        return t

    def make_GS(self, st, mv, r_mod, gT_name, ch_scale, ch_shift):
        c, P = self.c, self.P
        g = self.sb(st, "gT", [128, c.KC], F32)
        r_g = Res()
        self.dma("sp", g[:], self.din[gT_name][:, :], writes=[r_g])
        GT = []
        SHT = []
        for r in range(2):
            G = self.sb(st, "G", [128, c.KC], F32)
            sc = mv[(r, ch_scale)]
            P.add("dve", lambda e, G=G, sc=sc: e.scalar_tensor_tensor(out=G[:], in0=sc[:], scalar=1.0, in1=g[:], op0=ALU.add, op1=ALU.mult),
                  reads=[r_mod, r_g], writes=[r_mod])
            GT.append(G)
            SHT.append(mv[(r, ch_shift)])
        return GT, SHT

    def ph_mod(self):
        c, P, nc = self.c, self.P, self.nc
        P.begin()
        with contextlib.ExitStack() as st:
            cTf = self.sb(st, "cTf", [128, c.KC * 2], F32)
            scT = self.sb(st, "scT", [128, c.KC * 2], BF16)
            wb = Rot([self.sb(st, "wb", [128, c.KC, 512], BF16) for _ in range(2)])
            bt = Rot([self.sb(st, "bt", [2, 512], F32) for _ in range(2)])
            mo = Rot([self.sb(st, "mo", [2, 512], F32) for _ in range(2)])
            pm = Rot([self.ps(st, "pm", [128, 512], F32) for _ in range(2)])
            r_c, r_sc = Res(), Res()
            self.dma("sp", cTf[:], self.din["cT"][:, :], writes=[r_c])
            P.add("act", lambda e: e.activation(out=scT[:], in_=cTf[:], func=AF.Silu), reads=[r_c], writes=[r_sc])
            wa, ba = self.din["w_ada"], self.din["b_ada"]
            for cb in range(6 * c.D // 512):
                w, r_w = wb.next()
                b, r_b = bt.next()
                m, r_m = mo.next()
                p, r_p = pm.next()
                self.dma("pool", w[:], wa[:, cb * 512:(cb + 1) * 512].rearrange("(k p) n -> p k n", p=128), writes=[r_w])
                self.dma("sp", b[:], ba[cb * 512:(cb + 1) * 512].partition_broadcast(2), writes=[r_b])
                for kc in range(c.KC):
                    P.add("pe", lambda e, kc=kc, w=w, p=p: e.matmul(p[0:2, :], lhsT=scT[:, kc * 2:kc * 2 + 2], rhs=w[:, kc, :], start=(kc == 0), stop=(kc == c.KC - 1)),
                          reads=[r_sc, r_w], writes=[r_p])
                P.add("dve", lambda e, m=m, p=p, b=b: e.tensor_tensor(out=m[:], in0=p[0:2, :], in1=b[:], op=ALU.add), reads=[r_p, r_b], writes=[r_m])
                self.dma("sp", self.modD[:, cb * 512:(cb + 1) * 512], m[:], reads=[r_m])
            P.end()

    def proj_pass(self, st, tiles, units, w_ap, hT, r_hT, wrot, prot, after_tile=None):
        c, P = self.c, self.P
        for u in units:
            wbs = []
            off = 0
            for (c0, n) in u["blocks"]:
                w, r_w = wrot.next()
                self.dma("pool", w[:, :, :n], w_ap[:, c0:c0 + n].rearrange("(k p) n -> p k n", p=128), writes=[r_w])
                wbs.append((w, r_w, off, n))
                off += n
            for ti, t in enumerate(tiles):
                np_, col0 = t["np"], t["col0"]
                pp, r_pp = prot.next()
                for (w, r_w, o, n) in wbs:
                    for kc in range(c.KC):
                        P.add("pe", lambda e, kc=kc, w=w, pp=pp, o=o, n=n, np_=np_, col0=col0: e.matmul(
                            pp[:np_, o:o + n], lhsT=hT[:, kc, col0:col0 + np_], rhs=w[:, kc, :n], start=(kc == 0), stop=(kc == c.KC - 1)),
                            reads=[r_hT[ti], r_w], writes=[r_pp])
                u["evac"](ti, t, pp, r_pp)
                if after_tile is not None and u is units[-1]:
                    after_tile(ti)

    def kv_evacs(self, st, W, outs, scr, gkv, r_gkv):
        c, P = self.c, self.P
        H = c.H
        units = []

        def lat_T_store(src32, r_src, np_, k0):
            lst, r_lst = W["latst"].next()

            def ev(kk, n, pt, r_pt):
                P.add("act", lambda e: e.activation(out=lst[:, 0:4, :np_], in_=pt[:, 0:4, :np_], func=AF.Copy), reads=[r_pt], writes=[r_lst])
            self.transposes_evac(src32, np_, 4, W["ptf"], ev, r_src, dtype_is_bf16=False, grp=4)
            self.dma("sp", scr["latT"][:, :, k0:k0 + np_], lst[:, :, :np_], reads=[r_lst])

        def kr_T_store(src32, r_src, np_, k0):
            kst, r_kst = W["krst"].next()
            pt, r_pt = W["ptf"].next()
            P.add("pe", lambda e: e.transpose(out=pt[0:64, 0, :np_], in_=src32[:np_, 0:64], identity=self.identf[:np_, :np_]),
                  reads=[r_src, self.r_const], writes=[r_pt])
            P.add("act", lambda e: e.activation(out=kst[0:64, :np_], in_=pt[0:64, 0, :np_], func=AF.Copy), reads=[r_pt], writes=[r_kst])
            self.dma("sp", scr["krT"][:, k0:k0 + np_], kst[0:64, :np_], reads=[r_kst])

        def k_T_group(src32, r_src, np_, h0, k0):
            kst, r_kst = W["kst"].next()

            def ev(kk, n, pt, r_pt):
                P.add("dve", lambda e: e.tensor_copy(out=kst[:, 0:4, :np_], in_=pt[:, 0:4, :np_]), reads=[r_pt], writes=[r_kst])
            self.transposes_evac(src32, np_, 4, W["ptf"], ev, r_src, dtype_is_bf16=False, grp=4)
            self.dma("sp", scr["kT"][:, h0:h0 + 4, k0:k0 + np_], kst[:, :, :np_], reads=[r_kst])

        self.lat_T_store, self.kr_T_store, self.k_T_group = lat_T_store, kr_T_store, k_T_group

        def ev_lat(ti, t, pp, r_pp):
            np_, k0 = t["np"], t["tok0"]
            ss, r_ss = W["ss"].next(); sv, r_sv = W["sv"].next(); rl, r_rl = W["rs"].next()
            j16, r_j16 = W["ob16"].next()
            o32, r_o32 = W["of32"].next()
            P.add("act", lambda e: e.activation(out=j16[:np_], in_=pp[:np_, :], func=AF.Square, accum_out=ss[:np_]), reads=[r_pp], writes=[r_j16, r_ss])
            self.rsqrt_chain(ss, np_, 1.0 / 512, sv, rl, r_ss, r_sv, r_rl)
            P.add("dve", lambda e: e.scalar_tensor_tensor(out=o32[:np_], in0=pp[:np_, :], scalar=rl[:np_], in1=gkv[:np_], op0=ALU.mult, op1=ALU.mult),
                  reads=[r_pp, r_rl, r_gkv], writes=[r_o32])
            self.dma("sp", outs["lat"][k0:k0 + np_, :], o32[:np_], reads=[r_o32])
            P.add("act", lambda e: e.activation(out=j16[:np_], in_=o32[:np_], func=AF.Copy), reads=[r_o32], writes=[r_j16])
            self.dma("sp", scr["lat"][k0:k0 + np_, :], j16[:np_], reads=[r_j16])
            lat_T_store(o32, r_o32, np_, k0)

        units.append(dict(blocks=[(c.QL, 512)], evac=ev_lat))

        def ev_kr(ti, t, pp, r_pp):
            np_, k0 = t["np"], t["tok0"]
            rt, r_rt = W["rt"].next()
            t1, r_t1 = W["kr1"].next(); t2, r_t2 = W["kr2"].next()
            self.dma("sp", rt[:np_], t["rope"], writes=[r_rt])
            P.add("dve", lambda e: e.tensor_tensor(out=t1[:np_], in0=pp[:np_, 0:64], in1=rt[:np_, 0:64], op=ALU.mult), reads=[r_pp, r_rt], writes=[r_t1])
            P.add("dve", lambda e: e.tensor_tensor(out=t2[:np_, 0:32], in0=pp[:np_, 32:64], in1=rt[:np_, 64:96], op=ALU.mult), reads=[r_pp, r_rt], writes=[r_t2])
            P.add("dve", lambda e: e.tensor_tensor(out=t2[:np_, 32:64], in0=pp[:np_, 0:32], in1=rt[:np_, 96:128], op=ALU.mult), reads=[r_pp, r_rt], writes=[r_t2])
            P.add("dve", lambda e: e.tensor_tensor(out=t1[:np_], in0=t1[:np_], in1=t2[:np_], op=ALU.add), reads=[r_t1, r_t2], writes=[r_t1])
            self.dma("sp", outs["kr"][k0:k0 + np_, :], t1[:np_], reads=[r_t1])
            kr_T_store(t1, r_t1, np_, k0)

        units.append(dict(blocks=[(c.QL + 512, 64)], evac=ev_kr))

        kstate = {}

        def mk_sbk(u):
            def ev(ti, t, pp, r_pp):
                np_, k0 = t["np"], t["tok0"]
                o32, r_o32 = W["of32"].next()
                P.add("act", lambda e: e.activation(out=o32[:np_], in_=pp[:np_, :], func=AF.Copy), reads=[r_pp], writes=[r_o32])
                self.dma("sp", outs["sbk"][k0:k0 + np_, u * 512:(u + 1) * 512], o32[:np_], reads=[r_o32])
                k_T_group(o32, r_o32, np_, 4 * u, k0)
            return ev

        def mk_sbv(u):
            def ev(ti, t, pp, r_pp):
                np_, k0 = t["np"], t["tok0"]
                o32, r_o32 = W["of32"].next()
                b16, r_b16 = W["ob16"].next()
                P.add("act", lambda e: e.activation(out=o32[:np_], in_=pp[:np_, :], func=AF.Copy), reads=[r_pp], writes=[r_o32])
                self.dma("sp", outs["sbv"][k0:k0 + np_, u * 512:(u + 1) * 512], o32[:np_], reads=[r_o32])
                P.add("dve", lambda e: e.tensor_copy(out=b16[:np_], in_=pp[:np_, :]), reads=[r_pp], writes=[r_b16])
                self.dma("sp", scr["v"][k0:k0 + np_, u * 512:(u + 1) * 512], b16[:np_], reads=[r_b16])
            return ev

        for u in range(H // 4):
            base = c.MLA_IN + c.HW + u * 512
            units.append(dict(blocks=[(base, 512)], evac=mk_sbk(u)))
        for u in range(H // 4):
            base = c.MLA_IN + 2 * c.HW + u * 512
            units.append(dict(blocks=[(base, 512)], evac=mk_sbv(u)))
        return units

    def ph_a1(self):
        c, P, nc = self.c, self.P, self.nc
        P.begin()
        with contextlib.ExitStack() as st:
            ntb = c.TB // 128
            W = {}
            W["xt"] = Rot([self.sb(st, "xt", [128, c.D], F32) for _ in range(2)])
            W["xn"] = Rot([self.sb(st, "xn", [128, c.D], BF16) for _ in range(2)])
            for k in ("ss", "sv", "rs"):
                W[k] = Rot([self.sb(st, k, [128, 1], F32) for _ in range(4)])
            W["ptr"] = Rot([self.ps(st, "ptr", [128, 8, 128], BF16) for _ in range(2)])
            W["ptf"] = Rot([self.ps(st, "ptf", [128, 4, 128], F32) for _ in range(2)])
            W["of32"] = Rot([self.sb(st, "of32", [128, 512], F32) for _ in range(3)])
            W["ob16"] = Rot([self.sb(st, "ob16", [128, 512], BF16) for _ in range(3)])
            W["latst"] = Rot([self.sb(st, "latst", [128, 4, 128], BF16) for _ in range(2)])
            W["krst"] = Rot([self.sb(st, "krst", [64, 128], BF16) for _ in range(2)])
            W["kst"] = Rot([self.sb(st, "kst", [128, 4, 128], BF16) for _ in range(3)])
            W["rt"] = Rot([self.sb(st, "rt", [128, 128], F32) for _ in range(2)])
            W["kr1"] = Rot([self.sb(st, "kr1", [128, 64], F32) for _ in range(2)])
            W["kr2"] = Rot([self.sb(st, "kr2", [128, 64], F32) for _ in range(2)])
            hT = self.sb(st, "hT", [128, c.KC, c.TB], BF16)
            wrot = Rot([self.sb(st, "wA", [128, c.KC, 512], BF16) for _ in range(2)])
            prot = Rot([self.ps(st, "pA", [128, 512], F32) for _ in range(3)])
            gkv = self.sb(st, "gkv", [128, 512], F32)
            r_gkv = Res()
            self.dma("sp", gkv[:], self.din["g_kv"].partition_broadcast(128), writes=[r_gkv])
            mv, r_mod = self.load_modvecs(st, [0, 1], Rot([self.ps(st, "pmv", [128, 128], F32)]))
            GT, SHT = self.make_GS(st, mv, r_mod, "g_mixT", 1, 0)
            P.mark("after_GS")
            sS = self.scr["S"]
            self.kv_evacs(st, W, None, sS, None, None)
            self.dma("pool", sS["lat"][0:c.PAST, :], self.din["c_lat"][:, :])
            self.dma("pool", sS["v"][0:c.PAST, :], self.din["c_v"][:, :])
            for ct in range(c.PAST // 128):
                k0 = ct * 128
                o32, r_o32 = W["of32"].next()
                self.dma("sp", o32[:], self.din["c_lat"][k0:k0 + 128, :], writes=[r_o32])
                self.lat_T_store(o32, r_o32, 128, k0)
                t1, r_t1 = W["kr1"].next()
                self.dma("sp", t1[:], self.din["c_kr"][k0:k0 + 128, :], writes=[r_t1])
                self.kr_T_store(t1, r_t1, 128, k0)
                for u in range(c.H // 4):
                    ck, r_ck = W["of32"].next()
                    self.dma("sp", ck[:], self.din["c_k"][k0:k0 + 128, u * 512:(u + 1) * 512], writes=[r_ck])
                    self.k_T_group(ck, r_ck, 128, 4 * u, k0)
            streams = []
            for tb in range(c.SEQ // c.TB):
                tiles = []
                for k in range(ntb):
                    tok0 = tb * c.TB + k * 128
                    tiles.append(dict(np=128, col0=k * 128, tok0=tok0, x=self.din["xp"][tok0:tok0 + 128, :], rope=self.din["ropeKp"][tok0:tok0 + 128, :]))
                streams.append((0, "P", tiles, dict(lat=self.dout["p_lat"], kr=self.dout["p_kr"], sbk=self.dout["p_sbk"], sbv=self.dout["p_sbv"])))
            streams.append((1, "S", [dict(np=c.DS, col0=0, tok0=c.PAST, x=self.din["xs"][:, :], rope=self.din["ropeKs"][:, :], otok0=0)],
                            dict(lat=self.dout["s_lat"], kr=self.dout["s_kr"], sbk=self.dout["s_sbk"], sbv=self.dout["s_sbv"])))
            RH = [Res() for _ in range(ntb)]
            r0, _, tiles0, _ = streams[0]
            for ti, t in enumerate(tiles0):
                self.prep_hT(t["x"], t["np"], GT[r0], SHT[r0], r_mod, hT, t["col0"], RH[ti], W)
            for si, (r, sname, tiles, outs) in enumerate(streams):
                nxt = streams[si + 1] if si + 1 < len(streams) else None

                pend = {}

                def after_tile(ti, nxt=nxt, pend=pend):
                    if nxt is None:
                        return
                    r2, _, tiles2, _ = nxt
                    if ti == 0 and len(tiles2) > 0:
                        pend[0] = self.prep_s1(tiles2[0]["x"], tiles2[0]["np"], W)
                    if ti < len(tiles2):
                        t2 = tiles2[ti]
                        if ti + 1 < len(tiles2):
                            pend[ti + 1] = self.prep_s1(tiles2[ti + 1]["x"], tiles2[ti + 1]["np"], W)
                        self.prep_s2(pend.pop(ti), GT[r2], SHT[r2], r_mod, hT, t2["col0"], RH[ti], W)
                if sname == "S":
                    outs = {k: _Shift(v, -c.PAST) for k, v in outs.items()}
                units = self.kv_evacs(st, W, outs, self.scr[sname], gkv, r_gkv)
                self.proj_pass(st, tiles, units, self.din["w_in"], hT, RH[:len(tiles)], wrot, prot, after_tile=after_tile)
            P.end()

    def ph_a2(self):
        c, P, nc = self.c, self.P, self.nc
        H, QL = c.H, c.QL
        RC = QL // 128
        with contextlib.ExitStack() as st0:
            qlatT = self.sb(st0, "qlatT", [128, RC, c.NTA], BF16)
            r_qlatT = Res()
            P.begin()
            with contextlib.ExitStack() as st:
                GMAX = c.GMAX
                W = {}
                W["xt"] = Rot([self.sb(st, "xt", [128, c.D], F32) for _ in range(1)])
                W["xn"] = Rot([self.sb(st, "xn", [128, c.D], BF16) for _ in range(2)])
                for k in ("ss", "sv", "rs"):
                    W[k] = Rot([self.sb(st, k, [128, 1], F32) for _ in range(4)])
                W["ptr"] = Rot([self.ps(st, "ptr", [128, 8, 128], BF16) for _ in range(2)])
                hT = self.sb(st, "hT2", [128, c.KC, GMAX * 128], BF16)
                wrot = Rot([self.sb(st, "wA", [128, c.KC, 512], BF16) for _ in range(2)])
                prot = Rot([self.ps(st, "pA", [128, 512], F32) for _ in range(2)])
                mv, r_mod = self.load_modvecs(st, [0, 1], Rot([self.ps(st, "pmv", [128, 128], F32)]))
                GT, SHT = self.make_GS(st, mv, r_mod, "g_mixT", 1, 0)
                gq = self.sb(st, "gq", [128, QL], F32)
                r_gq = Res()
                self.dma("sp", gq[:], self.din["g_q"].partition_broadcast(128), writes=[r_gq])
                alltiles = []
                for s in range(c.NS):
                    alltiles.append(dict(np=128, gcol=s * 128, slot=s, r=0, x=self.din["xq"][s * 128:(s + 1) * 128, :]))
                alltiles.append(dict(np=c.DS, gcol=c.NTQ, slot=c.NS, r=1, x=self.din["xs"][:, :]))
                groups = []
                k = 0
                while k < len(alltiles):
                    n = (len(alltiles) - k) if len(alltiles) - k <= GMAX else GMAX - 1
                    groups.append(alltiles[k:k + n])
                    k += n
                qlf = [self.sb(st, "qlf", [128, QL], F32) for _ in range(GMAX)]
                ssq = [self.sb(st, "ssq", [128, 4], F32) for _ in range(GMAX)]
                junk = Rot([self.sb(st, "junk", [128, 512], BF16) for _ in range(2)])
                qn = Rot([self.sb(st, "qn", [128, QL], BF16) for _ in range(2)])
                b16 = Rot([self.sb(st, "b16", [128, 512], BF16) for _ in range(2)])
                sbqst = Rot([self.sb(st, "sbqst", [128, 4, 128], BF16) for _ in range(3)])
                nqu = QL // 512
                RH = [Res() for _ in range(GMAX)]
                r_qlf = [Res() for _ in range(GMAX)]
                r_ssq = [Res() for _ in range(GMAX)]
                gtiles = [[dict(t, col0=i * 128) for i, t in enumerate(g)] for g in groups]
                for ti, t in enumerate(gtiles[0]):
                    self.prep_hT(t["x"], t["np"], GT[t["r"]], SHT[t["r"]], r_mod, hT, t["col0"], RH[ti], W)
                for gi, tiles in enumerate(gtiles):
                    nxt = gtiles[gi + 1] if gi + 1 < len(gtiles) else None
                    r_hT = RH[:len(tiles)]

                    pend = {}

                    def after_tile(ti, nxt=nxt, pend=pend, ntl=len(tiles)):
                        if nxt is None:
                            return
                        if ti == 0:
                            pend[0] = self.prep_s1(nxt[0]["x"], nxt[0]["np"], W)
                        if ti < len(nxt):
                            t2 = nxt[ti]
                            if ti + 1 < len(nxt):
                                pend[ti + 1] = self.prep_s1(nxt[ti + 1]["x"], nxt[ti + 1]["np"], W)
                            self.prep_s2(pend.pop(ti), GT[t2["r"]], SHT[t2["r"]], r_mod, hT, t2["col0"], RH[ti], W)
                        if ti == ntl - 1:
                            for tj in range(ntl, len(nxt)):
                                t2 = nxt[tj]
                                if tj + 1 < len(nxt):
                                    pend[tj + 1] = self.prep_s1(nxt[tj + 1]["x"], nxt[tj + 1]["np"], W)
                                self.prep_s2(pend.pop(tj), GT[t2["r"]], SHT[t2["r"]], r_mod, hT, t2["col0"], RH[tj], W)
                    units = []

                    def mk_q(u, r_qlf=r_qlf, r_ssq=r_ssq):
                        def ev(ti, t, pp, r_pp):
                            np_ = t["np"]
                            j, r_j = junk.next()
                            P.add("act", lambda e: e.activation(out=qlf[ti][:np_, u * 512:(u + 1) * 512], in_=pp[:np_, :], func=AF.Copy), reads=[r_pp], writes=[r_qlf[ti]])
                            P.add("act", lambda e: e.activation(out=j[:np_], in_=pp[:np_, :], func=AF.Square, accum_out=ssq[ti][:np_, u:u + 1]), reads=[r_pp], writes=[r_j, r_ssq[ti]])
                            if u == nqu - 1:
                                ss, r_ss = W["ss"].next(); sv, r_sv = W["sv"].next(); rq, r_rq = W["rs"].next()
                                P.add("dve", lambda e: e.tensor_reduce(out=ss[:np_], in_=ssq[ti][:np_, 0:nqu], axis=AX.X, op=ALU.add), reads=[r_ssq[ti]], writes=[r_ss])
                                self.rsqrt_chain(ss, np_, 1.0 / QL, sv, rq, r_ss, r_sv, r_rq)
                                q, r_q = qn.next()
                                P.add("dve", lambda e: e.scalar_tensor_tensor(out=q[:np_], in0=qlf[ti][:np_], scalar=rq[:np_], in1=gq[:np_], op0=ALU.mult, op1=ALU.mult),
                                      reads=[r_qlf[ti], r_rq, r_gq], writes=[r_q])
                                gcol = t["gcol"]

                                def evq(k0, n, pt, r_pt):
                                    P.add("act", lambda e: e.activation(out=qlatT[:, k0:k0 + n, gcol:gcol + np_], in_=pt[:, 0:n, :np_], func=AF.Copy), reads=[r_pt], writes=[r_qlatT])
                                self.transposes_evac(q, np_, RC, W["ptr"], evq, r_q)
                        return ev

                    def mk_sq(u):
                        def ev(ti, t, pp, r_pp):
                            np_ = t["np"]
                            b, r_b = b16.next()
                            P.add("dve", lambda e: e.tensor_copy(out=b[:np_], in_=pp[:np_, :]), reads=[r_pp], writes=[r_b])
                            sq, r_sq = sbqst.next()

                            def evs(k0, n, pt, r_pt):
                                P.add("act", lambda e: e.activation(out=sq[:, 0:4, :np_], in_=pt[:, 0:4, :np_], func=AF.Copy), reads=[r_pt], writes=[r_sq])
                            self.transposes_evac(b, np_, 4, W["ptr"], evs, r_b, grp=4)
                            self.dma("sp", self.sbq[t["slot"], :, 4 * u:4 * u + 4, 0:np_], sq[:, :, :np_], reads=[r_sq])
                        return ev

                    for u in range(nqu):
                        units.append(dict(blocks=[(u * 512, 512)], evac=mk_q(u)))
                    for u in range(H // 4):
                        base = c.MLA_IN + u * 512
                        units.append(dict(blocks=[(base, 512)], evac=mk_sq(u)))
                    self.proj_pass(st, tiles, units, self.din["w_in"], hT, r_hT, wrot, prot, after_tile=after_tile)
                P.end()
            P.begin()
            r_qlatT = Res()
            with contextlib.ExitStack() as st:
                wuq = self.sb(st, "wuq", [128, RC, H * 192], BF16)
                wuqs = self.sb(st, "wuqs", [128, RC, H * 64], BF16)
                wuk = self.sb(st, "wuk", [128, H, 512], BF16)
                cs = self.sb(st, "cs", [64, 2, c.NTA], F32)
                r_w = Res()
                self.dma("pool", wuq[:], self.din["w_uq"].rearrange("(k p) n -> p k n", p=128), writes=[r_w])
                self.dma("pool", wuqs[:], self.din["w_uqs"].rearrange("(k p) n -> p k n", p=128), writes=[r_w])
                self.dma("pool", wuk[:], self.din["w_ukT"].rearrange("h n c -> n h c"), writes=[r_w])
                self.dma("sp", cs[:], self.din["ropeQ"][:, :, :], writes=[r_w])
                tgs = []
                s0 = 0
                while s0 < c.NS:
                    ns = min(4, c.NS - s0)
                    tgs.append((s0 * 128, ns * 128, s0, ns))
                    s0 += ns
                tgs.append((c.NTQ, c.DS, c.NS, 0))
                pq = Rot([self.ps(st, "pq", [128, 512], F32) for _ in range(6)])
                qnope = Rot([self.sb(st, "qnope", [128, 512], BF16) for _ in range(2)])
                qt1 = Rot([self.sb(st, "qt1", [64, 512], F32) for _ in range(2)])
                qt2 = Rot([self.sb(st, "qt2", [64, 512], F32) for _ in range(2)])
                qrst = Rot([self.sb(st, "qrst", [64, 512], BF16) for _ in range(2)])
                qast = Rot([self.sb(st, "qast", [128, 4, 512], BF16) for _ in range(2)])
                for h in range(H):
                    for (col0, n, s0, ns) in tgs:
                        pn, r_pn = pq.next()
                        for rc in range(RC):
                            P.add("pe", lambda e, rc=rc, pn=pn, h=h, col0=col0, n=n: e.matmul(pn[:, :n], lhsT=wuq[:, rc, h * 192:h * 192 + 128], rhs=qlatT[:, rc, col0:col0 + n], start=(rc == 0), stop=(rc == RC - 1)),
                                  reads=[r_w, r_qlatT], writes=[r_pn])
                        qp, r_qp = qnope.next()
                        P.add("act", lambda e, qp=qp, pn=pn, n=n: e.activation(out=qp[:, :n], in_=pn[:, :n], func=AF.Copy), reads=[r_pn], writes=[r_qp])
                        pr, r_pr = pq.next()
                        for rc in range(RC):
                            P.add("pe", lambda e, rc=rc, pr=pr, h=h, col0=col0, n=n: e.matmul(pr[0:64, :n], lhsT=wuq[:, rc, h * 192 + 128:h * 192 + 192], rhs=qlatT[:, rc, col0:col0 + n], start=(rc == 0), stop=(rc == RC - 1)),
                                  reads=[r_w, r_qlatT], writes=[r_pr])
                        pz, r_pz = pq.next()
                        for rc in range(RC):
                            P.add("pe", lambda e, rc=rc, pz=pz, h=h, col0=col0, n=n: e.matmul(pz[0:64, :n], lhsT=wuqs[:, rc, h * 64:(h + 1) * 64], rhs=qlatT[:, rc, col0:col0 + n], start=(rc == 0), stop=(rc == RC - 1)),
                                  reads=[r_w, r_qlatT], writes=[r_pz])
                        a1, r_a1 = qt1.next(); a2, r_a2 = qt2.next(); qr, r_qr = qrst.next()
                        P.add("dve", lambda e, a1=a1, pr=pr, col0=col0, n=n: e.tensor_tensor(out=a1[:, :n], in0=pr[0:64, :n], in1=cs[:, 0, col0:col0 + n], op=ALU.mult), reads=[r_pr, r_w], writes=[r_a1])
                        P.add("dve", lambda e, a2=a2, pz=pz, col0=col0, n=n: e.tensor_tensor(out=a2[:, :n], in0=pz[0:64, :n], in1=cs[:, 1, col0:col0 + n], op=ALU.mult), reads=[r_pz, r_w], writes=[r_a2])
                        P.add("dve", lambda e, a1=a1, a2=a2, qr=qr, n=n: e.tensor_tensor(out=qr[:, :n], in0=a1[:, :n], in1=a2[:, :n], op=ALU.add), reads=[r_a1, r_a2], writes=[r_qr])
                        qa, r_qa = qast.next()
                        for cc in range(4):
                            pa, r_pa = pq.next()
                            P.add("pe", lambda e, cc=cc, pa=pa, qp=qp, h=h, n=n: e.matmul(pa[:, :n], lhsT=wuk[:, h, cc * 128:(cc + 1) * 128], rhs=qp[:, :n], start=True, stop=True),
                                  reads=[r_w, r_qp], writes=[r_pa])
                            if cc % 2 == 0:
                                P.add("act", lambda e, cc=cc, pa=pa, qa=qa, n=n: e.activation(out=qa[:, cc, :n], in_=pa[:, :n], func=AF.Copy), reads=[r_pa], writes=[r_qa])
                            else:
                                P.add("dve", lambda e, cc=cc, pa=pa, qa=qa, n=n: e.tensor_copy(out=qa[:, cc, :n], in_=pa[:, :n]), reads=[r_pa], writes=[r_qa])
                        if ns > 0:
                            self.dma("sp", self.qrope[s0:s0 + ns, :, h, :].rearrange("s p q -> p s q"), qr[:, :n].rearrange("p (s q) -> p s q", q=128), reads=[r_qr])
                            for cc in range(4):
                                self.dma("sp", self.qabs[s0:s0 + ns, :, h, cc, :].rearrange("s p q -> p s q"), qa[:, cc, :n].rearrange("p (s q) -> p s q", q=128), reads=[r_qa])
                        else:
                            self.dma("sp", self.qrope[c.NS, :, h, 0:n], qr[:, :n], reads=[r_qr])
                            self.dma("sp", self.qabs[c.NS, :, h, :, 0:n], qa[:, :, :n], reads=[r_qa])
                P.end()

    def slots(self):
        c = self.c
        out = []
        for i in range(c.NS):
            blocks = [dict(k0=kb * 512, nk=512, mask=(kb == i)) for kb in range(i + 1)]
            out.append(dict(np=128, slot=i, stream="P", nkeys=(i + 1) * 512, blocks=blocks, col0=i * 128))
        blocks = [dict(k0=kb * 512, nk=min(512, c.PAST - kb * 512), mask=False) for kb in range((c.PAST + 511) // 512)]
        blocks.append(dict(k0=c.PAST, nk=c.DS, mask=True))
        out.append(dict(np=c.DS, slot=c.NS, stream="S", nkeys=c.KS, blocks=blocks, col0=c.NTQ))
        return out

    def merged_store(self, st, W, o32, r_o32, np_, kc0, col0, gT, r_g):
        c, P = self.c, self.P
        nch = c.HW // 128
        ss, r_ss = W["ss"].next(); sv, r_sv = W["sv"].next(); rm, r_rm = W["rs"].next()
        on, r_on = W["on16"].next()
        P.add("act", lambda e: e.activation(out=on[:np_], in_=o32[:np_], func=AF.Square, accum_out=ss[:np_]), reads=[r_o32], writes=[r_on, r_ss])
        self.rsqrt_chain(ss, np_, 1.0 / c.HW, sv, rm, r_ss, r_sv, r_rm)
        P.add("dve", lambda e: e.tensor_scalar(out=on[:np_], in0=o32[:np_], scalar1=rm[:np_], scalar2=None, op0=ALU.mult), reads=[r_o32, r_rm], writes=[r_on])
        mst, r_mst = W["mst"].next()

        def ev(k0, n, pt, r_pt):
            for j in range(n):
                k = k0 + j
                P.add("act", lambda e, k=k, j=j: e.activation(out=mst[:, k, :np_], in_=pt[:, j, :np_], func=AF.Identity, scale=gT[:, kc0 + k:kc0 + k + 1]),
                      reads=[r_pt, r_g], writes=[r_mst])
        self.transposes_evac(on, np_, nch, W["ptr"], ev, r_on)
        self.dma("sp", self.mT[:, kc0:kc0 + nch, col0:col0 + np_], mst[:, :, :np_], reads=[r_mst])

    @staticmethod
    def pipeline(units, stages):
        n = len(units)
        maxd = max(d for d, _ in stages)
        for step in range(n + maxd):
            for d, fn in stages:
                u = step - d
                if 0 <= u < n:
                    fn(units[u])

    def ph_mla(self):
        c, P, nc = self.c, self.P, self.nc
        H = c.H
        P.begin()
        with contextlib.ExitStack() as st:
            KT = (c.KMAX + 127) // 128
            lat = self.sb(st, "lat", [128, KT, 512], BF16)
            latT = self.sb(st, "latT", [128, 4, c.KMAX], BF16)
            krT = self.sb(st, "krT", [64, c.KMAX], BF16)
            r_K = Res()
            wuv = self.sb(st, "wuv", [128, 4, c.HW], BF16)
            r_wuv = Res()
            self.dma("pool", wuv[:], self.din["w_uv"].rearrange("(k p) n -> p k n", p=128), writes=[r_wuv])
            gT = self.sb(st, "goT", [128, c.KC], F32)
            r_g = Res()
            self.dma("sp", gT[:], self.din["g_outT"][:, :], writes=[r_g])
            mb = self.sb(st, "mb", [128, 512], F32)
            r_mb = Res()
            self.dma("sp", mb[:], self.din["m_mla"][:, :], writes=[r_mb])
            qa = Rot([self.sb(st, "qa", [128, H, 4, 128], BF16) for _ in range(2)])
            qr = Rot([self.sb(st, "qr", [64, H, 128], BF16) for _ in range(2)])
            W = {}
            for k in ("ss", "sv", "rs"):
                W[k] = Rot([self.sb(st, k, [128, 1], F32) for _ in range(4)])
            W["ptr"] = Rot([self.ps(st, "ptr", [128, 8, 128], BF16) for _ in range(2)])
            W["on16"] = Rot([self.sb(st, "on16", [128, c.HW], BF16)])
            W["mst"] = Rot([self.sb(st, "mst", [128, c.HW // 128, 128], BF16) for _ in range(2)])
            pS = Rot([self.ps(st, "pS", [128, 512], F32) for _ in range(3)])
            pO = Rot([self.ps(st, "pO", [128, 512], F32) for _ in range(2)])
            pV = Rot([self.ps(st, "pV", [128, 512], F32)])
            sm = Rot([self.sb(st, "sm", [128, 512], F32) for _ in range(2)])
            pexp = Rot([self.sb(st, "pexp", [128, 512], BF16) for _ in range(4)])
            pT = Rot([self.sb(st, "pT", [128, 4, 128], BF16) for _ in range(3)])
            rsum = Rot([self.sb(st, "rsum", [128, 16], F32) for _ in range(7)])
            for bb_, rr_ in zip(rsum.bufs, rsum.res):
                P.add("dve", lambda e, bb_=bb_: e.memset(bb_[:], 0.0), writes=[rr_])
            den = Rot([self.sb(st, "den", [128, 1], F32) for _ in range(2)])
            rden = Rot([self.sb(st, "rden", [128, 1], F32) for _ in range(2)])
            oln = Rot([self.sb(st, "oln", [128, 512], BF16) for _ in range(2)])
            olT = Rot([self.sb(st, "olT", [128, 4, 128], BF16) for _ in range(2)])
            o32 = Rot([self.sb(st, "o32", [128, c.HW], F32) for _ in range(2)])
            cur_stream = None
            for sl in self.slots():
                np_ = sl["np"]
                if sl["stream"] != cur_stream:
                    cur_stream = sl["stream"]
                    sc = self.scr[cur_stream]
                    nk = sc["nk"]
                    nfull = nk // 128
                    self.dma("sp", lat[:, 0:nfull, :], sc["lat"][0:nfull * 128, :].rearrange("(t p) c -> p t c", p=128), writes=[r_K])
                    if nk % 128:
                        self.dma("sp", lat[0:nk % 128, nfull, :], sc["lat"][nfull * 128:nk, :], writes=[r_K])
                    self.dma("sp", latT[:, :, 0:nk], sc["latT"][:, :, :], writes=[r_K])
                    self.dma("sp", krT[:, 0:nk], sc["krT"][:, :], writes=[r_K])
                qa_t, r_qa = qa.next()
                qr_t, r_qr = qr.next()
                self.dma("sp", qa_t[:, :, :, 0:np_], self.qabs[sl["slot"], :, :, :, 0:np_], writes=[r_qa])
                self.dma("sp", qr_t[:, :, 0:np_], self.qrope[sl["slot"], :, :, 0:np_], writes=[r_qr])
                o_t, r_o = o32.next()
                nb = len(sl["blocks"])
                is_p = (cur_stream == "P")
                units = []
                for h in range(H):
                    hd = dict(h=h)
                    for bi, blk in enumerate(sl["blocks"]):
                        units.append(dict(h=h, hd=hd, bi=bi, k0=blk["k0"], nk=blk["nk"], mask=(blk["mask"] and is_p), first=(bi == 0), last=(bi == nb - 1)))

                def stA(u):
                    h, k0, nk, bi = u["h"], u["k0"], u["nk"], u["bi"]
                    if u["first"]:
                        u["hd"]["O"] = pO.next()
                        u["hd"]["rs"] = rsum.next()
                    rs_t, r_rs = u["hd"]["rs"]
                    S, r_S = pS.next()
                    for cc in range(4):
                        P.add("pe", lambda e, cc=cc: e.matmul(S[:np_, :nk], lhsT=qa_t[:, h, cc, :np_], rhs=latT[:, cc, k0:k0 + nk], start=(cc == 0), stop=False),
                              reads=[r_qa, r_K], writes=[r_S])
                    P.add("pe", lambda e: e.matmul(S[:np_, :nk], lhsT=qr_t[0:64, h, :np_], rhs=krT[0:64, k0:k0 + nk], start=False, stop=True),
                          reads=[r_qr, r_K], writes=[r_S])
                    pe_t, r_pe = pexp.next()
                    u["pe"] = (pe_t, r_pe)
                    if u["mask"]:
                        sm_t, r_sm = sm.next()
                        P.add("dve", lambda e: e.scalar_tensor_tensor(out=sm_t[:np_, :nk], in0=S[:np_, :nk], scalar=c.MLA_SCALE, in1=mb[:np_, :nk], op0=ALU.mult, op1=ALU.add),
                              reads=[r_S, r_mb], writes=[r_sm])
                        P.add("act", lambda e: e.activation(out=pe_t[:np_, :nk], in_=sm_t[:np_, :nk], func=AF.Exp, accum_out=rs_t[:np_, bi:bi + 1]),
                              reads=[r_sm], writes=[r_pe, r_rs])
                    else:
                        P.add("act", lambda e: e.activation(out=pe_t[:np_, :nk], in_=S[:np_, :nk], func=AF.Exp, scale=c.MLA_SCALE, accum_out=rs_t[:np_, bi:bi + 1]),
                              reads=[r_S], writes=[r_pe, r_rs])

                def stB(u):
                    nk = u["nk"]
                    pe_t, r_pe = u["pe"]
                    nsub = (nk + 127) // 128
                    pt, r_pt = W["ptr"].next()
                    for j in range(nsub):
                        nkt = min(128, nk - j * 128)
                        P.add("pe", lambda e, j=j, nkt=nkt: e.transpose(out=pt[:nkt, j, :np_], in_=pe_t[:np_, j * 128:j * 128 + nkt], identity=self.identb[:np_, :np_]),
                              reads=[r_pe, self.r_const], writes=[r_pt])
                    pT_t, r_pT = pT.next()
                    u["pT"] = (pT_t, r_pT)
                    if nk % 128 == 0:
                        P.add("dve", lambda e: e.tensor_copy(out=pT_t[:, 0:nsub, :np_], in_=pt[:, 0:nsub, :np_]), reads=[r_pt], writes=[r_pT])
                    else:
                        for j in range(nsub):
                            nkt = min(128, nk - j * 128)
                            P.add("dve", lambda e, j=j, nkt=nkt: e.tensor_copy(out=pT_t[:nkt, j, :np_], in_=pt[:nkt, j, :np_]), reads=[r_pt], writes=[r_pT])

                def stC(u):
                    nk, k0 = u["nk"], u["k0"]
                    pT_t, r_pT = u["pT"]
                    O, r_O = u["hd"]["O"]
                    nsub = (nk + 127) // 128
                    for j in range(nsub):
                        nkt = min(128, nk - j * 128)
                        P.add("pe", lambda e, j=j, nkt=nkt, st_=(u["first"] and j == 0), sp_=(u["last"] and j == nsub - 1): e.matmul(
                            O[:np_, :], lhsT=pT_t[:nkt, j, :np_], rhs=lat[:nkt, k0 // 128 + j, :], start=st_, stop=sp_),
                            reads=[r_pT, r_K], writes=[r_O])

                def stD(u):
                    if not u["last"]:
                        return
                    O, r_O = u["hd"]["O"]
                    rs_t, r_rs = u["hd"]["rs"]
                    dn, r_dn = den.next()
                    rd, r_rd = rden.next()
                    P.add("dve", lambda e: e.tensor_reduce(out=dn[:np_], in_=rs_t[:np_, 0:nb], axis=AX.X, op=ALU.add), reads=[r_rs], writes=[r_dn])
                    P.add("dve", lambda e: e.reciprocal(out=rd[:np_], in_=dn[:np_]), reads=[r_dn], writes=[r_rd])
                    ol, r_ol = oln.next()
                    P.add("act", lambda e: e.activation(out=ol[:np_], in_=O[:np_, :], func=AF.Identity, scale=rd[:np_]), reads=[r_O, r_rd], writes=[r_ol])
                    u["hd"]["ol"] = (ol, r_ol)

                def stE(u):
                    if not u["last"]:
                        return
                    ol, r_ol = u["hd"]["ol"]
                    oT, r_oT = olT.next()
                    u["hd"]["oT"] = (oT, r_oT)

                    def ev(k0_, n, pt, r_pt):
                        P.add("dve", lambda e: e.tensor_copy(out=oT[:, 0:4, :np_], in_=pt[:, 0:4, :np_]), reads=[r_pt], writes=[r_oT])
                    self.transposes_evac(ol, np_, 4, W["ptr"], ev, r_ol, grp=4)

                def stF(u):
                    if not u["last"]:
                        return
                    h = u["h"]
                    oT, r_oT = u["hd"]["oT"]
                    V, r_V = pV.next()
                    for cc in range(4):
                        P.add("pe", lambda e, cc=cc: e.matmul(V[:np_, 0:128], lhsT=oT[:, cc, :np_], rhs=wuv[:, cc, h * 128:(h + 1) * 128], start=(cc == 0), stop=(cc == 3)),
                              reads=[r_oT, r_wuv], writes=[r_V])
                    P.add("act", lambda e: e.activation(out=o_t[:np_, h * 128:(h + 1) * 128], in_=V[:np_, 0:128], func=AF.Copy), reads=[r_V], writes=[r_o])

                self.pipeline(units, [(0, stA), (1, stB), (2, stC), (3, stD), (4, stE), (5, stF)])
                self.merged_store(st, W, o_t, r_o, np_, 0, sl["col0"], gT, r_g)
            P.end()

    def ph_sb(self):
        c, P, nc = self.c, self.P, self.nc
        H = c.H
        P.begin()
        with contextlib.ExitStack() as st:
            KT = (c.KMAX + 127) // 128
            gT = self.sb(st, "goT", [128, c.KC], F32)
            r_g = Res()
            self.dma("sp", gT[:], self.din["g_outT"][:, :], writes=[r_g])
            mk = self.sb(st, "mk", [128, 512], F32); nmk = self.sb(st, "nmk", [128, 512], F32)
            mks = self.sb(st, "mks", [c.DS, c.DS], F32); nmks = self.sb(st, "nmks", [c.DS, c.DS], F32)
            zeros = self.sb(st, "zeros", [128, 512], F32)
            r_mk = Res()
            self.dma("sp", mk[:], self.din["m_sb"][:, :], writes=[r_mk])
            self.dma("sp", nmk[:], self.din["m_nsb"][:, :], writes=[r_mk])
            self.dma("sp", mks[:], self.din["ms_sb"][:, :], writes=[r_mk])
            self.dma("sp", nmks[:], self.din["ms_nsb"][:, :], writes=[r_mk])
            P.add("dve", lambda e: e.memset(zeros[:], 0.0), writes=[r_mk])
            qs = Rot([self.sb(st, "qs", [128, H, 128], BF16) for _ in range(2)])
            kTh = Rot([self.sb(st, "kTh", [128, c.KMAX], BF16) for _ in range(5)])
            vh = Rot([self.sb(st, "vh", [128, KT, 128], BF16) for _ in range(5)])
            W = {}
            for k in ("ss", "sv", "rs"):
                W[k] = Rot([self.sb(st, k, [128, 1], F32) for _ in range(4)])
            W["ptr"] = Rot([self.ps(st, "ptr", [128, 8, 128], BF16) for _ in range(2)])
            W["on16"] = Rot([self.sb(st, "on16", [128, c.HW], BF16)])
            W["mst"] = Rot([self.sb(st, "mst", [128, c.HW // 128, 128], BF16) for _ in range(2)])
            ND = 6
            pZ = Rot([self.ps(st, "pZ", [128, 512], F32) for _ in range(3)])
            pOS = Rot([self.ps(st, "pOS", [128, 512], F32) for _ in range(2)])
            beta = Rot([self.sb(st, "beta", [128, 512], F32) for _ in range(ND)])
            nu = Rot([self.sb(st, "nu", [128, 512], F32) for _ in range(ND)])
            Pb = Rot([self.sb(st, "Pb", [128, 514], F32) for _ in range(ND)])
            a16 = Rot([self.sb(st, "a16", [128, 512], BF16) for _ in range(ND)])
            aT = Rot([self.sb(st, "aT", [128, 4, 128], BF16) for _ in range(3)])
            o32 = Rot([self.sb(st, "o32", [128, c.HW], F32) for _ in range(2)])
            for sl in self.slots():
                np_ = sl["np"]
                sc = self.scr[sl["stream"]]
                nkeys = sl["nkeys"]
                qs_t, r_qs = qs.next()
                self.dma("sp", qs_t[:, :, 0:np_], self.sbq[sl["slot"], :, :, 0:np_], writes=[r_qs])
                o_t, r_o = o32.next()
                blocks = list(reversed(sl["blocks"]))
                nb = len(blocks)
                m_ap, nm_ap = (mk, nmk) if sl["stream"] == "P" else (mks, nmks)
                units = []
                for h in range(H):
                    hd = dict(h=h, prevPb=None)
                    for bi, blk in enumerate(blocks):
                        units.append(dict(h=h, hd=hd, bi=bi, k0=blk["k0"], nk=blk["nk"], mask=blk["mask"], first=(bi == 0), last=(bi == nb - 1)))

                def stA(u):
                    h, k0, nk = u["h"], u["k0"], u["nk"]
                    hd = u["hd"]
                    if u["first"]:
                        kT_t, r_kT = kTh.next()
                        v_t, r_v = vh.next()
                        hd["kT"] = (kT_t, r_kT); hd["v"] = (v_t, r_v)
                        self.dma("sp", kT_t[:, 0:nkeys], sc["kT"][:, h, 0:nkeys], writes=[r_kT])
                        nfull = nkeys // 128
                        self.dma("sp", v_t[:, 0:nfull, :], sc["v"][0:nfull * 128, h * 128:(h + 1) * 128].rearrange("(t p) d -> p t d", p=128), writes=[r_v])
                        if nkeys % 128:
                            self.dma("sp", v_t[0:nkeys % 128, nfull, :], sc["v"][nfull * 128:nkeys, h * 128:(h + 1) * 128], writes=[r_v])
                        hd["OS"] = pOS.next()
                    kT_t, r_kT = hd["kT"]
                    Z, r_Z = pZ.next()
                    P.add("pe", lambda e: e.matmul(Z[:np_, :nk], lhsT=qs_t[:, h, :np_], rhs=kT_t[:, k0:k0 + nk], start=True, stop=True),
                          reads=[r_qs, r_kT], writes=[r_Z])
                    b_t, r_b = beta.next()
                    n_t, r_n = nu.next()
                    P.add("act", lambda e: e.activation(out=b_t[:np_, :nk], in_=Z[:np_, :nk], func=AF.Sigmoid, scale=c.SB_SCALE), reads=[r_Z], writes=[r_b])
                    P.add("act", lambda e: e.activation(out=n_t[:np_, :nk], in_=Z[:np_, :nk], func=AF.Sigmoid, scale=-c.SB_SCALE), reads=[r_Z], writes=[r_n])
                    if u["mask"]:
                        P.add("dve", lambda e: e.tensor_tensor(out=n_t[:np_, :nk], in0=n_t[:np_, :nk], in1=nm_ap[:np_, :nk], op=ALU.max),
                              reads=[r_n, r_mk], writes=[r_n])
                    pb_t, r_pb = Pb.next()
                    if hd["prevPb"] is None:
                        P.add("dve", lambda e: e.memset(pb_t[:np_, nk:nk + 1], 1.0), writes=[r_pb])
                    else:
                        ppb, r_ppb = hd["prevPb"]
                        P.add("dve", lambda e: e.tensor_copy(out=pb_t[:np_, nk:nk + 1], in_=ppb[:np_, 0:1]), reads=[r_ppb], writes=[r_pb])
                    P.add("dve", lambda e: e.tensor_tensor_scan(
                        out=pb_t[:np_, 0:nk][:, ::-1], data0=n_t[:np_, 0:nk][:, ::-1], data1=zeros[:np_, 0:nk], initial=pb_t[:np_, nk:nk + 1], op0=ALU.mult, op1=ALU.add),
                        reads=[r_n, r_mk, r_pb], writes=[r_pb])
                    hd["prevPb"] = (pb_t, r_pb)
                    a_t, r_a = a16.next()
                    u["a"] = (a_t, r_a)
                    P.add("pool", lambda e: e.tensor_tensor(out=a_t[:np_, :nk], in0=b_t[:np_, :nk], in1=pb_t[:np_, 1:nk + 1], op=ALU.mult),
                          reads=[r_b, r_pb], writes=[r_a])
                    if u["mask"]:
                        P.add("pool", lambda e: e.tensor_tensor(out=a_t[:np_, :nk], in0=a_t[:np_, :nk], in1=m_ap[:np_, :nk], op=ALU.mult),
                              reads=[r_a, r_mk], writes=[r_a])

                def stB(u):
                    nk = u["nk"]
                    a_t, r_a = u["a"]
                    nsub = (nk + 127) // 128
                    pt, r_pt = W["ptr"].next()
                    for j in range(nsub):
                        nkt = min(128, nk - j * 128)
                        P.add("pe", lambda e, j=j, nkt=nkt: e.transpose(out=pt[:nkt, j, :np_], in_=a_t[:np_, j * 128:j * 128 + nkt], identity=self.identb[:np_, :np_]),
                              reads=[r_a, self.r_const], writes=[r_pt])
                    aT_t, r_aT = aT.next()
                    u["aT"] = (aT_t, r_aT)
                    if nk % 128 == 0:
                        P.add("act", lambda e: e.activation(out=aT_t[:, 0:nsub, :np_], in_=pt[:, 0:nsub, :np_], func=AF.Copy), reads=[r_pt], writes=[r_aT])
                    else:
                        for j in range(nsub):
                            nkt = min(128, nk - j * 128)
                            P.add("act", lambda e, j=j, nkt=nkt: e.activation(out=aT_t[:nkt, j, :np_], in_=pt[:nkt, j, :np_], func=AF.Copy), reads=[r_pt], writes=[r_aT])

                def stC(u):
                    nk, k0, h = u["nk"], u["k0"], u["h"]
                    aT_t, r_aT = u["aT"]
                    OS, r_OS = u["hd"]["OS"]
                    v_t, r_v = u["hd"]["v"]
                    nsub = (nk + 127) // 128
                    for j in range(nsub):
                        nkt = min(128, nk - j * 128)
                        P.add("pe", lambda e, j=j, nkt=nkt, st_=(u["first"] and j == 0), sp_=(u["last"] and j == nsub - 1): e.matmul(
                            OS[:np_, 0:128], lhsT=aT_t[:nkt, j, :np_], rhs=v_t[:nkt, k0 // 128 + j, :], start=st_, stop=sp_),
                            reads=[r_aT, r_v], writes=[r_OS])
                    if u["last"]:
                        P.add("act", lambda e: e.activation(out=o_t[:np_, h * 128:(h + 1) * 128], in_=OS[:np_, 0:128], func=AF.Copy), reads=[r_OS], writes=[r_o])

                self.pipeline(units, [(0, stA), (3, stB), (4, stC)])
                self.merged_store(st, W, o_t, r_o, np_, c.HW // 128, sl["col0"], gT, r_g)
            P.end()

    def own_tiles(self):
        c = self.c
        tiles = []
        for s in range(c.NS):
            tiles.append(dict(np=128, col0=s * 128, r=0, x=self.din["xq"][s * 128:(s + 1) * 128, :], idx=s))
        tiles.append(dict(np=c.DS, col0=c.NTQ, r=1, x=self.din["xs"][:, :], idx=c.NS))
        return tiles

    def ph_wout(self):
        c, P, nc = self.c, self.P, self.nc
        P.begin()
        with contextlib.ExitStack() as st:
            mT = self.sb(st, "mT", [128, c.KC, c.NTA], BF16)
            r_mT = Res()
            self.dma("sp", mT[:], self.mT[:, :, :], writes=[r_mT])
            tiles = self.own_tiles()
            r_hT = [r_mT for _ in tiles]
            wrot = Rot([self.sb(st, "wA", [128, c.KC, 512], BF16) for _ in range(2)])
            prot = Rot([self.ps(st, "pA", [128, 512], F32) for _ in range(3)])
            gbc = [Rot([self.sb(st, "gbc", [128, 512], F32) for _ in range(2)]) for _ in range(2)]
            xb = Rot([self.sb(st, "xb", [128, 512], F32) for _ in range(3)])
            tb = Rot([self.sb(st, "tb", [128, 512], F32) for _ in range(3)])
            junk = Rot([self.sb(st, "junk", [128, 512], BF16) for _ in range(2)])
            units = []
            ss2, r_ss2 = self.ss2, self.r_ss2
            nU = c.D // 512

            def mk(u):
                cur = {}

                def ev(ti, t, pp, r_pp):
                    np_, r = t["np"], t["r"]
                    if r not in cur:
                        g, r_g = gbc[r].next()
                        self.dma("sp", g[:], self.modD[r, 2 * c.D + u * 512:2 * c.D + (u + 1) * 512].partition_broadcast(128), writes=[r_g])
                        cur[r] = (g, r_g)
                    g, r_g = cur[r]
                    x_t, r_x = xb.next()
                    t_t, r_t = tb.next()
                    j, r_j = junk.next()
                    self.dma("sp", x_t[:np_], t["x"][:, u * 512:(u + 1) * 512], writes=[r_x])
                    P.add("dve", lambda e: e.tensor_tensor(out=t_t[:np_], in0=pp[:np_, :], in1=g[:np_], op=ALU.mult), reads=[r_pp, r_g], writes=[r_t])
                    P.add("dve", lambda e: e.tensor_tensor(out=t_t[:np_], in0=t_t[:np_], in1=x_t[:np_], op=ALU.add), reads=[r_t, r_x], writes=[r_t])
                    P.add("act", lambda e: e.activation(out=j[:np_], in_=t_t[:np_], func=AF.Square, accum_out=ss2[:np_, t["idx"] * nU + u:t["idx"] * nU + u + 1]),
                          reads=[r_t], writes=[r_j, r_ss2])
                    self.dma("sp", self.x2[t["col0"]:t["col0"] + np_, u * 512:(u + 1) * 512], t_t[:np_], reads=[r_t])
                return ev

            for u in range(nU):
                units.append(dict(blocks=[(u * 512, 512)], evac=mk(u)))
            self.proj_pass(st, tiles, units, self.din["w_out"], mT, r_hT, wrot, prot)
            P.end()

    def ph_ffn(self):
        c, P, nc = self.c, self.P, self.nc
        FC = c.FC
        nU = c.D // 512
        P.begin()
        with contextlib.ExitStack() as st:
            W = {}
            DH = c.D // 2
            W["xt"] = Rot([self.sb(st, "xt", [128, DH], F32)])
            W["xn"] = Rot([self.sb(st, "xn", [128, DH], BF16)])
            for k in ("ss", "sv", "rs"):
                W[k] = Rot([self.sb(st, k, [128, 1], F32) for _ in range(4)])
            W["ptr"] = Rot([self.ps(st, "ptr", [128, 8, 128], BF16) for _ in range(1)])
            mv, r_mod = self.load_modvecs(st, [3, 4], Rot([self.ps(st, "pmv", [128, 128], F32)]))
            GT, SHT = self.make_GS(st, mv, r_mod, "g_ffnT", 4, 3)
            h2T = self.sb(st, "h2T", [128, c.KC, 512 + c.DS], BF16)
            actT = self.sb(st, "actT", [128, FC, 512 + c.DS], BF16)
            wg = Rot([self.sb(st, "wg", [128, c.KC, 128], BF16) for _ in range(2)])
            wu = Rot([self.sb(st, "wu", [128, c.KC, 128], BF16) for _ in range(2)])
            wd = Rot([self.sb(st, "wd", [128, 4, 512], BF16) for _ in range(2)])
            pbank = [self.ps(st, "pb", [128, 512], F32) for _ in range(5)]
            r_pbank = [Res(excl=True) for _ in range(5)]
            sil = Rot([self.sb(st, "sil", [128, 512 + c.DS], F32) for _ in range(2)])
            gbc = [Rot([self.sb(st, "gbc", [128, 512], F32) for _ in range(1)]) for _ in range(2)]
            xb = Rot([self.sb(st, "xb", [128, 512], F32) for _ in range(2)])
            tb = Rot([self.sb(st, "tb", [128, 512], F32) for _ in range(2)])
            junk = Rot([self.sb(st, "junk", [128, 512], BF16) for _ in range(1)])
            ss2, r_ss2, ss3, r_ss3 = self.ss2, self.r_ss2, self.ss3, self.r_ss3
            all_tiles = self.own_tiles()
            groups = []
            s0 = 0
            while s0 < c.NS:
                ns = min(4, c.NS - s0)
                g = [dict(t, lcol=k * 128) for k, t in enumerate(all_tiles[s0:s0 + ns])]
                groups.append(g)
                s0 += ns
            groups[0].append(dict(all_tiles[-1], lcol=512))
            for g in groups:
                r_h2 = Res()
                r_act = Res()
                npr = sum(t["np"] for t in g if t["r"] == 0)
                has_s = any(t["r"] == 1 for t in g)
                for t in g:
                    np_ = t["np"]
                    ss, r_ss = W["ss"].next(); sv, r_sv = W["sv"].next(); rs, r_rs = W["rs"].next()
                    P.add("dve", lambda e, ss=ss, t=t, np_=np_: e.tensor_reduce(out=ss[:np_], in_=ss2[:np_, t["idx"] * nU:(t["idx"] + 1) * nU], axis=AX.X, op=ALU.add), reads=[r_ss2], writes=[r_ss])
                    self.rsqrt_chain(ss, np_, 1.0 / c.D, sv, rs, r_ss, r_sv, r_rs)
                    GTr, SHr = GT[t["r"]], SHT[t["r"]]
                    lcol = t["lcol"]
                    for hf in range(2):
                        xt, r_xt = W["xt"].next(); xn, r_xn = W["xn"].next()
                        self.dma("sp", xt[:np_], self.x2[t["col0"]:t["col0"] + np_, hf * DH:(hf + 1) * DH], writes=[r_xt])
                        P.add("dve", lambda e, xn=xn, xt=xt, rs=rs, np_=np_: e.tensor_scalar(out=xn[:np_], in0=xt[:np_], scalar1=rs[:np_], scalar2=None, op0=ALU.mult), reads=[r_xt, r_rs], writes=[r_xn])
                        kb = hf * (c.KC // 2)

                        def evac(k0, n, pt, r_pt, np_=np_, lcol=lcol, GTr=GTr, SHr=SHr, kb=kb):
                            for j in range(n):
                                kc = kb + k0 + j
                                if j % 2 == 0:
                                    P.add("act", lambda e, kc=kc, j=j: e.activation(out=h2T[:, kc, lcol:lcol + np_], in_=pt[:, j, :np_], func=AF.Identity, scale=GTr[:, kc:kc + 1], bias=SHr[:, kc:kc + 1]),
                                          reads=[r_pt, r_mod], writes=[r_h2])
                                else:
                                    P.add("dve", lambda e, kc=kc, j=j: e.tensor_scalar(out=h2T[:, kc, lcol:lcol + np_], in0=pt[:, j, :np_], scalar1=GTr[:, kc:kc + 1], scalar2=SHr[:, kc:kc + 1], op0=ALU.mult, op1=ALU.add),
                                          reads=[r_pt, r_mod], writes=[r_h2])
                        self.transposes_evac(xn, np_, c.KC // 2, W["ptr"], evac, r_xn)
                for fc in range(FC):
                    wg_t, r_wg = wg.next()
                    wu_t, r_wu = wu.next()
                    self.dma("pool", wg_t[:], self.din["w_gate"][:, fc * 128:(fc + 1) * 128].rearrange("(k p) n -> p k n", p=128), writes=[r_wg])
                    self.dma("pool", wu_t[:], self.din["w_up"][:, fc * 128:(fc + 1) * 128].rearrange("(k p) n -> p k n", p=128), writes=[r_wu])
                    bg, bu = (fc % 2) * 2, (fc % 2) * 2 + 1
                    for (w_t, r_w, bk) in ((wg_t, r_wg, bg), (wu_t, r_wu, bu)):
                        for kc in range(c.KC):
                            P.add("pe", lambda e, kc=kc, w_t=w_t, bk=bk: e.matmul(pbank[bk][:, 0:npr], lhsT=w_t[:, kc, :], rhs=h2T[:, kc, 0:npr], start=(kc == 0), stop=(kc == c.KC - 1)),
                                  reads=[r_w, r_h2], writes=[r_pbank[bk]])
                    if has_s:
                        for (w_t, r_w, o) in ((wg_t, r_wg, 0), (wu_t, r_wu, 64)):
                            for kc in range(c.KC):
                                P.add("pe", lambda e, kc=kc, w_t=w_t, o=o: e.matmul(pbank[4][:, o:o + c.DS], lhsT=w_t[:, kc, :], rhs=h2T[:, kc, 512:512 + c.DS], start=(kc == 0), stop=(kc == c.KC - 1)),
                                      reads=[r_w, r_h2], writes=[r_pbank[4]])
                    s_t, r_s = sil.next()
                    P.add("act", lambda e, s_t=s_t, bg=bg: e.activation(out=s_t[:, 0:npr], in_=pbank[bg][:, 0:npr], func=AF.Silu), reads=[r_pbank[bg]], writes=[r_s])
                    P.add("dve", lambda e, s_t=s_t, bu=bu, fc=fc: e.tensor_tensor(out=actT[:, fc, 0:npr], in0=pbank[bu][:, 0:npr], in1=s_t[:, 0:npr], op=ALU.mult), reads=[r_pbank[bu], r_s], writes=[r_act])
                    if has_s:
                        P.add("act", lambda e, s_t=s_t: e.activation(out=s_t[:, 512:512 + c.DS], in_=pbank[4][:, 0:c.DS], func=AF.Silu), reads=[r_pbank[4]], writes=[r_s])
                        P.add("dve", lambda e, s_t=s_t, fc=fc: e.tensor_tensor(out=actT[:, fc, 512:512 + c.DS], in0=pbank[4][:, 64:64 + c.DS], in1=s_t[:, 512:512 + c.DS], op=ALU.mult),
                              reads=[r_pbank[4], r_s], writes=[r_act])
                for u in range(nU):
                    cur = {}
                    nfg = (FC + 3) // 4
                    for fg in range(nfg):
                        ng = min(4, FC - fg * 4)
                        wd_t, r_wd = wd.next()
                        self.dma("pool", wd_t[:, 0:ng, :], self.din["w_down"][fg * 512:fg * 512 + ng * 128, u * 512:(u + 1) * 512].rearrange("(g p) n -> p g n", p=128), writes=[r_wd])
                        for k, t in enumerate(g):
                            np_, lcol = t["np"], t["lcol"]
                            for gi in range(ng):
                                fc = fg * 4 + gi
                                P.add("pe", lambda e, k=k, gi=gi, fc=fc, np_=np_, lcol=lcol, wd_t=wd_t: e.matmul(
                                    pbank[k][:np_, :], lhsT=actT[:, fc, lcol:lcol + np_], rhs=wd_t[:, gi, :], start=(fc == 0), stop=(fc == FC - 1)),
                                    reads=[r_act, r_wd], writes=[r_pbank[k]])
                    for k, t in enumerate(g):
                        np_, r = t["np"], t["r"]
                        if r not in cur:
                            gb, r_gb = gbc[r].next()
                            self.dma("sp", gb[:], self.modD[r, 5 * c.D + u * 512:5 * c.D + (u + 1) * 512].partition_broadcast(128), writes=[r_gb])
                            cur[r] = (gb, r_gb)
                        gb, r_gb = cur[r]
                        x_t, r_x = xb.next()
                        t_t, r_t = tb.next()
                        j, r_j = junk.next()
                        self.dma("sp", x_t[:np_], self.x2[t["col0"]:t["col0"] + np_, u * 512:(u + 1) * 512], writes=[r_x])
                        P.add("dve", lambda e, k=k, t_t=t_t, gb=gb, np_=np_: e.tensor_tensor(out=t_t[:np_], in0=pbank[k][:np_, :], in1=gb[:np_], op=ALU.mult), reads=[r_pbank[k], r_gb], writes=[r_t])
                        P.add("dve", lambda e, t_t=t_t, x_t=x_t, np_=np_: e.tensor_tensor(out=t_t[:np_], in0=t_t[:np_], in1=x_t[:np_], op=ALU.add), reads=[r_t, r_x], writes=[r_t])
                        P.add("act", lambda e, j=j, t_t=t_t, np_=np_, t=t, u=u: e.activation(out=j[:np_], in_=t_t[:np_], func=AF.Square, accum_out=ss3[:np_, t["idx"] * nU + u:t["idx"] * nU + u + 1]),
                              reads=[r_t], writes=[r_j, r_ss3])
                        self.dma("sp", self.x3[t["col0"]:t["col0"] + np_, u * 512:(u + 1) * 512], t_t[:np_], reads=[r_t])
            P.end()

    def ph_final(self):
        c, P, nc = self.c, self.P, self.nc
        nU = c.D // 512
        P.begin()
        with contextlib.ExitStack() as st:
            gf = self.sb(st, "gf", [128, c.D], F32)
            r_gf = Res()
            self.dma("sp", gf[:], self.din["g_fin"].partition_broadcast(128), writes=[r_gf])
            xt = Rot([self.sb(st, "xt", [128, c.D], F32) for _ in range(2)])
            yt = Rot([self.sb(st, "yt", [128, c.D], F32) for _ in range(2)])
            W = {}
            for k in ("ss", "sv", "rs"):
                W[k] = Rot([self.sb(st, k, [128, 1], F32) for _ in range(4)])
            for t in self.own_tiles():
                np_ = t["np"]
                x_t, r_x = xt.next()
                y_t, r_y = yt.next()
                ss, r_ss = W["ss"].next(); sv, r_sv = W["sv"].next(); rs, r_rs = W["rs"].next()
                self.dma("sp", x_t[:np_], self.x3[t["col0"]:t["col0"] + np_, :], writes=[r_x])
                P.add("dve", lambda e, ss=ss, t=t, np_=np_: e.tensor_reduce(out=ss[:np_], in_=self.ss3[:np_, t["idx"] * nU:(t["idx"] + 1) * nU], axis=AX.X, op=ALU.add), reads=[self.r_ss3], writes=[r_ss])
                self.rsqrt_chain(ss, np_, 1.0 / c.D, sv, rs, r_ss, r_sv, r_rs)
                P.add("dve", lambda e, y_t=y_t, x_t=x_t, rs=rs, np_=np_: e.scalar_tensor_tensor(out=y_t[:np_], in0=x_t[:np_], scalar=rs[:np_], in1=gf[:np_], op0=ALU.mult, op1=ALU.mult),
                      reads=[r_x, r_rs, r_gf], writes=[r_y])
                if t["r"] == 0:
                    self.dma("sp", self.dout["y_p"][t["col0"]:t["col0"] + np_, :], y_t[:np_], reads=[r_y])
                else:
                    self.dma("sp", self.dout["y_s"][:, :], y_t[:np_], reads=[r_y])
            P.end()

    def build(self, phases=None):
        c, nc = self.c, self.nc
        self.declare()
        with contextlib.ExitStack() as st:
            self.P = Prog(nc, st)
            P = self.P
            self.identf = st.enter_context(nc.sbuf_tensor("identf", [128, 128], F32))
            self.identb = st.enter_context(nc.sbuf_tensor("identb", [128, 128], BF16))
            self.nhalf = st.enter_context(nc.sbuf_tensor("nhalf", [128, 1], F32))
            nU = c.D // 512
            self.ss2 = st.enter_context(nc.sbuf_tensor("ss2", [128, (c.NS + 1) * nU], F32))
            self.ss3 = st.enter_context(nc.sbuf_tensor("ss3", [128, (c.NS + 1) * nU], F32))
            self.r_ss2, self.r_ss3 = Res(), Res()
            self.r_const = Res()
            P.begin()
            idf, idb, nh = self.identf, self.identb, self.nhalf
            P.add("pool", lambda e: e.memset(idf[:], 1.0), writes=[self.r_const])
            P.add("pool", lambda e: e.affine_select(out=idf[:], in_=idf[:], pattern=[[-1, 128]], compare_op=ALU.is_equal, fill=0.0, base=0, channel_multiplier=1),
                  reads=[self.r_const], writes=[self.r_const])
            P.add("pool", lambda e: e.tensor_copy(out=idb[:], in_=idf[:]), reads=[self.r_const], writes=[self.r_const])
            P.add("pool", lambda e: e.memset(nh[:], -0.5), writes=[self.r_const])
            P.end()
            allph = [("mod", self.ph_mod), ("a1", self.ph_a1), ("a2", self.ph_a2), ("mla", self.ph_mla), ("sb", self.ph_sb),
                     ("wout", self.ph_wout), ("ffn", self.ph_ffn), ("final", self.ph_final)]
            for name, fn in allph:
                if phases is None or name in phases:
                    fn()
        return nc


class _Shift:
    def __init__(self, ap, shift):
        self.ap, self.shift = ap, shift

    def __getitem__(self, key):
        rows, cols = key
        rows = slice(rows.start + self.shift, rows.stop + self.shift)
        return self.ap[rows, cols]


def rope_tables(pos, dtype=np.float32):
    half = 32
    inv = (1.0 / (np.float32(10000.0) ** (np.arange(half, dtype=np.float32) / np.float32(half)))).astype(np.float32)
    ang = pos.astype(np.float32)[:, None] * inv[None, :]
    return np.cos(ang).astype(dtype), np.sin(ang).astype(dtype)


def prep_inputs(cfg, inp):
    c = cfg
    D, H, KC = c.D, c.H, c.KC
    f = lambda a: np.ascontiguousarray(np.asarray(a, dtype=np.float32))
    shared = {}
    shared["w_ada"] = f(inp["w_ada"][0]); shared["b_ada"] = f(inp["b_ada"][0])
    featT = lambda v: f(np.asarray(v).reshape(KC, 128).T)
    shared["g_mixT"] = featT(inp["g_mix"][0]); shared["g_ffnT"] = featT(inp["g_ffn"][0])
    shared["g_outT"] = featT(np.concatenate([np.asarray(inp["g_out_mla"][0]), np.asarray(inp["g_out_sb"][0])]))
    shared["w_in"] = f(inp["w_in"][0]); shared["g_q"] = f(inp["g_q_lat"][0]); shared["g_kv"] = f(inp["g_kv_lat"][0])
    wuq = np.asarray(inp["w_uq"][0])
    shared["w_uq"] = f(wuq.reshape(c.QL, H * 192))
    sw = np.concatenate([wuq[:, :, 160:192], wuq[:, :, 128:160]], axis=2)
    shared["w_uqs"] = f(sw.reshape(c.QL, H * 64))
    shared["w_ukT"] = f(np.transpose(np.asarray(inp["w_uk"][0]), (1, 2, 0)))
    shared["w_uv"] = f(np.asarray(inp["w_uv"][0]).reshape(512, H * 128))
    shared["w_out"] = f(inp["w_out"][0]); shared["w_gate"] = f(inp["w_gate"][0]); shared["w_up"] = f(inp["w_up"][0])
    shared["w_down"] = f(inp["w_down"][0]); shared["g_fin"] = f(inp["g_final"])
    cosp, sinp = rope_tables(np.arange(c.SEQ))
    shared["ropeKp"] = f(np.concatenate([cosp, cosp, -sinp, sinp], axis=1))
    coss, sins = rope_tables(c.PAST + np.arange(c.DS))
    shared["ropeKs"] = f(np.concatenate([coss, coss, -sins, sins], axis=1))
    qi = np.arange(c.DS)[:, None]; ki = np.arange(c.DS)[None, :]
    ms = (ki < qi).astype(np.float32)
    shared["ms_sb"] = f(ms); shared["ms_nsb"] = f(1.0 - ms)
    xpr = np.asarray(inp["x_prompt"]); xsm = np.asarray(inp["x_sample"])
    cp = np.asarray(inp["c_prompt"]); csm = np.asarray(inp["c_sample"])
    in_maps = []
    for core in range(c.NCORES):
        b, j = core // c.G, core % c.G
        m = dict(shared)
        m["xp"] = f(xpr[b])
        own_tiles = [c.G * i + j for i in range(c.NS)]
        pos_own = np.concatenate([np.arange(t * 128, (t + 1) * 128) for t in own_tiles])
        m["xq"] = f(xpr[b][pos_own])
        m["xs"] = f(xsm[core])
        m["c_lat"] = f(inp["cache_mla_latent"][0][core]); m["c_kr"] = f(inp["cache_mla_krope"][0][core])
        m["c_k"] = f(np.asarray(inp["cache_sb_k"][0][core]).reshape(c.PAST, c.HW))
        m["c_v"] = f(np.asarray(inp["cache_sb_v"][0][core]).reshape(c.PAST, c.HW))
        cc = np.stack([cp[b], csm[core]], axis=0)
        m["cT"] = f(cc.reshape(2, KC, 128).transpose(2, 1, 0).reshape(128, KC * 2))
        pos_all = np.concatenate([pos_own, c.PAST + np.arange(c.DS)])
        cq, sq = rope_tables(pos_all)
        cs1 = np.concatenate([cq, cq], axis=1).T
        cs2 = np.concatenate([-sq, sq], axis=1).T
        m["ropeQ"] = f(np.stack([cs1, cs2], axis=1))
        qpos = 128 * j + np.arange(128)[:, None]
        kk = np.arange(512)[None, :]
        vis_mla = (kk // 64) <= (qpos // 64)
        m["m_mla"] = f(np.where(vis_mla, 0.0, NEG))
        vis_sb = kk < qpos
        m["m_sb"] = f(vis_sb.astype(np.float32)); m["m_nsb"] = f(1.0 - vis_sb.astype(np.float32))
        in_maps.append(m)
    return in_maps


def assemble(cfg, results):
    c = cfg
    D, H = c.D, c.H
    y_p = np.zeros((c.NB, c.SEQ, D), np.float32)
    y_s = np.zeros((c.NCORES, c.DS, D), np.float32)
    p_lat = np.zeros((1, c.NB, c.SEQ, 512), np.float32); p_kr = np.zeros((1, c.NB, c.SEQ, 64), np.float32)
    p_k = np.zeros((1, c.NB, c.SEQ, H, 128), np.float32); p_v = np.zeros((1, c.NB, c.SEQ, H, 128), np.float32)
    s_lat = np.zeros((1, c.NCORES, c.DS, 512), np.float32); s_kr = np.zeros((1, c.NCORES, c.DS, 64), np.float32)
    s_k = np.zeros((1, c.NCORES, c.DS, H, 128), np.float32); s_v = np.zeros((1, c.NCORES, c.DS, H, 128), np.float32)
    for core in range(c.NCORES):
        r = results[core]
        b, j = core // c.G, core % c.G
        for i in range(c.NS):
            t = c.G * i + j
            y_p[b, t * 128:(t + 1) * 128] = r["y_p"][i * 128:(i + 1) * 128]
        y_s[core] = r["y_s"]
        if j == 0:
            p_lat[0, b] = r["p_lat"]; p_kr[0, b] = r["p_kr"]
            p_k[0, b] = r["p_sbk"].reshape(c.SEQ, H, 128); p_v[0, b] = r["p_sbv"].reshape(c.SEQ, H, 128)
        s_lat[0, core] = r["s_lat"]; s_kr[0, core] = r["s_kr"]
        s_k[0, core] = r["s_sbk"].reshape(c.DS, H, 128); s_v[0, core] = r["s_sbv"].reshape(c.DS, H, 128)
    return (y_p, y_s, p_lat, p_kr, p_k, p_v, s_lat, s_kr, s_k, s_v)


def run_cfg(cfg, inputs, phases=None, trace=False):
    b = Builder(cfg)
    nc = b.build(phases)
    in_maps = prep_inputs(cfg, inputs)
    res = run_bass_kernel_spmd(nc, in_maps, core_ids=list(range(cfg.NCORES)), **({"trace": True} if trace else {}))
    return res


def kernel(**inputs):
    cfg = Cfg()
    res = run_cfg(cfg, inputs)
    return assemble(cfg, res.results)
```

```python
import contextlib
import math
import types
import os
DBG = set(os.environ.get('KDBG', '').split(','))
import numpy as np
import concourse.bass as bass
import concourse.mybir as mybir
from concourse.bass_utils import run_bass_kernel_spmd

F32 = mybir.dt.float32
BF16 = mybir.dt.bfloat16
AF = mybir.ActivationFunctionType
ALU = mybir.AluOpType
AX = mybir.AxisListType
EPS = 1e-6
NEG = -30000.0


class Cfg:
    def __init__(s, D=4096, QL=1024, DFF=11008, SEQ=4096, PAST=2048, DS=32, NB=2, G=4, NCORES=8, TB=1024, GMAX=5):
        s.D, s.QL, s.DFF, s.SEQ, s.PAST, s.DS, s.NB, s.G, s.NCORES = D, QL, DFF, SEQ, PAST, DS, NB, G, NCORES
        s.H = D // 256
        s.HW = s.H * 128
        s.KC = D // 128
        s.MLA_IN = QL + 512 + 64
        s.INC = s.MLA_IN + 3 * s.HW
        s.NS = SEQ // (128 * G)
        s.NTQ = s.NS * 128
        s.NTA = s.NTQ + DS
        s.TB = min(TB, SEQ)
        s.GMAX = GMAX
        s.FC = DFF // 128
        s.KS = PAST + DS
        s.KMAX = max(SEQ, s.KS)
        s.MLA_SCALE = (128 + 64) ** -0.5
        s.SB_SCALE = 128 ** -0.5


PSUM_IDS = set()


def _freeze(fn):
    if fn.__closure__ is None:
        return fn
    cells = []
    for cl in fn.__closure__:
        try:
            cells.append(types.CellType(cl.cell_contents))
        except ValueError:
            cells.append(cl)
    return types.FunctionType(fn.__code__, fn.__globals__, fn.__name__, fn.__defaults__, tuple(cells))


class Res:
    __slots__ = ("name", "w", "rs", "excl")

    def __init__(self, name="", excl=False):
        self.name = name
        self.w = None
        self.rs = []
        self.excl = excl


class Op:
    __slots__ = ("eng", "fn", "reads", "writes", "dma", "deps", "marked", "sem", "cnt", "waits", "ph")

    def __init__(self, eng, fn, reads, writes, dma):
        self.eng = eng; self.fn = fn; self.reads = reads; self.writes = writes; self.dma = dma
        self.deps = []; self.marked = False; self.sem = None; self.cnt = 0; self.waits = []; self.ph = 0


class Prog:
    ENGS = ("pe", "act", "dve", "pool", "sp")

    def __init__(self, nc, stack, n_dma_sems=8, sem_limit=4000):
        self.nc = nc
        self.stack = stack
        self.n_dma_sems = n_dma_sems
        self.sem_limit = sem_limit
        self.eng_sem = {}
        self.eng_cnt = {}
        self.dma_sems = {}
        self.dma_rr = {}
        self.dma_last = {}
        self.nsem = 0
        self.prev_done = None
        self.ops = []
        self.nops_total = 0
        self.phase_idx = 0

    def newsem(self, tag):
        self.nsem += 1
        return self.stack.enter_context(self.nc.semaphore("s_%s_%d" % (tag, self.nsem)))

    def begin(self):
        self.ops = []
        self.phase_idx += 1

    def add(self, eng, fn, reads=(), writes=(), dma=False):
        op = Op(eng, _freeze(fn), list(reads), list(writes), dma)
        op.ph = self.phase_idx
        self.ops.append(op)
        return op

    def mark(self, name):
        if "marks" in DBG:
            print("MARK phase %d %s ops=%d" % (self.phase_idx, name, len(self.ops)))

    def end(self):
        nc = self.nc
        mo = os.environ.get("KMAXOPS")
        if mo:
            ph, n = mo.split(":")
            if int(ph) == self.phase_idx:
                self.ops = self.ops[:int(n)]
        ops = self.ops
        self.nops_total += len(ops)
        for op in ops:
            deps = []
            reads = [r for r in op.reads if not r.excl]
            writes = op.writes + [r for r in op.reads if r.excl]
            for r in reads:
                if r.w is not None:
                    deps.append(r.w)
            for r in writes:
                if r.w is not None:
                    deps.append(r.w)
                deps.extend(r.rs)
            for r in reads:
                r.rs.append(op)
            for r in writes:
                r.w = op
                r.rs = []
            seen = set()
            for d in deps:
                if d is op or id(d) in seen or d.ph != op.ph:
                    continue
                seen.add(id(d))
                if (not d.dma) and (not op.dma) and d.eng == "pe" and op.eng == "pe":
                    continue
                op.deps.append(d)
                d.marked = True
        last_comp = {}
        for op in ops:
            if not op.dma:
                last_comp[op.eng] = op
        for op in last_comp.values():
            op.marked = True
        waited = {e: {} for e in self.ENGS}
        for op in ops:
            e = op.eng
            w = waited[e]
            waits = []
            for d in op.deps:
                key = id(d.sem)
                if w.get(key, 0) >= d.cnt:
                    continue
                w[key] = d.cnt
                waits.append((d.sem, d.cnt))
            if op.dma:
                if e not in self.dma_sems:
                    self.dma_sems[e] = [self.newsem("dma" + e) for _ in range(self.n_dma_sems)]
                    self.dma_rr[e] = 0
                    self.dma_last[e] = [0] * self.n_dma_sems
                i = self.dma_rr[e]
                self.dma_rr[e] = (i + 1) % self.n_dma_sems
                s = self.dma_sems[e][i]
                prev = self.dma_last[e][i]
                if prev > 0 and w.get(id(s), 0) < prev:
                    w[id(s)] = prev
                    waits.append((s, prev))
                op.sem = s
                op.cnt = prev + 16
                self.dma_last[e][i] = op.cnt
                op.marked = True
            elif op.marked:
                if e not in self.eng_sem or self.eng_cnt[e] >= self.sem_limit:
                    self.eng_sem[e] = self.newsem(e)
                    self.eng_cnt[e] = 0
                self.eng_cnt[e] += 1
                op.sem = self.eng_sem[e]
                op.cnt = self.eng_cnt[e]
            m = {}
            for s, c in waits:
                k = id(s)
                if k not in m or m[k][1] < c:
                    m[k] = (s, c)
            op.waits = list(m.values())
        by_eng = {e: [op for op in ops if op.eng == e] for e in self.ENGS}
        done = self.newsem("done")
        prev_done = self.prev_done
        prog = self

        with nc.Block() as block:
            def run(eng_obj, ename):
                if prev_done is not None:
                    eng_obj.wait_ge(prev_done, len(prog.ENGS))
                for op in by_eng[ename]:
                    for s, c in op.waits:
                        eng_obj.wait_ge(s, c)
                    ins = op.fn(eng_obj)
                    if op.marked:
                        ins.then_inc(op.sem, 16 if op.dma else 1)
                if ename in prog.dma_sems:
                    for s, c in zip(prog.dma_sems[ename], prog.dma_last[ename]):
                        if c > 0:
                            eng_obj.wait_ge(s, c)
                lc = last_comp.get(ename)
                if lc is not None:
                    eng_obj.wait_ge(lc.sem, lc.cnt)
                eng_obj.sem_inc(done, 1)

            @block.tensor
            def _(eng):
                run(eng, "pe")

            @block.scalar
            def _(eng):
                run(eng, "act")

            @block.vector
            def _(eng):
                run(eng, "dve")

            @block.gpsimd
            def _(eng):
                run(eng, "pool")

            @block.sync
            def _(eng):
                run(eng, "sp")

        self.prev_done = done
        self.ops = []


class Rot:
    def __init__(self, bufs):
        self.bufs = bufs
        self.res = [Res(excl=(id(b) in PSUM_IDS)) for b in bufs]
        self.i = 0

    def next(self):
        k = self.i % len(self.bufs)
        self.i += 1
        return self.bufs[k], self.res[k]


class Builder:
    def __init__(self, cfg):
        self.c = cfg
        self.nc = bass.Bass("TRN2", target_bir_lowering=False)
        self.din = {}
        self.dout = {}

    def declare(self):
        c, nc = self.c, self.nc

        def I(name, shape):
            self.din[name] = nc.dram_tensor(name, list(shape), F32, kind="ExternalInput").ap()

        def O(name, shape):
            self.dout[name] = nc.dram_tensor(name, list(shape), F32, kind="ExternalOutput").ap()

        def S(name, shape, dt):
            return nc.dram_tensor(name, list(shape), dt).ap()

        D, H, HW, KC = c.D, c.H, c.HW, c.KC
        I("xp", [c.SEQ, D]); I("xq", [c.NTQ, D]); I("xs", [c.DS, D])
        I("c_lat", [c.PAST, 512]); I("c_kr", [c.PAST, 64]); I("c_k", [c.PAST, HW]); I("c_v", [c.PAST, HW])
        I("cT", [128, KC * 2])
        I("w_ada", [D, 6 * D]); I("b_ada", [6 * D])
        I("g_mixT", [128, KC]); I("g_ffnT", [128, KC]); I("g_outT", [128, KC])
        I("w_in", [D, c.INC]); I("g_q", [c.QL]); I("g_kv", [512])
        I("w_uq", [c.QL, H * 192]); I("w_uqs", [c.QL, H * 64]); I("w_ukT", [H, 128, 512]); I("w_uv", [512, HW])
        I("w_out", [D, D]); I("w_gate", [D, c.DFF]); I("w_up", [D, c.DFF]); I("w_down", [c.DFF, D]); I("g_fin", [D])
        I("ropeKp", [c.SEQ, 128]); I("ropeKs", [c.DS, 128]); I("ropeQ", [64, 2, c.NTA])
        I("m_mla", [128, 512]); I("m_sb", [128, 512]); I("m_nsb", [128, 512])
        I("ms_sb", [c.DS, c.DS]); I("ms_nsb", [c.DS, c.DS])
        O("y_p", [c.NTQ, D]); O("y_s", [c.DS, D])
        O("p_lat", [c.SEQ, 512]); O("p_kr", [c.SEQ, 64]); O("p_sbk", [c.SEQ, HW]); O("p_sbv", [c.SEQ, HW])
        O("s_lat", [c.DS, 512]); O("s_kr", [c.DS, 64]); O("s_sbk", [c.DS, HW]); O("s_sbv", [c.DS, HW])
        self.modD = S("modD", [2, 6 * D], F32)
        self.scr = {}
        for st, nk in (("P", c.SEQ), ("S", c.KS)):
            self.scr[st] = dict(
                lat=S("lat" + st, [nk, 512], BF16), latT=S("latT" + st, [128, 4, nk], BF16),
                krT=S("krT" + st, [64, nk], BF16), kT=S("kT" + st, [128, H, nk], BF16), v=S("v" + st, [nk, HW], BF16), nk=nk)
        self.qabs = S("qabs", [c.NS + 1, 128, H, 4, 128], BF16)
        self.qrope = S("qrope", [c.NS + 1, 64, H, 128], BF16)
        self.sbq = S("sbq", [c.NS + 1, 128, H, 128], BF16)
        self.mT = S("mTs", [128, KC, c.NTA], BF16)
        self.x2 = S("x2s", [c.NTA, D], F32)
        self.x3 = S("x3s", [c.NTA, D], F32)

    def dma(self, q, out, in_, reads=(), writes=()):
        return self.P.add(q, lambda e: e.dma_start(out=out, in_=in_), reads, writes, dma=True)

    def rsqrt_chain(self, ss, np_, inv_n, tmp, out, r_ss, r_tmp, r_out):
        P = self.P
        P.add("dve", lambda e: e.tensor_scalar(out=tmp[:np_], in0=ss[:np_], scalar1=inv_n, scalar2=EPS, op0=ALU.mult, op1=ALU.add),
              reads=[r_ss], writes=[r_tmp])
        nh = self.nhalf
        P.add("pool", lambda e: e.tensor_tensor(out=out[:np_], in0=tmp[:np_], in1=nh[:np_], op=ALU.pow),
              reads=[r_tmp, self.r_const], writes=[r_out])

    def transposes_evac(self, src, np_, nchunk, ptr_rot, evac, r_src, dtype_is_bf16=True, grp=8):
        P = self.P
        ident = self.identb if dtype_is_bf16 else self.identf
        k0 = 0
        while k0 < nchunk:
            n = min(grp, nchunk - k0)
            pt, r_pt = ptr_rot.next()
            for j in range(n):
                k = k0 + j
                P.add("pe", lambda e, k=k, j=j, pt=pt: e.transpose(out=pt[:, j, :np_], in_=src[:np_, k * 128:(k + 1) * 128], identity=ident[:np_, :np_]),
                      reads=[r_src, self.r_const], writes=[r_pt])
            evac(k0, n, pt, r_pt)
            k0 += n

    def prep_s1(self, x_ap, np_, W):
        c, P = self.c, self.P
        xt, r_xt = W["xt"].next()
        xn, r_xn = W["xn"].next()
        ss, r_ss = W["ss"].next()
        sv, r_sv = W["sv"].next()
        rs, r_rs = W["rs"].next()
        self.dma("act", xt[:np_], x_ap, writes=[r_xt])
        P.add("act", lambda e: e.activation(out=xn[:np_], in_=xt[:np_], func=AF.Square, accum_out=ss[:np_]), reads=[r_xt], writes=[r_xn, r_ss])
        self.rsqrt_chain(ss, np_, 1.0 / c.D, sv, rs, r_ss, r_sv, r_rs)
        P.add("dve", lambda e: e.tensor_scalar(out=xn[:np_], in0=xt[:np_], scalar1=rs[:np_], scalar2=None, op0=ALU.mult), reads=[r_xt, r_rs], writes=[r_xn])
        return (xn, r_xn, np_)

    def prep_s2(self, h, GT, SHT, r_mod, hT, col0, r_hT, W):
        c, P = self.c, self.P
        xn, r_xn, np_ = h
        cnt = [0]

        def evac(k0, n, pt, r_pt):
            for j in range(n):
                kc = k0 + j
                if cnt[0] % 2 == 0:
                    P.add("act", lambda e, kc=kc, j=j, pt=pt: e.activation(out=hT[:, kc, col0:col0 + np_], in_=pt[:, j, :np_], func=AF.Identity,
                                                                        scale=GT[:, kc:kc + 1], bias=SHT[:, kc:kc + 1]),
                          reads=[r_pt, r_mod], writes=[r_hT])
                else:
                    P.add("dve", lambda e, kc=kc, j=j, pt=pt: e.tensor_scalar(out=hT[:, kc, col0:col0 + np_], in0=pt[:, j, :np_], scalar1=GT[:, kc:kc + 1],
                                                                           scalar2=SHT[:, kc:kc + 1], op0=ALU.mult, op1=ALU.add),
                          reads=[r_pt, r_mod], writes=[r_hT])
                cnt[0] += 1

        self.transposes_evac(xn, np_, c.KC, W["ptr"], evac, r_xn)

    def prep_hT(self, x_ap, np_, GT, SHT, r_mod, hT, col0, r_hT, W):
        h = self.prep_s1(x_ap, np_, W)
        self.prep_s2(h, GT, SHT, r_mod, hT, col0, r_hT, W)

    def load_modvecs(self, st, chunks, psum_rot):
        c, P = self.c, self.P
        r_mod = Res("mod")
        out = {}
        for r in range(2):
            for ch in chunks:
                t = st.enter_context(self.nc.sbuf_tensor("mv_%d_%d_%d" % (self.uid(), r, ch), [128, c.KC], F32))
                tmp = st.enter_context(self.nc.sbuf_tensor("mvt_%d_%d_%d" % (self.uid(), r, ch), [c.KC, 128], F32))
                r_tmp = Res()
                self.dma("sp", tmp[:], self.modD[r, ch * c.D:(ch + 1) * c.D].rearrange("(k p) -> k p", p=128), writes=[r_tmp])
                pt, r_pt = psum_rot.next()
                P.add("pe", lambda e, pt=pt, tmp=tmp: e.transpose(out=pt[:, :c.KC], in_=tmp[:, :], identity=self.identf[:c.KC, :c.KC]),
                      reads=[r_tmp, self.r_const], writes=[r_pt])
                P.add("dve", lambda e, pt=pt, t=t: e.tensor_copy(out=t[:], in_=pt[:, :c.KC]), reads=[r_pt], writes=[r_mod])
                out[(r, ch)] = t
        return out, r_mod

    _uid = 0

    def uid(self):
        Builder._uid += 1
        return Builder._uid

    def sb(self, st, name, shape, dt):
        return st.enter_context(self.nc.sbuf_tensor("%s_%d" % (name, self.uid()), list(shape), dt))

    def ps(self, st, name, shape, dt):
        t = st.enter_context(self.nc.psum_tensor("%s_%d" % (name, self.uid()), list(shape), dt))
        PSUM_IDS.add(id(t))
        self._keep = getattr(self, "_keep", [])
        self._keep.append(t)
        return t

    def make_GS(self, st, mv, r_mod, gT_name, ch_scale, ch_shift):
        c, P = self.c, self.P
        g = self.sb(st, "gT", [128, c.KC], F32)
        r_g = Res()
        self.dma("sp", g[:], self.din[gT_name][:, :], writes=[r_g])
        GT = []
        SHT = []
        for r in range(2):
            G = self.sb(st, "G", [128, c.KC], F32)
            sc = mv[(r, ch_scale)]
            P.add("dve", lambda e, G=G, sc=sc: e.scalar_tensor_tensor(out=G[:], in0=sc[:], scalar=1.0, in1=g[:], op0=ALU.add, op1=ALU.mult),
                  reads=[r_mod, r_g], writes=[r_mod])
            GT.append(G)
            SHT.append(mv[(r, ch_shift)])
        return GT, SHT

    def ph_mod(self):
        c, P, nc = self.c, self.P, self.nc
        P.begin()
        with contextlib.ExitStack() as st:
            cTf = self.sb(st, "cTf", [128, c.KC * 2], F32)
            scT = self.sb(st, "scT", [128, c.KC * 2], BF16)
            wb = Rot([self.sb(st, "wb", [128, c.KC, 512], BF16) for _ in range(2)])
            bt = Rot([self.sb(st, "bt", [2, 512], F32) for _ in range(2)])
            mo = Rot([self.sb(st, "mo", [2, 512], F32) for _ in range(2)])
            pm = Rot([self.ps(st, "pm", [128, 512], F32) for _ in range(2)])
            r_c, r_sc = Res(), Res()
            self.dma("sp", cTf[:], self.din["cT"][:, :], writes=[r_c])
            P.add("act", lambda e: e.activation(out=scT[:], in_=cTf[:], func=AF.Silu), reads=[r_c], writes=[r_sc])
            wa, ba = self.din["w_ada"], self.din["b_ada"]
            for cb in range(6 * c.D // 512):
                w, r_w = wb.next()
                b, r_b = bt.next()
                m, r_m = mo.next()
                p, r_p = pm.next()
                self.dma("pool", w[:], wa[:, cb * 512:(cb + 1) * 512].rearrange("(k p) n -> p k n", p=128), writes=[r_w])
                self.dma("sp", b[:], ba[cb * 512:(cb + 1) * 512].partition_broadcast(2), writes=[r_b])
                for kc in range(c.KC):
                    P.add("pe", lambda e, kc=kc, w=w, p=p: e.matmul(p[0:2, :], lhsT=scT[:, kc * 2:kc * 2 + 2], rhs=w[:, kc, :], start=(kc == 0), stop=(kc == c.KC - 1)),
                          reads=[r_sc, r_w], writes=[r_p])
                P.add("dve", lambda e, m=m, p=p, b=b: e.tensor_tensor(out=m[:], in0=p[0:2, :], in1=b[:], op=ALU.add), reads=[r_p, r_b], writes=[r_m])
                self.dma("sp", self.modD[:, cb * 512:(cb + 1) * 512], m[:], reads=[r_m])
            P.end()

    def proj_pass(self, st, tiles, units, w_ap, hT, r_hT, wrot, prot, after_tile=None):
        c, P = self.c, self.P
        for u in units:
            wbs = []
            off = 0
            for (c0, n) in u["blocks"]:
                w, r_w = wrot.next()
                self.dma("pool", w[:, :, :n], w_ap[:, c0:c0 + n].rearrange("(k p) n -> p k n", p=128), writes=[r_w])
                wbs.append((w, r_w, off, n))
                off += n
            for ti, t in enumerate(tiles):
                np_, col0 = t["np"], t["col0"]
                pp, r_pp = prot.next()
                for (w, r_w, o, n) in wbs:
                    for kc in range(c.KC):
                        P.add("pe", lambda e, kc=kc, w=w, pp=pp, o=o, n=n, np_=np_, col0=col0: e.matmul(
                            pp[:np_, o:o + n], lhsT=hT[:, kc, col0:col0 + np_], rhs=w[:, kc, :n], start=(kc == 0), stop=(kc == c.KC - 1)),
                            reads=[r_hT[ti], r_w], writes=[r_pp])
                u["evac"](ti, t, pp, r_pp)
                if after_tile is not None and u is units[-1]:
                    after_tile(ti)

    def kv_evacs(self, st, W, outs, scr, gkv, r_gkv):
        c, P = self.c, self.P
        H = c.H
        units = []

        def lat_T_store(src32, r_src, np_, k0):
            lst, r_lst = W["latst"].next()

            def ev(kk, n, pt, r_pt):
                P.add("act", lambda e: e.activation(out=lst[:, 0:4, :np_], in_=pt[:, 0:4, :np_], func=AF.Copy), reads=[r_pt], writes=[r_lst])
            self.transposes_evac(src32, np_, 4, W["ptf"], ev, r_src, dtype_is_bf16=False, grp=4)
            self.dma("sp", scr["latT"][:, :, k0:k0 + np_], lst[:, :, :np_], reads=[r_lst])

        def kr_T_store(src32, r_src, np_, k0):
            kst, r_kst = W["krst"].next()
            pt, r_pt = W["ptf"].next()
            P.add("pe", lambda e: e.transpose(out=pt[0:64, 0, :np_], in_=src32[:np_, 0:64], identity=self.identf[:np_, :np_]),
                  reads=[r_src, self.r_const], writes=[r_pt])
            P.add("act", lambda e: e.activation(out=kst[0:64, :np_], in_=pt[0:64, 0, :np_], func=AF.Copy), reads=[r_pt], writes=[r_kst])
            self.dma("sp", scr["krT"][:, k0:k0 + np_], kst[0:64, :np_], reads=[r_kst])

        def k_T_group(src32, r_src, np_, h0, k0):
            kst, r_kst = W["kst"].next()

            def ev(kk, n, pt, r_pt):
                P.add("dve", lambda e: e.tensor_copy(out=kst[:, 0:4, :np_], in_=pt[:, 0:4, :np_]), reads=[r_pt], writes=[r_kst])
            self.transposes_evac(src32, np_, 4, W["ptf"], ev, r_src, dtype_is_bf16=False, grp=4)
            self.dma("sp", scr["kT"][:, h0:h0 + 4, k0:k0 + np_], kst[:, :, :np_], reads=[r_kst])

        self.lat_T_store, self.kr_T_store, self.k_T_group = lat_T_store, kr_T_store, k_T_group

        def ev_lat(ti, t, pp, r_pp):
            np_, k0 = t["np"], t["tok0"]
            ss, r_ss = W["ss"].next(); sv, r_sv = W["sv"].next(); rl, r_rl = W["rs"].next()
            j16, r_j16 = W["ob16"].next()
            o32, r_o32 = W["of32"].next()
            P.add("act", lambda e: e.activation(out=j16[:np_], in_=pp[:np_, :], func=AF.Square, accum_out=ss[:np_]), reads=[r_pp], writes=[r_j16, r_ss])
            self.rsqrt_chain(ss, np_, 1.0 / 512, sv, rl, r_ss, r_sv, r_rl)
            P.add("dve", lambda e: e.scalar_tensor_tensor(out=o32[:np_], in0=pp[:np_, :], scalar=rl[:np_], in1=gkv[:np_], op0=ALU.mult, op1=ALU.mult),
                  reads=[r_pp, r_rl, r_gkv], writes=[r_o32])
            self.dma("sp", outs["lat"][k0:k0 + np_, :], o32[:np_], reads=[r_o32])
            P.add("act", lambda e: e.activation(out=j16[:np_], in_=o32[:np_], func=AF.Copy), reads=[r_o32], writes=[r_j16])
            self.dma("sp", scr["lat"][k0:k0 + np_, :], j16[:np_], reads=[r_j16])
            lat_T_store(o32, r_o32, np_, k0)

        units.append(dict(blocks=[(c.QL, 512)], evac=ev_lat))

        def ev_kr(ti, t, pp, r_pp):
            np_, k0 = t["np"], t["tok0"]
            rt, r_rt = W["rt"].next()
            t1, r_t1 = W["kr1"].next(); t2, r_t2 = W["kr2"].next()
            self.dma("sp", rt[:np_], t["rope"], writes=[r_rt])
            P.add("dve", lambda e: e.tensor_tensor(out=t1[:np_], in0=pp[:np_, 0:64], in1=rt[:np_, 0:64], op=ALU.mult), reads=[r_pp, r_rt], writes=[r_t1])
            P.add("dve", lambda e: e.tensor_tensor(out=t2[:np_, 0:32], in0=pp[:np_, 32:64], in1=rt[:np_, 64:96], op=ALU.mult), reads=[r_pp, r_rt], writes=[r_t2])
            P.add("dve", lambda e: e.tensor_tensor(out=t2[:np_, 32:64], in0=pp[:np_, 0:32], in1=rt[:np_, 96:128], op=ALU.mult), reads=[r_pp, r_rt], writes=[r_t2])
            P.add("dve", lambda e: e.tensor_tensor(out=t1[:np_], in0=t1[:np_], in1=t2[:np_], op=ALU.add), reads=[r_t1, r_t2], writes=[r_t1])
            self.dma("sp", outs["kr"][k0:k0 + np_, :], t1[:np_], reads=[r_t1])
            kr_T_store(t1, r_t1, np_, k0)

        units.append(dict(blocks=[(c.QL + 512, 64)], evac=ev_kr))

        kstate = {}

        def mk_sbk(u):
            def ev(ti, t, pp, r_pp):
                np_, k0 = t["np"], t["tok0"]
                o32, r_o32 = W["of32"].next()
                P.add("act", lambda e: e.activation(out=o32[:np_], in_=pp[:np_, :], func=AF.Copy), reads=[r_pp], writes=[r_o32])
                self.dma("sp", outs["sbk"][k0:k0 + np_, u * 512:(u + 1) * 512], o32[:np_], reads=[r_o32])
                k_T_group(o32, r_o32, np_, 4 * u, k0)
            return ev

        def mk_sbv(u):
            def ev(ti, t, pp, r_pp):
                np_, k0 = t["np"], t["tok0"]
                o32, r_o32 = W["of32"].next()
                b16, r_b16 = W["ob16"].next()
                P.add("act", lambda e: e.activation(out=o32[:np_], in_=pp[:np_, :], func=AF.Copy), reads=[r_pp], writes=[r_o32])
                self.dma("sp", outs["sbv"][k0:k0 + np_, u * 512:(u + 1) * 512], o32[:np_], reads=[r_o32])
                P.add("dve", lambda e: e.tensor_copy(out=b16[:np_], in_=pp[:np_, :]), reads=[r_pp], writes=[r_b16])
                self.dma("sp", scr["v"][k0:k0 + np_, u * 512:(u + 1) * 512], b16[:np_], reads=[r_b16])
            return ev

        for u in range(H // 4):
            base = c.MLA_IN + c.HW + u * 512
            units.append(dict(blocks=[(base, 512)], evac=mk_sbk(u)))
        for u in range(H // 4):
            base = c.MLA_IN + 2 * c.HW + u * 512
            units.append(dict(blocks=[(base, 512)], evac=mk_sbv(u)))
        return units

    def ph_a1(self):
        c, P, nc = self.c, self.P, self.nc
        P.begin()
        with contextlib.ExitStack() as st:
            ntb = c.TB // 128
            W = {}
            W["xt"] = Rot([self.sb(st, "xt", [128, c.D], F32) for _ in range(2)])
            W["xn"] = Rot([self.sb(st, "xn", [128, c.D], BF16) for _ in range(2)])
            for k in ("ss", "sv", "rs"):
                W[k] = Rot([self.sb(st, k, [128, 1], F32) for _ in range(4)])
            W["ptr"] = Rot([self.ps(st, "ptr", [128, 8, 128], BF16) for _ in range(2)])
            W["ptf"] = Rot([self.ps(st, "ptf", [128, 4, 128], F32) for _ in range(2)])
            W["of32"] = Rot([self.sb(st, "of32", [128, 512], F32) for _ in range(3)])
            W["ob16"] = Rot([self.sb(st, "ob16", [128, 512], BF16) for _ in range(3)])
            W["latst"] = Rot([self.sb(st, "latst", [128, 4, 128], BF16) for _ in range(2)])
            W["krst"] = Rot([self.sb(st, "krst", [64, 128], BF16) for _ in range(2)])
            W["kst"] = Rot([self.sb(st, "kst", [128, 4, 128], BF16) for _ in range(3)])
            W["rt"] = Rot([self.sb(st, "rt", [128, 128], F32) for _ in range(2)])
            W["kr1"] = Rot([self.sb(st, "kr1", [128, 64], F32) for _ in range(2)])
            W["kr2"] = Rot([self.sb(st, "kr2", [128, 64], F32) for _ in range(2)])
            hT = self.sb(st, "hT", [128, c.KC, c.TB], BF16)
            wrot = Rot([self.sb(st, "wA", [128, c.KC, 512], BF16) for _ in range(2)])
            prot = Rot([self.ps(st, "pA", [128, 512], F32) for _ in range(3)])
            gkv = self.sb(st, "gkv", [128, 512], F32)
            r_gkv = Res()
            self.dma("sp", gkv[:], self.din["g_kv"].partition_broadcast(128), writes=[r_gkv])
            mv, r_mod = self.load_modvecs(st, [0, 1], Rot([self.ps(st, "pmv", [128, 128], F32)]))
            GT, SHT = self.make_GS(st, mv, r_mod, "g_mixT", 1, 0)
            P.mark("after_GS")
            sS = self.scr["S"]
            self.kv_evacs(st, W, None, sS, None, None)
            self.dma("pool", sS["lat"][0:c.PAST, :], self.din["c_lat"][:, :])
            self.dma("pool", sS["v"][0:c.PAST, :], self.din["c_v"][:, :])
            for ct in range(c.PAST // 128):
                k0 = ct * 128
                o32, r_o32 = W["of32"].next()
                self.dma("sp", o32[:], self.din["c_lat"][k0:k0 + 128, :], writes=[r_o32])
                self.lat_T_store(o32, r_o32, 128, k0)
                t1, r_t1 = W["kr1"].next()
                self.dma("sp", t1[:], self.din["c_kr"][k0:k0 + 128, :], writes=[r_t1])
                self.kr_T_store(t1, r_t1, 128, k0)
                for u in range(c.H // 4):
                    ck, r_ck = W["of32"].next()
                    self.dma("sp", ck[:], self.din["c_k"][k0:k0 + 128, u * 512:(u + 1) * 512], writes=[r_ck])
                    self.k_T_group(ck, r_ck, 128, 4 * u, k0)
            streams = []
            for tb in range(c.SEQ // c.TB):
                tiles = []
                for k in range(ntb):
                    tok0 = tb * c.TB + k * 128
                    tiles.append(dict(np=128, col0=k * 128, tok0=tok0, x=self.din["xp"][tok0:tok0 + 128, :], rope=self.din["ropeKp"][tok0:tok0 + 128, :]))
                streams.append((0, "P", tiles, dict(lat=self.dout["p_lat"], kr=self.dout["p_kr"], sbk=self.dout["p_sbk"], sbv=self.dout["p_sbv"])))
            streams.append((1, "S", [dict(np=c.DS, col0=0, tok0=c.PAST, x=self.din["xs"][:, :], rope=self.din["ropeKs"][:, :], otok0=0)],
                            dict(lat=self.dout["s_lat"], kr=self.dout["s_kr"], sbk=self.dout["s_sbk"], sbv=self.dout["s_sbv"])))
            RH = [Res() for _ in range(ntb)]
            r0, _, tiles0, _ = streams[0]
            for ti, t in enumerate(tiles0):
                self.prep_hT(t["x"], t["np"], GT[r0], SHT[r0], r_mod, hT, t["col0"], RH[ti], W)
            for si, (r, sname, tiles, outs) in enumerate(streams):
                nxt = streams[si + 1] if si + 1 < len(streams) else None

                pend = {}

                def after_tile(ti, nxt=nxt, pend=pend):
                    if nxt is None:
                        return
                    r2, _, tiles2, _ = nxt
                    if ti == 0 and len(tiles2) > 0:
                        pend[0] = self.prep_s1(tiles2[0]["x"], tiles2[0]["np"], W)
                    if ti < len(tiles2):
                        t2 = tiles2[ti]
                        if ti + 1 < len(tiles2):
                            pend[ti + 1] = self.prep_s1(tiles2[ti + 1]["x"], tiles2[ti + 1]["np"], W)
                        self.prep_s2(pend.pop(ti), GT[r2], SHT[r2], r_mod, hT, t2["col0"], RH[ti], W)
                if sname == "S":
                    outs = {k: _Shift(v, -c.PAST) for k, v in outs.items()}
                units = self.kv_evacs(st, W, outs, self.scr[sname], gkv, r_gkv)
                self.proj_pass(st, tiles, units, self.din["w_in"], hT, RH[:len(tiles)], wrot, prot, after_tile=after_tile)
            P.end()

    def ph_a2(self):
        c, P, nc = self.c, self.P, self.nc
        H, QL = c.H, c.QL
        RC = QL // 128
        with contextlib.ExitStack() as st0:
            qlatT = self.sb(st0, "qlatT", [128, RC, c.NTA], BF16)
            r_qlatT = Res()
            P.begin()
            with contextlib.ExitStack() as st:
                GMAX = c.GMAX
                W = {}
                W["xt"] = Rot([self.sb(st, "xt", [128, c.D], F32) for _ in range(1)])
                W["xn"] = Rot([self.sb(st, "xn", [128, c.D], BF16) for _ in range(2)])
                for k in ("ss", "sv", "rs"):
                    W[k] = Rot([self.sb(st, k, [128, 1], F32) for _ in range(4)])
                W["ptr"] = Rot([self.ps(st, "ptr", [128, 8, 128], BF16) for _ in range(2)])
                hT = self.sb(st, "hT2", [128, c.KC, GMAX * 128], BF16)
                wrot = Rot([self.sb(st, "wA", [128, c.KC, 512], BF16) for _ in range(2)])
                prot = Rot([self.ps(st, "pA", [128, 512], F32) for _ in range(2)])
                mv, r_mod = self.load_modvecs(st, [0, 1], Rot([self.ps(st, "pmv", [128, 128], F32)]))
                GT, SHT = self.make_GS(st, mv, r_mod, "g_mixT", 1, 0)
                gq = self.sb(st, "gq", [128, QL], F32)
                r_gq = Res()
                self.dma("sp", gq[:], self.din["g_q"].partition_broadcast(128), writes=[r_gq])
                alltiles = []
                for s in range(c.NS):
                    alltiles.append(dict(np=128, gcol=s * 128, slot=s, r=0, x=self.din["xq"][s * 128:(s + 1) * 128, :]))
                alltiles.append(dict(np=c.DS, gcol=c.NTQ, slot=c.NS, r=1, x=self.din["xs"][:, :]))
                groups = []
                k = 0
                while k < len(alltiles):
                    n = (len(alltiles) - k) if len(alltiles) - k <= GMAX else GMAX - 1
                    groups.append(alltiles[k:k + n])
                    k += n
                qlf = [self.sb(st, "qlf", [128, QL], F32) for _ in range(GMAX)]
                ssq = [self.sb(st, "ssq", [128, 4], F32) for _ in range(GMAX)]
                junk = Rot([self.sb(st, "junk", [128, 512], BF16) for _ in range(2)])
                qn = Rot([self.sb(st, "qn", [128, QL], BF16) for _ in range(2)])
                b16 = Rot([self.sb(st, "b16", [128, 512], BF16) for _ in range(2)])
                sbqst = Rot([self.sb(st, "sbqst", [128, 4, 128], BF16) for _ in range(3)])
                nqu = QL // 512
                RH = [Res() for _ in range(GMAX)]
                r_qlf = [Res() for _ in range(GMAX)]
                r_ssq = [Res() for _ in range(GMAX)]
                gtiles = [[dict(t, col0=i * 128) for i, t in enumerate(g)] for g in groups]
                for ti, t in enumerate(gtiles[0]):
                    self.prep_hT(t["x"], t["np"], GT[t["r"]], SHT[t["r"]], r_mod, hT, t["col0"], RH[ti], W)
                for gi, tiles in enumerate(gtiles):
                    nxt = gtiles[gi + 1] if gi + 1 < len(gtiles) else None
                    r_hT = RH[:len(tiles)]

                    pend = {}

                    def after_tile(ti, nxt=nxt, pend=pend, ntl=len(tiles)):
                        if nxt is None:
                            return
                        if ti == 0:
                            pend[0] = self.prep_s1(nxt[0]["x"], nxt[0]["np"], W)
                        if ti < len(nxt):
                            t2 = nxt[ti]
                            if ti + 1 < len(nxt):
                                pend[ti + 1] = self.prep_s1(nxt[ti + 1]["x"], nxt[ti + 1]["np"], W)
                            self.prep_s2(pend.pop(ti), GT[t2["r"]], SHT[t2["r"]], r_mod, hT, t2["col0"], RH[ti], W)
                        if ti == ntl - 1:
                            for tj in range(ntl, len(nxt)):
                                t2 = nxt[tj]
                                if tj + 1 < len(nxt):
                                    pend[tj + 1] = self.prep_s1(nxt[tj + 1]["x"], nxt[tj + 1]["np"], W)
                                self.prep_s2(pend.pop(tj), GT[t2["r"]], SHT[t2["r"]], r_mod, hT, t2["col0"], RH[tj], W)
                    units = []

                    def mk_q(u, r_qlf=r_qlf, r_ssq=r_ssq):
                        def ev(ti, t, pp, r_pp):
                            np_ = t["np"]
                            j, r_j = junk.next()
                            P.add("act", lambda e: e.activation(out=qlf[ti][:np_, u * 512:(u + 1) * 512], in_=pp[:np_, :], func=AF.Copy), reads=[r_pp], writes=[r_qlf[ti]])
                            P.add("act", lambda e: e.activation(out=j[:np_], in_=pp[:np_, :], func=AF.Square, accum_out=ssq[ti][:np_, u:u + 1]), reads=[r_pp], writes=[r_j, r_ssq[ti]])
                            if u == nqu - 1:
                                ss, r_ss = W["ss"].next(); sv, r_sv = W["sv"].next(); rq, r_rq = W["rs"].next()
                                P.add("dve", lambda e: e.tensor_reduce(out=ss[:np_], in_=ssq[ti][:np_, 0:nqu], axis=AX.X, op=ALU.add), reads=[r_ssq[ti]], writes=[r_ss])
                                self.rsqrt_chain(ss, np_, 1.0 / QL, sv, rq, r_ss, r_sv, r_rq)
                                q, r_q = qn.next()
                                P.add("dve", lambda e: e.scalar_tensor_tensor(out=q[:np_], in0=qlf[ti][:np_], scalar=rq[:np_], in1=gq[:np_], op0=ALU.mult, op1=ALU.mult),
                                      reads=[r_qlf[ti], r_rq, r_gq], writes=[r_q])
                                gcol = t["gcol"]

                                def evq(k0, n, pt, r_pt):
                                    P.add("act", lambda e: e.activation(out=qlatT[:, k0:k0 + n, gcol:gcol + np_], in_=pt[:, 0:n, :np_], func=AF.Copy), reads=[r_pt], writes=[r_qlatT])
                                self.transposes_evac(q, np_, RC, W["ptr"], evq, r_q)
                        return ev

                    def mk_sq(u):
                        def ev(ti, t, pp, r_pp):
                            np_ = t["np"]
                            b, r_b = b16.next()
                            P.add("dve", lambda e: e.tensor_copy(out=b[:np_], in_=pp[:np_, :]), reads=[r_pp], writes=[r_b])
                            sq, r_sq = sbqst.next()

                            def evs(k0, n, pt, r_pt):
                                P.add("act", lambda e: e.activation(out=sq[:, 0:4, :np_], in_=pt[:, 0:4, :np_], func=AF.Copy), reads=[r_pt], writes=[r_sq])
                            self.transposes_evac(b, np_, 4, W["ptr"], evs, r_b, grp=4)
                            self.dma("sp", self.sbq[t["slot"], :, 4 * u:4 * u + 4, 0:np_], sq[:, :, :np_], reads=[r_sq])
                        return ev

                    for u in range(nqu):
                        units.append(dict(blocks=[(u * 512, 512)], evac=mk_q(u)))
                    for u in range(H // 4):
                        base = c.MLA_IN + u * 512
                        units.append(dict(blocks=[(base, 512)], evac=mk_sq(u)))
                    self.proj_pass(st, tiles, units, self.din["w_in"], hT, r_hT, wrot, prot, after_tile=after_tile)
                P.end()
            P.begin()
            r_qlatT = Res()
            with contextlib.ExitStack() as st:
                wuq = self.sb(st, "wuq", [128, RC, H * 192], BF16)
                wuqs = self.sb(st, "wuqs", [128, RC, H * 64], BF16)
                wuk = self.sb(st, "wuk", [128, H, 512], BF16)
                cs = self.sb(st, "cs", [64, 2, c.NTA], F32)
                r_w = Res()
                self.dma("pool", wuq[:], self.din["w_uq"].rearrange("(k p) n -> p k n", p=128), writes=[r_w])
                self.dma("pool", wuqs[:], self.din["w_uqs"].rearrange("(k p) n -> p k n", p=128), writes=[r_w])
                self.dma("pool", wuk[:], self.din["w_ukT"].rearrange("h n c -> n h c"), writes=[r_w])
                self.dma("sp", cs[:], self.din["ropeQ"][:, :, :], writes=[r_w])
                tgs = []
                s0 = 0
                while s0 < c.NS:
                    ns = min(4, c.NS - s0)
                    tgs.append((s0 * 128, ns * 128, s0, ns))
                    s0 += ns
                tgs.append((c.NTQ, c.DS, c.NS, 0))
                pq = Rot([self.ps(st, "pq", [128, 512], F32) for _ in range(6)])
                qnope = Rot([self.sb(st, "qnope", [128, 512], BF16) for _ in range(2)])
                qt1 = Rot([self.sb(st, "qt1", [64, 512], F32) for _ in range(2)])
                qt2 = Rot([self.sb(st, "qt2", [64, 512], F32) for _ in range(2)])
                qrst = Rot([self.sb(st, "qrst", [64, 512], BF16) for _ in range(2)])
                qast = Rot([self.sb(st, "qast", [128, 4, 512], BF16) for _ in range(2)])
                for h in range(H):
                    for (col0, n, s0, ns) in tgs:
                        pn, r_pn = pq.next()
                        for rc in range(RC):
                            P.add("pe", lambda e, rc=rc, pn=pn, h=h, col0=col0, n=n: e.matmul(pn[:, :n], lhsT=wuq[:, rc, h * 192:h * 192 + 128], rhs=qlatT[:, rc, col0:col0 + n], start=(rc == 0), stop=(rc == RC - 1)),
                                  reads=[r_w, r_qlatT], writes=[r_pn])
                        qp, r_qp = qnope.next()
                        P.add("act", lambda e, qp=qp, pn=pn, n=n: e.activation(out=qp[:, :n], in_=pn[:, :n], func=AF.Copy), reads=[r_pn], writes=[r_qp])
                        pr, r_pr = pq.next()
                        for rc in range(RC):
                            P.add("pe", lambda e, rc=rc, pr=pr, h=h, col0=col0, n=n: e.matmul(pr[0:64, :n], lhsT=wuq[:, rc, h * 192 + 128:h * 192 + 192], rhs=qlatT[:, rc, col0:col0 + n], start=(rc == 0), stop=(rc == RC - 1)),
                                  reads=[r_w, r_qlatT], writes=[r_pr])
                        pz, r_pz = pq.next()
                        for rc in range(RC):
                            P.add("pe", lambda e, rc=rc, pz=pz, h=h, col0=col0, n=n: e.matmul(pz[0:64, :n], lhsT=wuqs[:, rc, h * 64:(h + 1) * 64], rhs=qlatT[:, rc, col0:col0 + n], start=(rc == 0), stop=(rc == RC - 1)),
                                  reads=[r_w, r_qlatT], writes=[r_pz])
                        a1, r_a1 = qt1.next(); a2, r_a2 = qt2.next(); qr, r_qr = qrst.next()
                        P.add("dve", lambda e, a1=a1, pr=pr, col0=col0, n=n: e.tensor_tensor(out=a1[:, :n], in0=pr[0:64, :n], in1=cs[:, 0, col0:col0 + n], op=ALU.mult), reads=[r_pr, r_w], writes=[r_a1])
                        P.add("dve", lambda e, a2=a2, pz=pz, col0=col0, n=n: e.tensor_tensor(out=a2[:, :n], in0=pz[0:64, :n], in1=cs[:, 1, col0:col0 + n], op=ALU.mult), reads=[r_pz, r_w], writes=[r_a2])
                        P.add("dve", lambda e, a1=a1, a2=a2, qr=qr, n=n: e.tensor_tensor(out=qr[:, :n], in0=a1[:, :n], in1=a2[:, :n], op=ALU.add), reads=[r_a1, r_a2], writes=[r_qr])
                        qa, r_qa = qast.next()
                        for cc in range(4):
                            pa, r_pa = pq.next()
                            P.add("pe", lambda e, cc=cc, pa=pa, qp=qp, h=h, n=n: e.matmul(pa[:, :n], lhsT=wuk[:, h, cc * 128:(cc + 1) * 128], rhs=qp[:, :n], start=True, stop=True),
                                  reads=[r_w, r_qp], writes=[r_pa])
                            if cc % 2 == 0:
                                P.add("act", lambda e, cc=cc, pa=pa, qa=qa, n=n: e.activation(out=qa[:, cc, :n], in_=pa[:, :n], func=AF.Copy), reads=[r_pa], writes=[r_qa])
                            else:
                                P.add("dve", lambda e, cc=cc, pa=pa, qa=qa, n=n: e.tensor_copy(out=qa[:, cc, :n], in_=pa[:, :n]), reads=[r_pa], writes=[r_qa])
                        if ns > 0:
                            self.dma("sp", self.qrope[s0:s0 + ns, :, h, :].rearrange("s p q -> p s q"), qr[:, :n].rearrange("p (s q) -> p s q", q=128), reads=[r_qr])
                            for cc in range(4):
                                self.dma("sp", self.qabs[s0:s0 + ns, :, h, cc, :].rearrange("s p q -> p s q"), qa[:, cc, :n].rearrange("p (s q) -> p s q", q=128), reads=[r_qa])
                        else:
                            self.dma("sp", self.qrope[c.NS, :, h, 0:n], qr[:, :n], reads=[r_qr])
                            self.dma("sp", self.qabs[c.NS, :, h, :, 0:n], qa[:, :, :n], reads=[r_qa])
                P.end()

    def slots(self):
        c = self.c
        out = []
        for i in range(c.NS):
            blocks = [dict(k0=kb * 512, nk=512, mask=(kb == i)) for kb in range(i + 1)]
            out.append(dict(np=128, slot=i, stream="P", nkeys=(i + 1) * 512, blocks=blocks, col0=i * 128))
        blocks = [dict(k0=kb * 512, nk=min(512, c.PAST - kb * 512), mask=False) for kb in range((c.PAST + 511) // 512)]
        blocks.append(dict(k0=c.PAST, nk=c.DS, mask=True))
        out.append(dict(np=c.DS, slot=c.NS, stream="S", nkeys=c.KS, blocks=blocks, col0=c.NTQ))
        return out

    def merged_store(self, st, W, o32, r_o32, np_, kc0, col0, gT, r_g):
        c, P = self.c, self.P
        nch = c.HW // 128
        ss, r_ss = W["ss"].next(); sv, r_sv = W["sv"].next(); rm, r_rm = W["rs"].next()
        on, r_on = W["on16"].next()
        P.add("act", lambda e: e.activation(out=on[:np_], in_=o32[:np_], func=AF.Square, accum_out=ss[:np_]), reads=[r_o32], writes=[r_on, r_ss])
        self.rsqrt_chain(ss, np_, 1.0 / c.HW, sv, rm, r_ss, r_sv, r_rm)
        P.add("dve", lambda e: e.tensor_scalar(out=on[:np_], in0=o32[:np_], scalar1=rm[:np_], scalar2=None, op0=ALU.mult), reads=[r_o32, r_rm], writes=[r_on])
        mst, r_mst = W["mst"].next()

        def ev(k0, n, pt, r_pt):
            for j in range(n):
                k = k0 + j
                P.add("act", lambda e, k=k, j=j: e.activation(out=mst[:, k, :np_], in_=pt[:, j, :np_], func=AF.Identity, scale=gT[:, kc0 + k:kc0 + k + 1]),
                      reads=[r_pt, r_g], writes=[r_mst])
        self.transposes_evac(on, np_, nch, W["ptr"], ev, r_on)
        self.dma("sp", self.mT[:, kc0:kc0 + nch, col0:col0 + np_], mst[:, :, :np_], reads=[r_mst])

    @staticmethod
    def pipeline(units, stages):
        n = len(units)
        maxd = max(d for d, _ in stages)
        for step in range(n + maxd):
            for d, fn in stages:
                u = step - d
                if 0 <= u < n:
                    fn(units[u])

    def ph_mla(self):
        c, P, nc = self.c, self.P, self.nc
        H = c.H
        P.begin()
        with contextlib.ExitStack() as st:
            KT = (c.KMAX + 127) // 128
            lat = self.sb(st, "lat", [128, KT, 512], BF16)
            latT = self.sb(st, "latT", [128, 4, c.KMAX], BF16)
            krT = self.sb(st, "krT", [64, c.KMAX], BF16)
            r_K = Res()
            wuv = self.sb(st, "wuv", [128, 4, c.HW], BF16)
            r_wuv = Res()
            self.dma("pool", wuv[:], self.din["w_uv"].rearrange("(k p) n -> p k n", p=128), writes=[r_wuv])
            gT = self.sb(st, "goT", [128, c.KC], F32)
            r_g = Res()
            self.dma("sp", gT[:], self.din["g_outT"][:, :], writes=[r_g])
            mb = self.sb(st, "mb", [128, 512], F32)
            r_mb = Res()
            self.dma("sp", mb[:], self.din["m_mla"][:, :], writes=[r_mb])
            qa = Rot([self.sb(st, "qa", [128, H, 4, 128], BF16) for _ in range(2)])
            qr = Rot([self.sb(st, "qr", [64, H, 128], BF16) for _ in range(2)])
            W = {}
            for k in ("ss", "sv", "rs"):
                W[k] = Rot([self.sb(st, k, [128, 1], F32) for _ in range(4)])
            W["ptr"] = Rot([self.ps(st, "ptr", [128, 8, 128], BF16) for _ in range(2)])
            W["on16"] = Rot([self.sb(st, "on16", [128, c.HW], BF16)])
            W["mst"] = Rot([self.sb(st, "mst", [128, c.HW // 128, 128], BF16) for _ in range(2)])
            pS = Rot([self.ps(st, "pS", [128, 512], F32) for _ in range(3)])
            pO = Rot([self.ps(st, "pO", [128, 512], F32) for _ in range(2)])
            pV = Rot([self.ps(st, "pV", [128, 512], F32)])
            sm = Rot([self.sb(st, "sm", [128, 512], F32) for _ in range(2)])
            pexp = Rot([self.sb(st, "pexp", [128, 512], BF16) for _ in range(4)])
            pT = Rot([self.sb(st, "pT", [128, 4, 128], BF16) for _ in range(3)])
            rsum = Rot([self.sb(st, "rsum", [128, 16], F32) for _ in range(7)])
            for bb_, rr_ in zip(rsum.bufs, rsum.res):
                P.add("dve", lambda e, bb_=bb_: e.memset(bb_[:], 0.0), writes=[rr_])
            den = Rot([self.sb(st, "den", [128, 1], F32) for _ in range(2)])
            rden = Rot([self.sb(st, "rden", [128, 1], F32) for _ in range(2)])
            oln = Rot([self.sb(st, "oln", [128, 512], BF16) for _ in range(2)])
            olT = Rot([self.sb(st, "olT", [128, 4, 128], BF16) for _ in range(2)])
            o32 = Rot([self.sb(st, "o32", [128, c.HW], F32) for _ in range(2)])
            cur_stream = None
            for sl in self.slots():
                np_ = sl["np"]
                if sl["stream"] != cur_stream:
                    cur_stream = sl["stream"]
                    sc = self.scr[cur_stream]
                    nk = sc["nk"]
                    nfull = nk // 128
                    self.dma("sp", lat[:, 0:nfull, :], sc["lat"][0:nfull * 128, :].rearrange("(t p) c -> p t c", p=128), writes=[r_K])
                    if nk % 128:
                        self.dma("sp", lat[0:nk % 128, nfull, :], sc["lat"][nfull * 128:nk, :], writes=[r_K])
                    self.dma("sp", latT[:, :, 0:nk], sc["latT"][:, :, :], writes=[r_K])
                    self.dma("sp", krT[:, 0:nk], sc["krT"][:, :], writes=[r_K])
                qa_t, r_qa = qa.next()
                qr_t, r_qr = qr.next()
                self.dma("sp", qa_t[:, :, :, 0:np_], self.qabs[sl["slot"], :, :, :, 0:np_], writes=[r_qa])
                self.dma("sp", qr_t[:, :, 0:np_], self.qrope[sl["slot"], :, :, 0:np_], writes=[r_qr])
                o_t, r_o = o32.next()
                nb = len(sl["blocks"])
                is_p = (cur_stream == "P")
                units = []
                for h in range(H):
                    hd = dict(h=h)
                    for bi, blk in enumerate(sl["blocks"]):
                        units.append(dict(h=h, hd=hd, bi=bi, k0=blk["k0"], nk=blk["nk"], mask=(blk["mask"] and is_p), first=(bi == 0), last=(bi == nb - 1)))

                def stA(u):
                    h, k0, nk, bi = u["h"], u["k0"], u["nk"], u["bi"]
                    if u["first"]:
                        u["hd"]["O"] = pO.next()
                        u["hd"]["rs"] = rsum.next()
                    rs_t, r_rs = u["hd"]["rs"]
                    S, r_S = pS.next()
                    for cc in range(4):
                        P.add("pe", lambda e, cc=cc: e.matmul(S[:np_, :nk], lhsT=qa_t[:, h, cc, :np_], rhs=latT[:, cc, k0:k0 + nk], start=(cc == 0), stop=False),
                              reads=[r_qa, r_K], writes=[r_S])
                    P.add("pe", lambda e: e.matmul(S[:np_, :nk], lhsT=qr_t[0:64, h, :np_], rhs=krT[0:64, k0:k0 + nk], start=False, stop=True),
                          reads=[r_qr, r_K], writes=[r_S])
                    pe_t, r_pe = pexp.next()
                    u["pe"] = (pe_t, r_pe)
                    if u["mask"]:
                        sm_t, r_sm = sm.next()
                        P.add("dve", lambda e: e.scalar_tensor_tensor(out=sm_t[:np_, :nk], in0=S[:np_, :nk], scalar=c.MLA_SCALE, in1=mb[:np_, :nk], op0=ALU.mult, op1=ALU.add),
                              reads=[r_S, r_mb], writes=[r_sm])
                        P.add("act", lambda e: e.activation(out=pe_t[:np_, :nk], in_=sm_t[:np_, :nk], func=AF.Exp, accum_out=rs_t[:np_, bi:bi + 1]),
                              reads=[r_sm], writes=[r_pe, r_rs])
                    else:
                        P.add("act", lambda e: e.activation(out=pe_t[:np_, :nk], in_=S[:np_, :nk], func=AF.Exp, scale=c.MLA_SCALE, accum_out=rs_t[:np_, bi:bi + 1]),
                              reads=[r_S], writes=[r_pe, r_rs])

                def stB(u):
                    nk = u["nk"]
                    pe_t, r_pe = u["pe"]
                    nsub = (nk + 127) // 128
                    pt, r_pt = W["ptr"].next()
                    for j in range(nsub):
                        nkt = min(128, nk - j * 128)
                        P.add("pe", lambda e, j=j, nkt=nkt: e.transpose(out=pt[:nkt, j, :np_], in_=pe_t[:np_, j * 128:j * 128 + nkt], identity=self.identb[:np_, :np_]),
                              reads=[r_pe, self.r_const], writes=[r_pt])
                    pT_t, r_pT = pT.next()
                    u["pT"] = (pT_t, r_pT)
                    if nk % 128 == 0:
                        P.add("dve", lambda e: e.tensor_copy(out=pT_t[:, 0:nsub, :np_], in_=pt[:, 0:nsub, :np_]), reads=[r_pt], writes=[r_pT])
                    else:
                        for j in range(nsub):
                            nkt = min(128, nk - j * 128)
                            P.add("dve", lambda e, j=j, nkt=nkt: e.tensor_copy(out=pT_t[:nkt, j, :np_], in_=pt[:nkt, j, :np_]), reads=[r_pt], writes=[r_pT])

                def stC(u):
                    nk, k0 = u["nk"], u["k0"]
                    pT_t, r_pT = u["pT"]
                    O, r_O = u["hd"]["O"]
                    nsub = (nk + 127) // 128
                    for j in range(nsub):
                        nkt = min(128, nk - j * 128)
                        P.add("pe", lambda e, j=j, nkt=nkt, st_=(u["first"] and j == 0), sp_=(u["last"] and j == nsub - 1): e.matmul(
                            O[:np_, :], lhsT=pT_t[:nkt, j, :np_], rhs=lat[:nkt, k0 // 128 + j, :], start=st_, stop=sp_),
                            reads=[r_pT, r_K], writes=[r_O])

                def stD(u):
                    if not u["last"]:
                        return
                    O, r_O = u["hd"]["O"]
                    rs_t, r_rs = u["hd"]["rs"]
                    dn, r_dn = den.next()
                    rd, r_rd = rden.next()
                    P.add("dve", lambda e: e.tensor_reduce(out=dn[:np_], in_=rs_t[:np_, 0:nb], axis=AX.X, op=ALU.add), reads=[r_rs], writes=[r_dn])
                    P.add("dve", lambda e: e.reciprocal(out=rd[:np_], in_=dn[:np_]), reads=[r_dn], writes=[r_rd])
                    ol, r_ol = oln.next()
                    P.add("act", lambda e: e.activation(out=ol[:np_], in_=O[:np_, :], func=AF.Identity, scale=rd[:np_]), reads=[r_O, r_rd], writes=[r_ol])
                    u["hd"]["ol"] = (ol, r_ol)

                def stE(u):
                    if not u["last"]:
                        return
                    ol, r_ol = u["hd"]["ol"]
                    oT, r_oT = olT.next()
                    u["hd"]["oT"] = (oT, r_oT)

                    def ev(k0_, n, pt, r_pt):
                        P.add("dve", lambda e: e.tensor_copy(out=oT[:, 0:4, :np_], in_=pt[:, 0:4, :np_]), reads=[r_pt], writes=[r_oT])
                    self.transposes_evac(ol, np_, 4, W["ptr"], ev, r_ol, grp=4)

                def stF(u):
                    if not u["last"]:
                        return
                    h = u["h"]
                    oT, r_oT = u["hd"]["oT"]
                    V, r_V = pV.next()
                    for cc in range(4):
                        P.add("pe", lambda e, cc=cc: e.matmul(V[:np_, 0:128], lhsT=oT[:, cc, :np_], rhs=wuv[:, cc, h * 128:(h + 1) * 128], start=(cc == 0), stop=(cc == 3)),
                              reads=[r_oT, r_wuv], writes=[r_V])
                    P.add("act", lambda e: e.activation(out=o_t[:np_, h * 128:(h + 1) * 128], in_=V[:np_, 0:128], func=AF.Copy), reads=[r_V], writes=[r_o])

                self.pipeline(units, [(0, stA), (1, stB), (2, stC), (3, stD), (4, stE), (5, stF)])
                self.merged_store(st, W, o_t, r_o, np_, 0, sl["col0"], gT, r_g)
            P.end()

    def ph_sb(self):
        c, P, nc = self.c, self.P, self.nc
        H = c.H
        P.begin()
        with contextlib.ExitStack() as st:
            KT = (c.KMAX + 127) // 128
            gT = self.sb(st, "goT", [128, c.KC], F32)
            r_g = Res()
            self.dma("sp", gT[:], self.din["g_outT"][:, :], writes=[r_g])
            mk = self.sb(st, "mk", [128, 512], F32); nmk = self.sb(st, "nmk", [128, 512], F32)
            mks = self.sb(st, "mks", [c.DS, c.DS], F32); nmks = self.sb(st, "nmks", [c.DS, c.DS], F32)
            zeros = self.sb(st, "zeros", [128, 512], F32)
            r_mk = Res()
            self.dma("sp", mk[:], self.din["m_sb"][:, :], writes=[r_mk])
            self.dma("sp", nmk[:], self.din["m_nsb"][:, :], writes=[r_mk])
            self.dma("sp", mks[:], self.din["ms_sb"][:, :], writes=[r_mk])
            self.dma("sp", nmks[:], self.din["ms_nsb"][:, :], writes=[r_mk])
            P.add("dve", lambda e: e.memset(zeros[:], 0.0), writes=[r_mk])
            qs = Rot([self.sb(st, "qs", [128, H, 128], BF16) for _ in range(2)])
            kTh = Rot([self.sb(st, "kTh", [128, c.KMAX], BF16) for _ in range(5)])
            vh = Rot([self.sb(st, "vh", [128, KT, 128], BF16) for _ in range(5)])
            W = {}
            for k in ("ss", "sv", "rs"):
                W[k] = Rot([self.sb(st, k, [128, 1], F32) for _ in range(4)])
            W["ptr"] = Rot([self.ps(st, "ptr", [128, 8, 128], BF16) for _ in range(2)])
            W["on16"] = Rot([self.sb(st, "on16", [128, c.HW], BF16)])
            W["mst"] = Rot([self.sb(st, "mst", [128, c.HW // 128, 128], BF16) for _ in range(2)])
            ND = 6
            pZ = Rot([self.ps(st, "pZ", [128, 512], F32) for _ in range(3)])
            pOS = Rot([self.ps(st, "pOS", [128, 512], F32) for _ in range(2)])
            beta = Rot([self.sb(st, "beta", [128, 512], F32) for _ in range(ND)])
            nu = Rot([self.sb(st, "nu", [128, 512], F32) for _ in range(ND)])
            Pb = Rot([self.sb(st, "Pb", [128, 514], F32) for _ in range(ND)])
            a16 = Rot([self.sb(st, "a16", [128, 512], BF16) for _ in range(ND)])
            aT = Rot([self.sb(st, "aT", [128, 4, 128], BF16) for _ in range(3)])
            o32 = Rot([self.sb(st, "o32", [128, c.HW], F32) for _ in range(2)])
            for sl in self.slots():
                np_ = sl["np"]
                sc = self.scr[sl["stream"]]
                nkeys = sl["nkeys"]
                qs_t, r_qs = qs.next()
                self.dma("sp", qs_t[:, :, 0:np_], self.sbq[sl["slot"], :, :, 0:np_], writes=[r_qs])
                o_t, r_o = o32.next()
                blocks = list(reversed(sl["blocks"]))
                nb = len(blocks)
                m_ap, nm_ap = (mk, nmk) if sl["stream"] == "P" else (mks, nmks)
                units = []
                for h in range(H):
                    hd = dict(h=h, prevPb=None)
                    for bi, blk in enumerate(blocks):
                        units.append(dict(h=h, hd=hd, bi=bi, k0=blk["k0"], nk=blk["nk"], mask=blk["mask"], first=(bi == 0), last=(bi == nb - 1)))

                def stA(u):
                    h, k0, nk = u["h"], u["k0"], u["nk"]
                    hd = u["hd"]
                    if u["first"]:
                        kT_t, r_kT = kTh.next()
                        v_t, r_v = vh.next()
                        hd["kT"] = (kT_t, r_kT); hd["v"] = (v_t, r_v)
                        self.dma("sp", kT_t[:, 0:nkeys], sc["kT"][:, h, 0:nkeys], writes=[r_kT])
                        nfull = nkeys // 128
                        self.dma("sp", v_t[:, 0:nfull, :], sc["v"][0:nfull * 128, h * 128:(h + 1) * 128].rearrange("(t p) d -> p t d", p=128), writes=[r_v])
                        if nkeys % 128:
                            self.dma("sp", v_t[0:nkeys % 128, nfull, :], sc["v"][nfull * 128:nkeys, h * 128:(h + 1) * 128], writes=[r_v])
                        hd["OS"] = pOS.next()
                    kT_t, r_kT = hd["kT"]
                    Z, r_Z = pZ.next()
                    P.add("pe", lambda e: e.matmul(Z[:np_, :nk], lhsT=qs_t[:, h, :np_], rhs=kT_t[:, k0:k0 + nk], start=True, stop=True),
                          reads=[r_qs, r_kT], writes=[r_Z])
                    b_t, r_b = beta.next()
                    n_t, r_n = nu.next()
                    P.add("act", lambda e: e.activation(out=b_t[:np_, :nk], in_=Z[:np_, :nk], func=AF.Sigmoid, scale=c.SB_SCALE), reads=[r_Z], writes=[r_b])
                    P.add("act", lambda e: e.activation(out=n_t[:np_, :nk], in_=Z[:np_, :nk], func=AF.Sigmoid, scale=-c.SB_SCALE), reads=[r_Z], writes=[r_n])
                    if u["mask"]:
                        P.add("dve", lambda e: e.tensor_tensor(out=n_t[:np_, :nk], in0=n_t[:np_, :nk], in1=nm_ap[:np_, :nk], op=ALU.max),
                              reads=[r_n, r_mk], writes=[r_n])
                    pb_t, r_pb = Pb.next()
                    if hd["prevPb"] is None:
                        P.add("dve", lambda e: e.memset(pb_t[:np_, nk:nk + 1], 1.0), writes=[r_pb])
                    else:
                        ppb, r_ppb = hd["prevPb"]
                        P.add("dve", lambda e: e.tensor_copy(out=pb_t[:np_, nk:nk + 1], in_=ppb[:np_, 0:1]), reads=[r_ppb], writes=[r_pb])
                    P.add("dve", lambda e: e.tensor_tensor_scan(
                        out=pb_t[:np_, 0:nk][:, ::-1], data0=n_t[:np_, 0:nk][:, ::-1], data1=zeros[:np_, 0:nk], initial=pb_t[:np_, nk:nk + 1], op0=ALU.mult, op1=ALU.add),
                        reads=[r_n, r_mk, r_pb], writes=[r_pb])
                    hd["prevPb"] = (pb_t, r_pb)
                    a_t, r_a = a16.next()
                    u["a"] = (a_t, r_a)
                    P.add("pool", lambda e: e.tensor_tensor(out=a_t[:np_, :nk], in0=b_t[:np_, :nk], in1=pb_t[:np_, 1:nk + 1], op=ALU.mult),
                          reads=[r_b, r_pb], writes=[r_a])
                    if u["mask"]:
                        P.add("pool", lambda e: e.tensor_tensor(out=a_t[:np_, :nk], in0=a_t[:np_, :nk], in1=m_ap[:np_, :nk], op=ALU.mult),
                              reads=[r_a, r_mk], writes=[r_a])

                def stB(u):
                    nk = u["nk"]
                    a_t, r_a = u["a"]
                    nsub = (nk + 127) // 128
                    pt, r_pt = W["ptr"].next()
                    for j in range(nsub):
                        nkt = min(128, nk - j * 128)
                        P.add("pe", lambda e, j=j, nkt=nkt: e.transpose(out=pt[:nkt, j, :np_], in_=a_t[:np_, j * 128:j * 128 + nkt], identity=self.identb[:np_, :np_]),
                              reads=[r_a, self.r_const], writes=[r_pt])
                    aT_t, r_aT = aT.next()
                    u["aT"] = (aT_t, r_aT)
                    if nk % 128 == 0:
                        P.add("act", lambda e: e.activation(out=aT_t[:, 0:nsub, :np_], in_=pt[:, 0:nsub, :np_], func=AF.Copy), reads=[r_pt], writes=[r_aT])
                    else:
                        for j in range(nsub):
                            nkt = min(128, nk - j * 128)
                            P.add("act", lambda e, j=j, nkt=nkt: e.activation(out=aT_t[:nkt, j, :np_], in_=pt[:nkt, j, :np_], func=AF.Copy), reads=[r_pt], writes=[r_aT])

                def stC(u):
                    nk, k0, h = u["nk"], u["k0"], u["h"]
                    aT_t, r_aT = u["aT"]
                    OS, r_OS = u["hd"]["OS"]
                    v_t, r_v = u["hd"]["v"]
                    nsub = (nk + 127) // 128
                    for j in range(nsub):
                        nkt = min(128, nk - j * 128)
                        P.add("pe", lambda e, j=j, nkt=nkt, st_=(u["first"] and j == 0), sp_=(u["last"] and j == nsub - 1): e.matmul(
                            OS[:np_, 0:128], lhsT=aT_t[:nkt, j, :np_], rhs=v_t[:nkt, k0 // 128 + j, :], start=st_, stop=sp_),
                            reads=[r_aT, r_v], writes=[r_OS])
                    if u["last"]:
                        P.add("act", lambda e: e.activation(out=o_t[:np_, h * 128:(h + 1) * 128], in_=OS[:np_, 0:128], func=AF.Copy), reads=[r_OS], writes=[r_o])

                self.pipeline(units, [(0, stA), (3, stB), (4, stC)])
                self.merged_store(st, W, o_t, r_o, np_, c.HW // 128, sl["col0"], gT, r_g)
            P.end()

    def own_tiles(self):
        c = self.c
        tiles = []
        for s in range(c.NS):
            tiles.append(dict(np=128, col0=s * 128, r=0, x=self.din["xq"][s * 128:(s + 1) * 128, :], idx=s))
        tiles.append(dict(np=c.DS, col0=c.NTQ, r=1, x=self.din["xs"][:, :], idx=c.NS))
        return tiles

    def ph_wout(self):
        c, P, nc = self.c, self.P, self.nc
        P.begin()
        with contextlib.ExitStack() as st:
            mT = self.sb(st, "mT", [128, c.KC, c.NTA], BF16)
            r_mT = Res()
            self.dma("sp", mT[:], self.mT[:, :, :], writes=[r_mT])
            tiles = self.own_tiles()
            r_hT = [r_mT for _ in tiles]
            wrot = Rot([self.sb(st, "wA", [128, c.KC, 512], BF16) for _ in range(2)])
            prot = Rot([self.ps(st, "pA", [128, 512], F32) for _ in range(3)])
            gbc = [Rot([self.sb(st, "gbc", [128, 512], F32) for _ in range(2)]) for _ in range(2)]
            xb = Rot([self.sb(st, "xb", [128, 512], F32) for _ in range(3)])
            tb = Rot([self.sb(st, "tb", [128, 512], F32) for _ in range(3)])
            junk = Rot([self.sb(st, "junk", [128, 512], BF16) for _ in range(2)])
            units = []
            ss2, r_ss2 = self.ss2, self.r_ss2
            nU = c.D // 512

            def mk(u):
                cur = {}

                def ev(ti, t, pp, r_pp):
                    np_, r = t["np"], t["r"]
                    if r not in cur:
                        g, r_g = gbc[r].next()
                        self.dma("sp", g[:], self.modD[r, 2 * c.D + u * 512:2 * c.D + (u + 1) * 512].partition_broadcast(128), writes=[r_g])
                        cur[r] = (g, r_g)
                    g, r_g = cur[r]
                    x_t, r_x = xb.next()
                    t_t, r_t = tb.next()
                    j, r_j = junk.next()
                    self.dma("sp", x_t[:np_], t["x"][:, u * 512:(u + 1) * 512], writes=[r_x])
                    P.add("dve", lambda e: e.tensor_tensor(out=t_t[:np_], in0=pp[:np_, :], in1=g[:np_], op=ALU.mult), reads=[r_pp, r_g], writes=[r_t])
                    P.add("dve", lambda e: e.tensor_tensor(out=t_t[:np_], in0=t_t[:np_], in1=x_t[:np_], op=ALU.add), reads=[r_t, r_x], writes=[r_t])
                    P.add("act", lambda e: e.activation(out=j[:np_], in_=t_t[:np_], func=AF.Square, accum_out=ss2[:np_, t["idx"] * nU + u:t["idx"] * nU + u + 1]),
                          reads=[r_t], writes=[r_j, r_ss2])
                    self.dma("sp", self.x2[t["col0"]:t["col0"] + np_, u * 512:(u + 1) * 512], t_t[:np_], reads=[r_t])
                return ev

            for u in range(nU):
                units.append(dict(blocks=[(u * 512, 512)], evac=mk(u)))
            self.proj_pass(st, tiles, units, self.din["w_out"], mT, r_hT, wrot, prot)
            P.end()

    def ph_ffn(self):
        c, P, nc = self.c, self.P, self.nc
        FC = c.FC
        nU = c.D // 512
        P.begin()
        with contextlib.ExitStack() as st:
            W = {}
            DH = c.D // 2
            W["xt"] = Rot([self.sb(st, "xt", [128, DH], F32)])
            W["xn"] = Rot([self.sb(st, "xn", [128, DH], BF16)])
            for k in ("ss", "sv", "rs"):
                W[k] = Rot([self.sb(st, k, [128, 1], F32) for _ in range(4)])
            W["ptr"] = Rot([self.ps(st, "ptr", [128, 8, 128], BF16) for _ in range(1)])
            mv, r_mod = self.load_modvecs(st, [3, 4], Rot([self.ps(st, "pmv", [128, 128], F32)]))
            GT, SHT = self.make_GS(st, mv, r_mod, "g_ffnT", 4, 3)
            h2T = self.sb(st, "h2T", [128, c.KC, 512 + c.DS], BF16)
            actT = self.sb(st, "actT", [128, FC, 512 + c.DS], BF16)
            wg = Rot([self.sb(st, "wg", [128, c.KC, 128], BF16) for _ in range(2)])
            wu = Rot([self.sb(st, "wu", [128, c.KC, 128], BF16) for _ in range(2)])
            wd = Rot([self.sb(st, "wd", [128, 4, 512], BF16) for _ in range(2)])
            pbank = [self.ps(st, "pb", [128, 512], F32) for _ in range(5)]
            r_pbank = [Res(excl=True) for _ in range(5)]
            sil = Rot([self.sb(st, "sil", [128, 512 + c.DS], F32) for _ in range(2)])
            gbc = [Rot([self.sb(st, "gbc", [128, 512], F32) for _ in range(1)]) for _ in range(2)]
            xb = Rot([self.sb(st, "xb", [128, 512], F32) for _ in range(2)])
            tb = Rot([self.sb(st, "tb", [128, 512], F32) for _ in range(2)])
            junk = Rot([self.sb(st, "junk", [128, 512], BF16) for _ in range(1)])
            ss2, r_ss2, ss3, r_ss3 = self.ss2, self.r_ss2, self.ss3, self.r_ss3
            all_tiles = self.own_tiles()
            groups = []
            s0 = 0
            while s0 < c.NS:
                ns = min(4, c.NS - s0)
                g = [dict(t, lcol=k * 128) for k, t in enumerate(all_tiles[s0:s0 + ns])]
                groups.append(g)
                s0 += ns
            groups[0].append(dict(all_tiles[-1], lcol=512))
            for g in groups:
                r_h2 = Res()
                r_act = Res()
                npr = sum(t["np"] for t in g if t["r"] == 0)
                has_s = any(t["r"] == 1 for t in g)
                for t in g:
                    np_ = t["np"]
                    ss, r_ss = W["ss"].next(); sv, r_sv = W["sv"].next(); rs, r_rs = W["rs"].next()
                    P.add("dve", lambda e, ss=ss, t=t, np_=np_: e.tensor_reduce(out=ss[:np_], in_=ss2[:np_, t["idx"] * nU:(t["idx"] + 1) * nU], axis=AX.X, op=ALU.add), reads=[r_ss2], writes=[r_ss])
                    self.rsqrt_chain(ss, np_, 1.0 / c.D, sv, rs, r_ss, r_sv, r_rs)
                    GTr, SHr = GT[t["r"]], SHT[t["r"]]
                    lcol = t["lcol"]
                    for hf in range(2):
                        xt, r_xt = W["xt"].next(); xn, r_xn = W["xn"].next()
                        self.dma("sp", xt[:np_], self.x2[t["col0"]:t["col0"] + np_, hf * DH:(hf + 1) * DH], writes=[r_xt])
                        P.add("dve", lambda e, xn=xn, xt=xt, rs=rs, np_=np_: e.tensor_scalar(out=xn[:np_], in0=xt[:np_], scalar1=rs[:np_], scalar2=None, op0=ALU.mult), reads=[r_xt, r_rs], writes=[r_xn])
                        kb = hf * (c.KC // 2)

                        def evac(k0, n, pt, r_pt, np_=np_, lcol=lcol, GTr=GTr, SHr=SHr, kb=kb):
                            for j in range(n):
                                kc = kb + k0 + j
                                if j % 2 == 0:
                                    P.add("act", lambda e, kc=kc, j=j: e.activation(out=h2T[:, kc, lcol:lcol + np_], in_=pt[:, j, :np_], func=AF.Identity, scale=GTr[:, kc:kc + 1], bias=SHr[:, kc:kc + 1]),
                                          reads=[r_pt, r_mod], writes=[r_h2])
                                else:
                                    P.add("dve", lambda e, kc=kc, j=j: e.tensor_scalar(out=h2T[:, kc, lcol:lcol + np_], in0=pt[:, j, :np_], scalar1=GTr[:, kc:kc + 1], scalar2=SHr[:, kc:kc + 1], op0=ALU.mult, op1=ALU.add),
                                          reads=[r_pt, r_mod], writes=[r_h2])
                        self.transposes_evac(xn, np_, c.KC // 2, W["ptr"], evac, r_xn)
                for fc in range(FC):
                    wg_t, r_wg = wg.next()
                    wu_t, r_wu = wu.next()
                    self.dma("pool", wg_t[:], self.din["w_gate"][:, fc * 128:(fc + 1) * 128].rearrange("(k p) n -> p k n", p=128), writes=[r_wg])
                    self.dma("pool", wu_t[:], self.din["w_up"][:, fc * 128:(fc + 1) * 128].rearrange("(k p) n -> p k n", p=128), writes=[r_wu])
                    bg, bu = (fc % 2) * 2, (fc % 2) * 2 + 1
                    for (w_t, r_w, bk) in ((wg_t, r_wg, bg), (wu_t, r_wu, bu)):
                        for kc in range(c.KC):
                            P.add("pe", lambda e, kc=kc, w_t=w_t, bk=bk: e.matmul(pbank[bk][:, 0:npr], lhsT=w_t[:, kc, :], rhs=h2T[:, kc, 0:npr], start=(kc == 0), stop=(kc == c.KC - 1)),
                                  reads=[r_w, r_h2], writes=[r_pbank[bk]])
                    if has_s:
                        for (w_t, r_w, o) in ((wg_t, r_wg, 0), (wu_t, r_wu, 64)):
                            for kc in range(c.KC):
                                P.add("pe", lambda e, kc=kc, w_t=w_t, o=o: e.matmul(pbank[4][:, o:o + c.DS], lhsT=w_t[:, kc, :], rhs=h2T[:, kc, 512:512 + c.DS], start=(kc == 0), stop=(kc == c.KC - 1)),
                                      reads=[r_w, r_h2], writes=[r_pbank[4]])
                    s_t, r_s = sil.next()
                    P.add("act", lambda e, s_t=s_t, bg=bg: e.activation(out=s_t[:, 0:npr], in_=pbank[bg][:, 0:npr], func=AF.Silu), reads=[r_pbank[bg]], writes=[r_s])
                    P.add("dve", lambda e, s_t=s_t, bu=bu, fc=fc: e.tensor_tensor(out=actT[:, fc, 0:npr], in0=pbank[bu][:, 0:npr], in1=s_t[:, 0:npr], op=ALU.mult), reads=[r_pbank[bu], r_s], writes=[r_act])
                    if has_s:
                        P.add("act", lambda e, s_t=s_t: e.activation(out=s_t[:, 512:512 + c.DS], in_=pbank[4][:, 0:c.DS], func=AF.Silu), reads=[r_pbank[4]], writes=[r_s])
                        P.add("dve", lambda e, s_t=s_t, fc=fc: e.tensor_tensor(out=actT[:, fc, 512:512 + c.DS], in0=pbank[4][:, 64:64 + c.DS], in1=s_t[:, 512:512 + c.DS], op=ALU.mult),
                              reads=[r_pbank[4], r_s], writes=[r_act])
                for u in range(nU):
                    cur = {}
                    nfg = (FC + 3) // 4
                    for fg in range(nfg):
                        ng = min(4, FC - fg * 4)
                        wd_t, r_wd = wd.next()
                        self.dma("pool", wd_t[:, 0:ng, :], self.din["w_down"][fg * 512:fg * 512 + ng * 128, u * 512:(u + 1) * 512].rearrange("(g p) n -> p g n", p=128), writes=[r_wd])
                        for k, t in enumerate(g):
                            np_, lcol = t["np"], t["lcol"]
                            for gi in range(ng):
                                fc = fg * 4 + gi
                                P.add("pe", lambda e, k=k, gi=gi, fc=fc, np_=np_, lcol=lcol, wd_t=wd_t: e.matmul(
                                    pbank[k][:np_, :], lhsT=actT[:, fc, lcol:lcol + np_], rhs=wd_t[:, gi, :], start=(fc == 0), stop=(fc == FC - 1)),
                                    reads=[r_act, r_wd], writes=[r_pbank[k]])
                    for k, t in enumerate(g):
                        np_, r = t["np"], t["r"]
                        if r not in cur:
                            gb, r_gb = gbc[r].next()
                            self.dma("sp", gb[:], self.modD[r, 5 * c.D + u * 512:5 * c.D + (u + 1) * 512].partition_broadcast(128), writes=[r_gb])
                            cur[r] = (gb, r_gb)
                        gb, r_gb = cur[r]
                        x_t, r_x = xb.next()
                        t_t, r_t = tb.next()
                        j, r_j = junk.next()
                        self.dma("sp", x_t[:np_], self.x2[t["col0"]:t["col0"] + np_, u * 512:(u + 1) * 512], writes=[r_x])
                        P.add("dve", lambda e, k=k, t_t=t_t, gb=gb, np_=np_: e.tensor_tensor(out=t_t[:np_], in0=pbank[k][:np_, :], in1=gb[:np_], op=ALU.mult), reads=[r_pbank[k], r_gb], writes=[r_t])
                        P.add("dve", lambda e, t_t=t_t, x_t=x_t, np_=np_: e.tensor_tensor(out=t_t[:np_], in0=t_t[:np_], in1=x_t[:np_], op=ALU.add), reads=[r_t, r_x], writes=[r_t])
                        P.add("act", lambda e, j=j, t_t=t_t, np_=np_, t=t, u=u: e.activation(out=j[:np_], in_=t_t[:np_], func=AF.Square, accum_out=ss3[:np_, t["idx"] * nU + u:t["idx"] * nU + u + 1]),
                              reads=[r_t], writes=[r_j, r_ss3])
                        self.dma("sp", self.x3[t["col0"]:t["col0"] + np_, u * 512:(u + 1) * 512], t_t[:np_], reads=[r_t])
            P.end()

    def ph_final(self):
        c, P, nc = self.c, self.P, self.nc
        nU = c.D // 512
        P.begin()
        with contextlib.ExitStack() as st:
            gf = self.sb(st, "gf", [128, c.D], F32)
            r_gf = Res()
            self.dma("sp", gf[:], self.din["g_fin"].partition_broadcast(128), writes=[r_gf])
            xt = Rot([self.sb(st, "xt", [128, c.D], F32) for _ in range(2)])
            yt = Rot([self.sb(st, "yt", [128, c.D], F32) for _ in range(2)])
            W = {}
            for k in ("ss", "sv", "rs"):
                W[k] = Rot([self.sb(st, k, [128, 1], F32) for _ in range(4)])
            for t in self.own_tiles():
                np_ = t["np"]
                x_t, r_x = xt.next()
                y_t, r_y = yt.next()
                ss, r_ss = W["ss"].next(); sv, r_sv = W["sv"].next(); rs, r_rs = W["rs"].next()
                self.dma("sp", x_t[:np_], self.x3[t["col0"]:t["col0"] + np_, :], writes=[r_x])
                P.add("dve", lambda e, ss=ss, t=t, np_=np_: e.tensor_reduce(out=ss[:np_], in_=self.ss3[:np_, t["idx"] * nU:(t["idx"] + 1) * nU], axis=AX.X, op=ALU.add), reads=[self.r_ss3], writes=[r_ss])
                self.rsqrt_chain(ss, np_, 1.0 / c.D, sv, rs, r_ss, r_sv, r_rs)
                P.add("dve", lambda e, y_t=y_t, x_t=x_t, rs=rs, np_=np_: e.scalar_tensor_tensor(out=y_t[:np_], in0=x_t[:np_], scalar=rs[:np_], in1=gf[:np_], op0=ALU.mult, op1=ALU.mult),
                      reads=[r_x, r_rs, r_gf], writes=[r_y])
                if t["r"] == 0:
                    self.dma("sp", self.dout["y_p"][t["col0"]:t["col0"] + np_, :], y_t[:np_], reads=[r_y])
                else:
                    self.dma("sp", self.dout["y_s"][:, :], y_t[:np_], reads=[r_y])
            P.end()

    def build(self, phases=None):
        c, nc = self.c, self.nc
        self.declare()
        with contextlib.ExitStack() as st:
            self.P = Prog(nc, st)
            P = self.P
            self.identf = st.enter_context(nc.sbuf_tensor("identf", [128, 128], F32))
            self.identb = st.enter_context(nc.sbuf_tensor("identb", [128, 128], BF16))
            self.nhalf = st.enter_context(nc.sbuf_tensor("nhalf", [128, 1], F32))
            nU = c.D // 512
            self.ss2 = st.enter_context(nc.sbuf_tensor("ss2", [128, (c.NS + 1) * nU], F32))
            self.ss3 = st.enter_context(nc.sbuf_tensor("ss3", [128, (c.NS + 1) * nU], F32))
            self.r_ss2, self.r_ss3 = Res(), Res()
            self.r_const = Res()
            P.begin()
            idf, idb, nh = self.identf, self.identb, self.nhalf
            P.add("pool", lambda e: e.memset(idf[:], 1.0), writes=[self.r_const])
            P.add("pool", lambda e: e.affine_select(out=idf[:], in_=idf[:], pattern=[[-1, 128]], compare_op=ALU.is_equal, fill=0.0, base=0, channel_multiplier=1),
                  reads=[self.r_const], writes=[self.r_const])
            P.add("pool", lambda e: e.tensor_copy(out=idb[:], in_=idf[:]), reads=[self.r_const], writes=[self.r_const])
            P.add("pool", lambda e: e.memset(nh[:], -0.5), writes=[self.r_const])
            P.end()
            allph = [("mod", self.ph_mod), ("a1", self.ph_a1), ("a2", self.ph_a2), ("mla", self.ph_mla), ("sb", self.ph_sb),
                     ("wout", self.ph_wout), ("ffn", self.ph_ffn), ("final", self.ph_final)]
            for name, fn in allph:
                if phases is None or name in phases:
                    fn()
        return nc


class _Shift:
    def __init__(self, ap, shift):
        self.ap, self.shift = ap, shift

    def __getitem__(self, key):
        rows, cols = key
        rows = slice(rows.start + self.shift, rows.stop + self.shift)
        return self.ap[rows, cols]


def rope_tables(pos, dtype=np.float32):
    half = 32
    inv = (1.0 / (np.float32(10000.0) ** (np.arange(half, dtype=np.float32) / np.float32(half)))).astype(np.float32)
    ang = pos.astype(np.float32)[:, None] * inv[None, :]
    return np.cos(ang).astype(dtype), np.sin(ang).astype(dtype)


def prep_inputs(cfg, inp):
    c = cfg
    D, H, KC = c.D, c.H, c.KC
    f = lambda a: np.ascontiguousarray(np.asarray(a, dtype=np.float32))
    shared = {}
    shared["w_ada"] = f(inp["w_ada"][0]); shared["b_ada"] = f(inp["b_ada"][0])
    featT = lambda v: f(np.asarray(v).reshape(KC, 128).T)
    shared["g_mixT"] = featT(inp["g_mix"][0]); shared["g_ffnT"] = featT(inp["g_ffn"][0])
    shared["g_outT"] = featT(np.concatenate([np.asarray(inp["g_out_mla"][0]), np.asarray(inp["g_out_sb"][0])]))
    shared["w_in"] = f(inp["w_in"][0]); shared["g_q"] = f(inp["g_q_lat"][0]); shared["g_kv"] = f(inp["g_kv_lat"][0])
    wuq = np.asarray(inp["w_uq"][0])
    shared["w_uq"] = f(wuq.reshape(c.QL, H * 192))
    sw = np.concatenate([wuq[:, :, 160:192], wuq[:, :, 128:160]], axis=2)
    shared["w_uqs"] = f(sw.reshape(c.QL, H * 64))
    shared["w_ukT"] = f(np.transpose(np.asarray(inp["w_uk"][0]), (1, 2, 0)))
    shared["w_uv"] = f(np.asarray(inp["w_uv"][0]).reshape(512, H * 128))
    shared["w_out"] = f(inp["w_out"][0]); shared["w_gate"] = f(inp["w_gate"][0]); shared["w_up"] = f(inp["w_up"][0])
    shared["w_down"] = f(inp["w_down"][0]); shared["g_fin"] = f(inp["g_final"])
    cosp, sinp = rope_tables(np.arange(c.SEQ))
    shared["ropeKp"] = f(np.concatenate([cosp, cosp, -sinp, sinp], axis=1))
    coss, sins = rope_tables(c.PAST + np.arange(c.DS))
    shared["ropeKs"] = f(np.concatenate([coss, coss, -sins, sins], axis=1))
    qi = np.arange(c.DS)[:, None]; ki = np.arange(c.DS)[None, :]
    ms = (ki < qi).astype(np.float32)
    shared["ms_sb"] = f(ms); shared["ms_nsb"] = f(1.0 - ms)
    xpr = np.asarray(inp["x_prompt"]); xsm = np.asarray(inp["x_sample"])
    cp = np.asarray(inp["c_prompt"]); csm = np.asarray(inp["c_sample"])
    in_maps = []
    for core in range(c.NCORES):
        b, j = core // c.G, core % c.G
        m = dict(shared)
        m["xp"] = f(xpr[b])
        own_tiles = [c.G * i + j for i in range(c.NS)]
        pos_own = np.concatenate([np.arange(t * 128, (t + 1) * 128) for t in own_tiles])
        m["xq"] = f(xpr[b][pos_own])
        m["xs"] = f(xsm[core])
        m["c_lat"] = f(inp["cache_mla_latent"][0][core]); m["c_kr"] = f(inp["cache_mla_krope"][0][core])
        m["c_k"] = f(np.asarray(inp["cache_sb_k"][0][core]).reshape(c.PAST, c.HW))
        m["c_v"] = f(np.asarray(inp["cache_sb_v"][0][core]).reshape(c.PAST, c.HW))
        cc = np.stack([cp[b], csm[core]], axis=0)
        m["cT"] = f(cc.reshape(2, KC, 128).transpose(2, 1, 0).reshape(128, KC * 2))
        pos_all = np.concatenate([pos_own, c.PAST + np.arange(c.DS)])
        cq, sq = rope_tables(pos_all)
        cs1 = np.concatenate([cq, cq], axis=1).T
        cs2 = np.concatenate([-sq, sq], axis=1).T
        m["ropeQ"] = f(np.stack([cs1, cs2], axis=1))
        qpos = 128 * j + np.arange(128)[:, None]
        kk = np.arange(512)[None, :]
        vis_mla = (kk // 64) <= (qpos // 64)
        m["m_mla"] = f(np.where(vis_mla, 0.0, NEG))
        vis_sb = kk < qpos
        m["m_sb"] = f(vis_sb.astype(np.float32)); m["m_nsb"] = f(1.0 - vis_sb.astype(np.float32))
        in_maps.append(m)
    return in_maps


def assemble(cfg, results):
    c = cfg
    D, H = c.D, c.H
    y_p = np.zeros((c.NB, c.SEQ, D), np.float32)
    y_s = np.zeros((c.NCORES, c.DS, D), np.float32)
    p_lat = np.zeros((1, c.NB, c.SEQ, 512), np.float32); p_kr = np.zeros((1, c.NB, c.SEQ, 64), np.float32)
    p_k = np.zeros((1, c.NB, c.SEQ, H, 128), np.float32); p_v = np.zeros((1, c.NB, c.SEQ, H, 128), np.float32)
    s_lat = np.zeros((1, c.NCORES, c.DS, 512), np.float32); s_kr = np.zeros((1, c.NCORES, c.DS, 64), np.float32)
    s_k = np.zeros((1, c.NCORES, c.DS, H, 128), np.float32); s_v = np.zeros((1, c.NCORES, c.DS, H, 128), np.float32)
    for core in range(c.NCORES):
        r = results[core]
        b, j = core // c.G, core % c.G
        for i in range(c.NS):
            t = c.G * i + j
            y_p[b, t * 128:(t + 1) * 128] = r["y_p"][i * 128:(i + 1) * 128]
        y_s[core] = r["y_s"]
        if j == 0:
            p_lat[0, b] = r["p_lat"]; p_kr[0, b] = r["p_kr"]
            p_k[0, b] = r["p_sbk"].reshape(c.SEQ, H, 128); p_v[0, b] = r["p_sbv"].reshape(c.SEQ, H, 128)
        s_lat[0, core] = r["s_lat"]; s_kr[0, core] = r["s_kr"]
        s_k[0, core] = r["s_sbk"].reshape(c.DS, H, 128); s_v[0, core] = r["s_sbv"].reshape(c.DS, H, 128)
    return (y_p, y_s, p_lat, p_kr, p_k, p_v, s_lat, s_kr, s_k, s_v)


def run_cfg(cfg, inputs, phases=None, trace=False):
    b = Builder(cfg)
    nc = b.build(phases)
    in_maps = prep_inputs(cfg, inputs)
    res = run_bass_kernel_spmd(nc, in_maps, core_ids=list(range(cfg.NCORES)), **({"trace": True} if trace else {}))
    return res


def kernel(**inputs):
    cfg = Cfg()
    res = run_cfg(cfg, inputs)
    return assemble(cfg, res.results)
```
